# Optimizing a Trainium2 kernel written in Bass

```python
import math, functools
import jax, jax.numpy as jnp
from jax import lax
import numpy as np

D_MODEL = 1024
BATCH = 32
SEQ = 256
DEPTH = 4
DEC_BATCH = 4
DEC_SEQ = 1024
PAST_LEN = 256

GRID_W = 64
N_HEADS = 8
QK_NOPE = 64
QK_ROPE = 32
V_HEAD = 64
KV_LORA = 256
Q_LORA = 384
MLA_W = N_HEADS * V_HEAD
LRU_W = 512
LRU_BLOCKS = 8
LRU_BD = LRU_W // LRU_BLOCKS
LRU_C = 8.0
CONV_W = 4
CONV_LEFT = 2
D_FF = 2816
IN_W = Q_LORA + KV_LORA + QK_ROPE + 2 * LRU_W
IN_SPLITS = (Q_LORA, Q_LORA + KV_LORA, Q_LORA + KV_LORA + QK_ROPE, Q_LORA + KV_LORA + QK_ROPE + LRU_W)
MIX_W = MLA_W + LRU_W
N_MOD = 9
ALPHA = (2.0 * DEPTH) ** 0.25
BETA = (8.0 * DEPTH) ** -0.25
ROPE_BASE = 10000.0
ATTN_SCALE = 1.0 / math.sqrt(QK_NOPE + QK_ROPE)
Q_BLOCK = 128
LN_EPS = 1e-5
RMS_EPS = 1e-6

kernel_name = "hybrid_mla_rglru_diffusion_step"


def layer_norm(x, g, b):
    xf = x.astype(jnp.float32)
    mu = jnp.mean(xf, axis=-1, keepdims=True)
    var = jnp.mean(jnp.square(xf - mu), axis=-1, keepdims=True)
    return ((xf - mu) * lax.rsqrt(var + LN_EPS)).astype(x.dtype) * g + b


def rms_norm(x, g):
    xf = x.astype(jnp.float32)
    ms = jnp.mean(jnp.square(xf), axis=-1, keepdims=True)
    return (xf * lax.rsqrt(ms + RMS_EPS)).astype(x.dtype) * g


def modulation(cond, w_mod, b_mod):
    m = jnp.einsum("bd,de->be", jax.nn.silu(cond), w_mod) + b_mod
    return m.reshape(cond.shape[0], N_MOD, D_MODEL)


def swiglu(h, w_up, w_down):
    u = jnp.einsum("btd,df->btf", h, w_up)
    return jnp.einsum("btf,fd->btd", jax.nn.silu(u[..., :D_FF]) * u[..., D_FF:], w_down)


def centred_dwconv(x, w, b):
    t = x.shape[1]
    xp = jnp.pad(x, ((0, 0), (CONV_LEFT, CONV_W - 1 - CONV_LEFT), (0, 0)))
    return sum(xp[:, k:k + t] * w[k] for k in range(CONV_W)) + b


def axial_rope_tables(rows, dtype):
    n_freq = QK_ROPE // 4
    inv = ROPE_BASE ** (-jnp.arange(n_freq, dtype=jnp.float32) / n_freq)
    row = jnp.repeat(jnp.arange(rows, dtype=jnp.float32), GRID_W)
    col = jnp.tile(jnp.arange(GRID_W, dtype=jnp.float32), rows)
    ang = jnp.concatenate([row[:, None] * inv, col[:, None] * inv], axis=-1)
    return jnp.cos(ang).astype(dtype), jnp.sin(ang).astype(dtype)


def apply_rope(x, cos, sin):
    half = QK_ROPE // 2
    x1, x2 = x[..., :half], x[..., half:]
    return jnp.concatenate([x1 * cos - x2 * sin, x1 * sin + x2 * cos], axis=-1)


def mixer_inputs(h, lp):
    b, t, _ = h.shape
    proj = jnp.einsum("btd,de->bte", h, lp["w_in"])
    c_q, c_kv, k_rope, u_x, u_g = jnp.split(proj, IN_SPLITS, axis=-1)
    q = jnp.einsum("btc,ce->bte", rms_norm(c_q, lp["q_norm_g"]), lp["w_uq"])
    q = q.reshape(b, t, N_HEADS, QK_NOPE + QK_ROPE)
    c_kv = rms_norm(c_kv, lp["kv_norm_g"])
    u_x = centred_dwconv(u_x, lp["conv_w"], lp["conv_b"])
    return q[..., :QK_NOPE], q[..., QK_NOPE:], c_kv, k_rope, u_x, u_g


def decompress_kv(c_kv, w_ukv):
    b, t, _ = c_kv.shape
    kv = jnp.einsum("btc,ce->bte", c_kv, w_ukv).reshape(b, t, N_HEADS, QK_NOPE + V_HEAD)
    return kv[..., :QK_NOPE], kv[..., QK_NOPE:]


def mla_attention(q_nope, q_rope, k_nope, k_rope, v):
    b, tq = q_nope.shape[:2]
    blk = math.gcd(tq, Q_BLOCK)
    nb = tq // blk

    def one_block(qs):
        qn, qr = qs
        s = jnp.einsum("bqhd,bkhd->bhqk", qn, k_nope) + jnp.einsum("bqhr,bkr->bhqk", qr, k_rope)
        p = jax.nn.softmax(s.astype(jnp.float32) * ATTN_SCALE, axis=-1).astype(v.dtype)
        return jnp.einsum("bhqk,bkhd->bqhd", p, v)

    qn_b = q_nope.reshape(b, nb, blk, N_HEADS, QK_NOPE).swapaxes(0, 1)
    qr_b = q_rope.reshape(b, nb, blk, N_HEADS, QK_ROPE).swapaxes(0, 1)
    o = lax.map(one_block, (qn_b, qr_b))
    return o.swapaxes(0, 1).reshape(b, tq, MLA_W)


def _linear_combine(e1, e2):
    a1, b1 = e1
    a2, b2 = e2
    return a1 * a2, a2 * b1 + b2


def rg_lru(x, w_a, b_a, w_x, b_x, lam, h0, reverse):
    b, t, _ = x.shape
    xb = x.reshape(b, t, LRU_BLOCKS, LRU_BD)
    r = jax.nn.sigmoid((jnp.einsum("btnd,nde->btne", xb, w_a).reshape(b, t, LRU_W) + b_a).astype(jnp.float32))
    i = jax.nn.sigmoid((jnp.einsum("btnd,nde->btne", xb, w_x).reshape(b, t, LRU_W) + b_x).astype(jnp.float32))
    log_a = -LRU_C * r * jax.nn.softplus(-lam.astype(jnp.float32))
    a = jnp.exp(log_a)
    u = jnp.sqrt(-jnp.expm1(2.0 * log_a)) * (i * x.astype(jnp.float32))
    if reverse:
        a, u = a[:, ::-1], u[:, ::-1]
    u = u.at[:, 0].add(a[:, 0] * h0.astype(jnp.float32))
    _, h = lax.associative_scan(_linear_combine, (a, u), axis=1)
    if reverse:
        h = h[:, ::-1]
    return h.astype(x.dtype)


def bidir_rg_lru(u_x, h0, lp):
    hf = rg_lru(u_x, lp["lru_w_a"][0], lp["lru_b_a"][0], lp["lru_w_x"][0], lp["lru_b_x"][0],
                lp["lru_lambda"][0], h0[:, 0], False)
    hb = rg_lru(u_x, lp["lru_w_a"][1], lp["lru_b_a"][1], lp["lru_w_x"][1], lp["lru_b_x"][1],
                lp["lru_lambda"][1], h0[:, 1], True)
    return hf, hb


def context_mixer(h, lp):
    q_nope, q_rope, c_kv, k_rope, u_x, u_g = mixer_inputs(h, lp)
    k_nope, v = decompress_kv(c_kv, lp["w_ukv"])
    att = mla_attention(q_nope, q_rope, k_nope, k_rope, v)
    h0 = jnp.zeros((h.shape[0], 2, LRU_W), jnp.float32)
    hf, hb = bidir_rg_lru(u_x, h0, lp)
    lru = (hf + hb) * jax.nn.gelu(u_g)
    y = jnp.einsum("bte,ed->btd", jnp.concatenate([att, lru], axis=-1), lp["w_o"])
    final_state = jnp.stack([hf[:, -1], hb[:, 0]], axis=1)
    return y, (c_kv, k_rope, final_state)


def latent_mixer(h, lp, ckv_ctx, krope_ctx, h0, cos, sin):
    q_nope, q_rope, c_kv, k_rope, u_x, u_g = mixer_inputs(h, lp)
    q_rope = apply_rope(q_rope, cos[None, :, None, :], sin[None, :, None, :])
    k_rope = apply_rope(k_rope, cos[None], sin[None])
    k_nope, v = decompress_kv(jnp.concatenate([ckv_ctx, c_kv], axis=1), lp["w_ukv"])
    k_rope = jnp.concatenate([krope_ctx, k_rope], axis=1)
    att = mla_attention(q_nope, q_rope, k_nope, k_rope, v)
    hf, hb = bidir_rg_lru(u_x, h0, lp)
    lru = (hf + hb) * jax.nn.gelu(u_g)
    y = jnp.einsum("bte,ed->btd", jnp.concatenate([att, lru], axis=-1), lp["w_o"])
    return y, None


def trunk_layer(x, mod, mixer, lp):
    m = [mod[:, k, None, :] for k in range(N_MOD)]
    h = x * (1 + m[1]) + m[0]
    x = layer_norm(ALPHA * x + 0.5 * m[2] * swiglu(h, lp["w_ffn_up"][0], lp["w_ffn_down"][0]),
                   lp["ln_g"][0], lp["ln_b"][0])
    h = x * (1 + m[4]) + m[3]
    y, aux = mixer(h)
    x = layer_norm(ALPHA * x + m[5] * y, lp["ln_g"][1], lp["ln_b"][1])
    h = x * (1 + m[7]) + m[6]
    x = layer_norm(ALPHA * x + 0.5 * m[8] * swiglu(h, lp["w_ffn_up"][1], lp["w_ffn_down"][1]),
                   lp["ln_g"][2], lp["ln_b"][2])
    return x, aux


def setup_inputs(seed: int = 0) -> dict:
    key = jax.random.key(seed)
    ks = iter(jax.random.split(key, 32))
    f32 = jnp.float32

    def nrm(shape, s):
        return jax.random.normal(next(ks), shape, f32) * s

    lam_u = jax.random.uniform(next(ks), (DEPTH, 2, LRU_W), f32, 0.9, 0.999)
    return {
        "x_prompt": nrm((BATCH, SEQ, D_MODEL), 1.0),
        "x_sample": nrm((DEC_BATCH, DEC_SEQ, D_MODEL), 1.0),
        "cache_ckv": nrm((DEC_BATCH, DEPTH, PAST_LEN, KV_LORA), 1.0),
        "cache_krope": nrm((DEC_BATCH, DEPTH, PAST_LEN, QK_ROPE), 1.0),
        "state_lru": nrm((DEC_BATCH, DEPTH, 2, LRU_W), 0.5),
        "c": nrm((DEC_BATCH, D_MODEL), 1.0),
        "c_ctx": nrm((D_MODEL,), 1.0),
        "w_mod": nrm((DEPTH, D_MODEL, N_MOD * D_MODEL), 0.5 * D_MODEL ** -0.5),
        "b_mod": nrm((DEPTH, N_MOD * D_MODEL), 0.02),
        "ln_g": 1.0 + nrm((DEPTH, 3, D_MODEL), 0.02),
        "ln_b": nrm((DEPTH, 3, D_MODEL), 0.02),
        "w_ffn_up": nrm((DEPTH, 2, D_MODEL, 2 * D_FF), D_MODEL ** -0.5),
        "w_ffn_down": nrm((DEPTH, 2, D_FF, D_MODEL), BETA * D_FF ** -0.5),
        "w_in": nrm((DEPTH, D_MODEL, IN_W), D_MODEL ** -0.5),
        "q_norm_g": 1.0 + nrm((DEPTH, Q_LORA), 0.02),
        "kv_norm_g": 1.0 + nrm((DEPTH, KV_LORA), 0.02),
        "w_uq": nrm((DEPTH, Q_LORA, N_HEADS * (QK_NOPE + QK_ROPE)), Q_LORA ** -0.5),
        "w_ukv": nrm((DEPTH, KV_LORA, N_HEADS * (QK_NOPE + V_HEAD)), KV_LORA ** -0.5),
        "conv_w": nrm((DEPTH, CONV_W, LRU_W), CONV_W ** -0.5),
        "conv_b": nrm((DEPTH, LRU_W), 0.02),
        "lru_w_a": nrm((DEPTH, 2, LRU_BLOCKS, LRU_BD, LRU_BD), LRU_BD ** -0.5),
        "lru_b_a": nrm((DEPTH, 2, LRU_W), 0.1),
        "lru_w_x": nrm((DEPTH, 2, LRU_BLOCKS, LRU_BD, LRU_BD), LRU_BD ** -0.5),
        "lru_b_x": nrm((DEPTH, 2, LRU_W), 0.1),
        "lru_lambda": jnp.log(lam_u) - jnp.log1p(-lam_u),
        "w_o": nrm((DEPTH, MIX_W, D_MODEL), BETA * MIX_W ** -0.5),
    }


def reference(x_prompt, x_sample, cache_ckv, cache_krope, state_lru, c, c_ctx, w_mod, b_mod,
              ln_g, ln_b, w_ffn_up, w_ffn_down, w_in, q_norm_g, kv_norm_g, w_uq, w_ukv,
              conv_w, conv_b, lru_w_a, lru_b_a, lru_w_x, lru_b_x, lru_lambda, w_o):
    rows = x_sample.shape[1] // GRID_W
    cos, sin = axial_rope_tables(rows, x_sample.dtype)
    xp, xs = x_prompt, x_sample
    ckv_out, krope_out, lru_out = [], [], []
    for l in range(DEPTH):
        lp = {
            "ln_g": ln_g[l], "ln_b": ln_b[l], "w_ffn_up": w_ffn_up[l], "w_ffn_down": w_ffn_down[l],
            "w_in": w_in[l], "q_norm_g": q_norm_g[l], "kv_norm_g": kv_norm_g[l],
            "w_uq": w_uq[l], "w_ukv": w_ukv[l], "conv_w": conv_w[l], "conv_b": conv_b[l],
            "lru_w_a": lru_w_a[l], "lru_b_a": lru_b_a[l], "lru_w_x": lru_w_x[l],
            "lru_b_x": lru_b_x[l], "lru_lambda": lru_lambda[l], "w_o": w_o[l],
        }
        mod_ctx = modulation(c_ctx[None, :], w_mod[l], b_mod[l])
        mod_lat = modulation(c, w_mod[l], b_mod[l])
        xp, (ckv, krope, st) = trunk_layer(xp, mod_ctx, functools.partial(context_mixer, lp=lp), lp)
        ckv_out.append(ckv)
        krope_out.append(krope)
        lru_out.append(st)
        lat_mixer = functools.partial(latent_mixer, lp=lp, ckv_ctx=cache_ckv[:, l],
                                      krope_ctx=cache_krope[:, l], h0=state_lru[:, l], cos=cos, sin=sin)
        xs, _ = trunk_layer(xs, mod_lat, lat_mixer, lp)
    new_cache_ckv = jnp.stack(ckv_out, axis=1)
    new_cache_krope = jnp.stack(krope_out, axis=1)
    new_state_lru = jnp.stack(lru_out, axis=1)
    return (xp, xs, new_cache_ckv, new_cache_krope, new_state_lru)
```

```python
import math
from contextlib import ExitStack

import numpy as np
import concourse.bass as bass
import concourse.mybir as mybir
from concourse.bass_utils import run_bass_kernel_spmd

F32 = mybir.dt.float32
BF16 = mybir.dt.bfloat16
AF = mybir.ActivationFunctionType
ALU = mybir.AluOpType

D = 1024
DEPTH = 4
DFF = 2816
NFC = DFF // 128
TOK = 1024
NT = 8
QL, KVL, ROPE = 384, 256, 32
LRUW = 512
INW = QL + KVL + ROPE + 2 * LRUW
OFF_KV = QL
OFF_KR = QL + KVL
OFF_UX = QL + KVL + ROPE
OFF_UG = OFF_UX + LRUW
NH = 8
ALPHA = (2.0 * DEPTH) ** 0.25
LN_EPS_S = 1e-5 / (ALPHA * ALPHA)
RMS_EPS = 1e-6
ATTN_SCALE = 1.0 / math.sqrt(96.0)
N_LAYERS = DEPTH
GROUPS = ("P", "S")

ARENA_BYTES = 72704


class Sem:
    def __init__(self, handle, name):
        self.h = handle
        self.name = name
        self.cnt = 0


class Unit:
    __slots__ = ("w", "r")

    def __init__(self, fence=None):
        self.w = None
        self.r = dict(fence) if fence else {}


class Eng:
    def __init__(self, h, sem):
        self.h = h
        self.sem = sem
        self.known = {}
        self.relaxed = False


class Prog:
    def __init__(self):
        self.nc = bass.Bass("TRN2", target_bir_lowering=False)
        self.es = ExitStack()
        self.sems = []
        self.n_inst = 0

    def new_sem(self, name):
        s = Sem(self.es.enter_context(self.nc.semaphore(name)), name)
        self.sems.append(s)
        return s

    def fence(self):
        return {s: s.cnt for s in self.sems if s.cnt > 0}

    def sbuf(self, name, shape, dt):
        return self.es.enter_context(self.nc.sbuf_tensor(name, shape, dt))

    def psum(self, name, shape, dt):
        return self.es.enter_context(self.nc.psum_tensor(name, shape, dt))

    def _waits(self, eng, reads, writes, is_dma):
        need = {}

        def add(s, v):
            if need.get(s, 0) < v:
                need[s] = v

        for u in reads:
            if u.w is not None:
                add(*u.w)
        relaxed = eng.relaxed and not is_dma
        for u in writes:
            if u.w is not None and not (relaxed and u.w[0] is eng.sem):
                add(*u.w)
            for s, v in u.r.items():
                if not (relaxed and s is eng.sem):
                    add(s, v)
        for s, v in need.items():
            if eng.known.get(s, 0) < v:
                eng.h.wait_ge(s.h, v)
                eng.known[s] = v
                self.n_inst += 1

    def op(self, eng, fn, reads=(), writes=()):
        self._waits(eng, reads, writes, False)
        inst = fn()
        eng.sem.cnt += 1
        inst.then_inc(eng.sem.h, 1)
        mark = (eng.sem, eng.sem.cnt)
        self.n_inst += 1
        for u in reads:
            u.r[mark[0]] = mark[1]
        for u in writes:
            u.w = mark
            u.r = {}

    def dma(self, eng, sem, out, in_, reads=(), writes=(), nc_ok=False):
        self._waits(eng, reads, writes, True)
        if nc_ok:
            with self.nc.allow_non_contiguous_dma(reason="small strided parameter load"):
                inst = eng.h.dma_start(out=out, in_=in_)
        else:
            inst = eng.h.dma_start(out=out, in_=in_)
        sem.cnt += 16
        inst.then_inc(sem.h, 16)
        mark = (sem, sem.cnt)
        self.n_inst += 1
        for u in reads:
            u.r[mark[0]] = mark[1]
        for u in writes:
            u.w = mark
            u.r = {}


def build_program(n_layers=N_LAYERS, groups=GROUPS, debug=False):
    P = Prog()
    nc = P.nc

    def din(name, shape):
        return nc.dram_tensor(name, list(shape), F32, kind="ExternalInput").ap()

    def dout(name, shape):
        return nc.dram_tensor(name, list(shape), F32, kind="ExternalOutput").ap()

    x_prompt = din("x_prompt", [4, 256, D])
    x_sample = din("x_sample", [TOK, D])
    cache_ckv = din("cache_ckv", [DEPTH, 256, KVL])
    cache_krope = din("cache_krope", [DEPTH, 256, ROPE])
    state_lru = din("state_lru", [DEPTH, 2, LRUW])
    c_in = din("c", [D])
    c_ctx = din("c_ctx", [D])
    w_mod = din("w_mod", [DEPTH, D, 9 * D])
    b_mod = din("b_mod", [DEPTH, 9 * D])
    ln_g = din("ln_g", [DEPTH, 3, D])
    ln_b = din("ln_b", [DEPTH, 3, D])
    w_ffn_up = din("w_ffn_up", [DEPTH, 2, D, 2 * DFF])
    w_ffn_down = din("w_ffn_down", [DEPTH, 2, DFF, D])
    w_in = din("w_in", [DEPTH, D, INW])
    q_norm_g = din("q_norm_g", [DEPTH, QL])
    kv_norm_g = din("kv_norm_g", [DEPTH, KVL])
    w_uq = din("w_uq", [DEPTH, QL, NH * 96])
    w_ukv = din("w_ukv", [DEPTH, KVL, NH * 128])
    conv_w = din("conv_w", [DEPTH, 4, LRUW])
    conv_b = din("conv_b", [DEPTH, LRUW])
    lru_w_a = din("lru_w_a", [DEPTH, 2, 8, 64, 64])
    lru_b_a = din("lru_b_a", [DEPTH, 2, LRUW])
    lru_w_x = din("lru_w_x", [DEPTH, 2, 8, 64, 64])
    lru_b_x = din("lru_b_x", [DEPTH, 2, LRUW])
    lru_lambda = din("lru_lambda", [DEPTH, 2, LRUW])
    w_o = din("w_o", [DEPTH, D, D])
    ident_in = din("ident", [128, 128])
    rope_cs = din("rope_cs", [2, 32, TOK])

    y_prompt = dout("y_prompt", [4, 256, D])
    y_sample = dout("y_sample", [TOK, D])
    new_ckv = dout("new_ckv", [4, DEPTH, 256, KVL])
    new_krope = dout("new_krope", [4, DEPTH, 256, ROPE])
    new_state = dout("new_state", [128, 128])

    PE = Eng(nc.tensor, P.new_sem("s_pe"))
    PE.relaxed = True
    ACT = Eng(nc.scalar, P.new_sem("s_act"))
    DVE = Eng(nc.vector, P.new_sem("s_dve"))
    GQ = Eng(nc.gpsimd, P.new_sem("s_pool"))
    SP = Eng(nc.sync, P.new_sem("s_sp"))

    x_t = P.sbuf("x_t", [128, NT, D], F32)
    hT = P.sbuf("hT", [128, 8, TOK], BF16)
    arena = P.sbuf("arena", [128, ARENA_BYTES // 2], BF16)
    NWB = 4
    wbuf = [P.sbuf(f"wbuf{i}", [128, 4096], BF16) for i in range(NWB)]
    e_gm = P.sbuf("e_gm", [128, D], F32)
    e_g = P.sbuf("e_g", [128, D], F32)
    e_b = P.sbuf("e_b", [128, D], F32)
    NXN = 4
    xn_t = [P.sbuf(f"xn{i}", [128, D], F32) for i in range(NXN)]
    bmodT = P.sbuf("bmodT", [128, DEPTH, 72], F32)
    lngT = P.sbuf("lngT", [128, DEPTH, 3, 8], F32)
    lnbT = P.sbuf("lnbT", [128, DEPTH, 3, 8], F32)
    g2b2 = P.sbuf("g2b2", [128, 2, 16], F32)
    mT_t = P.sbuf("mT_t", [128, 24], F32)
    rope_t = P.sbuf("rope_t", [128, 2, TOK], F32)
    ident_b = P.sbuf("ident_b", [128, 128], BF16)
    ident_f = P.sbuf("ident_f", [128, 128], F32)
    ones_f = P.sbuf("ones_f", [128, 128], F32)
    sc_rep = P.sbuf("sc_rep", [128, 8, 128], BF16)
    cond_f = P.sbuf("cond_f", [128, 8], F32)
    cond_s = P.sbuf("cond_s", [128, 8], F32)
    qg_bc = P.sbuf("qg_bc", [128, QL], F32)
    kg_bc = P.sbuf("kg_bc", [128, KVL], F32)
    cw_t = P.sbuf("cw_t", [128, DEPTH, 4, 4], F32)
    cb_t = P.sbuf("cb_t", [128, DEPTH, 4], F32)
    ba_t = P.sbuf("ba_t", [128, DEPTH, 2, 4], F32)
    bx_t = P.sbuf("bx_t", [128, DEPTH, 2, 4], F32)
    lam_t = P.sbuf("lam_t", [128, DEPTH, 2, 4], F32)
    negc_t = P.sbuf("negc_t", [128, DEPTH, 2, 4], F32)
    neg2c_t = P.sbuf("neg2c_t", [128, DEPTH, 2, 4], F32)
    h0_t = P.sbuf("h0_t", [128, DEPTH, 2, 4], F32)
    fs_t = P.sbuf("fs_t", [128, 4, DEPTH, 2, 4], F32)
    fs_o = P.sbuf("fs_o", [128, 128], F32)
    bd_t = P.sbuf("bd_t", [128, 2, 2, 4, 128], BF16)
    NST = 6
    st_t = [P.sbuf(f"st{i}", [128, 2, 6], F32) for i in range(NST)]
    mv_t = [P.sbuf(f"mv{i}", [128, 2], F32) for i in range(NST)]
    lsm_t = [P.sbuf(f"lsm{i}", [128, 4], F32) for i in range(NST)]
    sm_t = [P.sbuf(f"sm{i}", [128, 8], F32) for i in range(2)]

    bigps = P.psum("bigps", [128, 8 * 512], F32)
    banks = [bigps[:, i * 512:(i + 1) * 512] for i in range(8)]
    bank_u = [Unit() for _ in range(8)]
    rot = {"A": [0, 1], "B": [2, 3], "C": [4, 5], "D": [6, 7]}
    rot_i = {k: 0 for k in rot}

    def nextbank(cls):
        b = rot[cls][rot_i[cls] % len(rot[cls])]
        rot_i[cls] += 1
        return b

    x_u = [Unit() for _ in range(NT)]
    hT_u = [Unit() for _ in range(NT)]
    wbuf_u = [Unit() for _ in range(NWB)]
    wbuf_sem = [P.new_sem(f"d_wb{i}") for i in range(NWB)]
    wb_i = [0]
    e_u = {k: Unit() for k in ("gm", "g", "b")}
    g2b2_u = [Unit(), Unit()]
    mT_u = Unit()
    g2_i = [0]
    e_sem = {k: P.new_sem(f"d_e_{k}") for k in ("g", "b", "bm")}
    xn_u = [Unit() for _ in range(NXN)]
    st_u = [Unit() for _ in range(NST)]
    x_sem = [P.new_sem(f"d_x{i}") for i in range(NT)]
    const_u = Unit()
    const_sem = P.new_sem("d_const")
    cond_u = Unit()
    cond_sem = P.new_sem("d_cond")
    screp_u = Unit()
    norm_u = Unit()
    norm_sem = P.new_sem("d_norm")
    bd_u = Unit()
    bd_sem = P.new_sem("d_bd")
    rope_u = Unit()
    fs_u = Unit()
    fso_u = Unit()
    fso_sem = P.new_sem("d_fso")
    arena_dma_sems = [P.new_sem(f"d_ar{i}") for i in range(8)]

    def aview(off, n_elem, dt):
        if dt == BF16:
            return arena[:, off // 2: off // 2 + n_elem]
        return arena[:, off // 2: off // 2 + 2 * n_elem].bitcast(F32)

    wb_reserved = set()

    def next_wbuf():
        while True:
            i = wb_i[0] % NWB
            wb_i[0] += 1
            if i not in wb_reserved:
                return i

    prefetched = {}

    def load_wup(l, jj, fp):
        wup = w_ffn_up[l, jj]
        wi = next_wbuf()
        wv = wbuf[wi][:, 0:4096].rearrange("p (k n) -> p k n", n=512)
        P.dma(GQ, wbuf_sem[wi], wv[:, :, 0:256], wup[:, fp * 256:(fp + 1) * 256].rearrange("(k p) n -> p k n", p=128),
              writes=[wbuf_u[wi]])
        P.dma(GQ, wbuf_sem[wi], wv[:, :, 256:512],
              wup[:, DFF + fp * 256:DFF + (fp + 1) * 256].rearrange("(k p) n -> p k n", p=128), writes=[wbuf_u[wi]])
        return wi, wv

    def do_prefetch(nxt):
        if nxt is None:
            return
        l2, j2 = nxt
        if j2 == 1:
            prefetched["wA"] = load_w_piece(w_in[l2][:, 0:QL], QL)
            prefetched["wB"] = load_w_piece(w_in[l2][:, OFF_KV:OFF_KV + 288], 288)
        else:
            for fp in (0, 1):
                prefetched[("wup", fp)] = load_wup(l2, 0 if j2 == 0 else 1, fp)

    def load_w_piece(src_ap, ncols, extra=None):
        i = next_wbuf()
        dst = wbuf[i][:, 0:8 * ncols].rearrange("p (k n) -> p k n", n=ncols)
        P.dma(GQ, wbuf_sem[i], dst, src_ap.rearrange("(k p) n -> p k n", p=128), writes=[wbuf_u[i]])
        return i, dst

    def mod_vector(l, v, dest, dest_u, kind):
        P.dma(SP, e_sem["bm"], dest[:], b_mod[l, v * D:(v + 1) * D].partition_broadcast(128), writes=[dest_u])
        for half in range(2):
            c0 = v * D + half * 512
            wi, wv = load_w_piece(w_mod[l][:, c0:c0 + 512], 512)
            b = nextbank("D")
            for k in range(8):
                P.op(PE, lambda k=k: nc.tensor.matmul(banks[b][:], lhsT=sc_rep[:, k, :], rhs=wv[:, k, :],
                                                        start=(k == 0), stop=(k == 7)),
                     reads=[screp_u, wbuf_u[wi]], writes=[bank_u[b]])
            dsl = dest[:, half * 512:(half + 1) * 512]
            P.op(DVE, lambda: nc.vector.tensor_tensor(out=dsl, in0=banks[b][:], in1=dsl, op=ALU.add),
                 reads=[bank_u[b], dest_u], writes=[dest_u])

    def emit_st(l, j, prev):
        b = nextbank("D")
        ps = banks[b][:, 0:32]
        for vi in range(2):
            v = 3 * j + vi
            for half in range(2):
                c0 = v * D + half * 512
                wi, wv = load_w_piece(w_mod[l][:, c0:c0 + 512], 512)
                for cc in range(4):
                    col = 2 * (vi * 8 + half * 4 + cc)
                    for k in range(8):
                        P.op(PE, lambda k=k: nc.tensor.matmul(ps[:, col:col + 2], lhsT=wv[:, k, cc * 128:(cc + 1) * 128],
                                                                rhs=sc_rep[:, k, 0:2], start=(k == 0), stop=(k == 7)),
                             reads=[screp_u, wbuf_u[wi]], writes=[bank_u[b]])
        psv = ps.rearrange("p (c t) -> p c t", t=2)[:, :, 0]
        P.op(DVE, lambda: nc.vector.tensor_tensor(out=mT_t[:, 0:16], in0=psv, in1=bmodT[:, l, 3 * j * 8:3 * j * 8 + 16], op=ALU.add),
             reads=[bank_u[b], const_u], writes=[mT_u])
        gi = g2_i[0] % 2
        g2_i[0] += 1
        G2 = g2b2[:, gi, 0:8]
        B2 = g2b2[:, gi, 8:16]
        if prev is None:
            P.op(DVE, lambda: nc.vector.tensor_scalar(out=G2, in0=mT_t[:, 8:16], scalar1=1.0, scalar2=None, op0=ALU.add),
                 reads=[mT_u], writes=[g2b2_u[gi]])
            P.op(DVE, lambda: nc.vector.tensor_copy(out=B2, in_=mT_t[:, 0:8]), reads=[mT_u], writes=[g2b2_u[gi]])
        else:
            lp, jp = prev
            P.op(DVE, lambda: nc.vector.scalar_tensor_tensor(out=G2, in0=mT_t[:, 8:16], scalar=1.0, in1=lngT[:, lp, jp, :],
                                                              op0=ALU.add, op1=ALU.mult),
                 reads=[mT_u, const_u], writes=[g2b2_u[gi]])
            P.op(DVE, lambda: nc.vector.scalar_tensor_tensor(out=mT_t[:, 16:24], in0=mT_t[:, 8:16], scalar=1.0, in1=lnbT[:, lp, jp, :],
                                                              op0=ALU.add, op1=ALU.mult),
                 reads=[mT_u, const_u], writes=[mT_u])
            P.op(DVE, lambda: nc.vector.tensor_tensor(out=B2, in0=mT_t[:, 16:24], in1=mT_t[:, 0:8], op=ALU.add),
                 reads=[mT_u], writes=[g2b2_u[gi]])

    def emit_gm_gb(l, j):
        mod_vector(l, 3 * j + 2, e_gm, e_u["gm"], "T")
        P.dma(SP, e_sem["g"], e_g[:], ln_g[l, j].partition_broadcast(128), writes=[e_u["g"]])
        P.dma(SP, e_sem["b"], e_b[:], ln_b[l, j].partition_broadcast(128), writes=[e_u["b"]])

    tp_cnt = [0]

    def make_h(i, src, src_u, hi_banks=False):
        gi = (g2_i[0] - 1) % 2
        par = tp_cnt[0] % 2
        tp_cnt[0] += 1
        bks = (6, 7) if (hi_banks or par == 1) else (4, 5)
        for c in range(8):
            b = bks[c // 4]
            P.op(PE, lambda c=c, b=b: nc.tensor.transpose(banks[b][:, (c % 4) * 128:(c % 4 + 1) * 128], src[:, c * 128:(c + 1) * 128], ident_f[:]),
                 reads=[src_u, const_u], writes=[bank_u[b]])
        for c in range(8):
            b = bks[c // 4]
            if c % 2 == 0:
                P.op(ACT, lambda c=c, b=b: nc.scalar.activation(out=hT[:, c, i * 128:(i + 1) * 128], in_=banks[b][:, (c % 4) * 128:(c % 4 + 1) * 128],
                                                               func=AF.Identity, bias=g2b2[:, gi, 8 + c:9 + c], scale=g2b2[:, gi, c:c + 1]),
                     reads=[bank_u[b], g2b2_u[gi]], writes=[hT_u[i]])
            else:
                P.op(DVE, lambda c=c, b=b: nc.vector.tensor_scalar(out=hT[:, c, i * 128:(i + 1) * 128], in0=banks[b][:, (c % 4) * 128:(c % 4 + 1) * 128],
                                                                  scalar1=g2b2[:, gi, c:c + 1], scalar2=g2b2[:, gi, 8 + c:9 + c],
                                                                  op0=ALU.mult, op1=ALU.add),
                     reads=[bank_u[b], g2b2_u[gi]], writes=[hT_u[i]])

    def residual_half(i, dh, b, kfac):
        xs = x_t[:, i, dh * 512:(dh + 1) * 512]
        P.op(DVE, lambda: nc.vector.scalar_tensor_tensor(out=xs, in0=banks[b][:], scalar=kfac, in1=xs, op0=ALU.mult, op1=ALU.add),
             reads=[bank_u[b], x_u[i]], writes=[x_u[i]])

    ln_cnt = [0]

    def epilogue_tiles(tiles, do_h, pre=None, bank_sets=((4, 5), (6, 7))):
        tiles = list(tiles)
        n = len(tiles)
        slot = {}
        gi = (g2_i[0] - 1) % 2

        def stA(j, i):
            sl_ = ln_cnt[0] % NST
            xs_ = ln_cnt[0] % NXN
            ln_cnt[0] += 1
            slot[j] = (sl_, xs_)
            if pre is not None:
                pre(i)
            st, mv = st_t[sl_], mv_t[sl_]
            for hf in range(2):
                P.op(DVE, lambda hf=hf: nc.vector.bn_stats(out=st[:, hf, :], in_=x_t[:, i, hf * 512:(hf + 1) * 512]),
                     reads=[x_u[i]], writes=[st_u[sl_]])
            P.op(DVE, lambda: nc.vector.bn_aggr(out=mv[:], in_=st[:].rearrange("p a b -> p (a b)")),
                 reads=[st_u[sl_]], writes=[st_u[sl_]])

        def stB(j, i):
            sl_, xs_ = slot[j]
            P.op(ACT, lambda: nc.scalar.activation(out=lsm_t[sl_][:, 0:1], in_=mv_t[sl_][:, 1:2], func=AF.Sqrt, bias=LN_EPS_S, scale=1.0),
                 reads=[st_u[sl_]], writes=[st_u[sl_]])

        def stC(j, i):
            sl_, xs_ = slot[j]
            sm = lsm_t[sl_]
            P.op(DVE, lambda: nc.vector.reciprocal(out=sm[:, 1:2], in_=sm[:, 0:1]), reads=[st_u[sl_]], writes=[st_u[sl_]])
            P.op(DVE, lambda: nc.vector.tensor_scalar(out=xn_t[xs_][:], in0=x_t[:, i, :], scalar1=mv_t[sl_][:, 0:1], scalar2=sm[:, 1:2],
                                                      op0=ALU.subtract, op1=ALU.mult),
                 reads=[x_u[i], st_u[sl_]], writes=[xn_u[xs_]])

        def stE(j, i):
            sl_, xs_ = slot[j]
            if j % 4 == 3:
                P.op(DVE, lambda: nc.vector.tensor_tensor(out=x_t[:, i, :], in0=xn_t[xs_][:], in1=e_g[:], op=ALU.mult),
                     reads=[xn_u[xs_], e_u["g"]], writes=[x_u[i]])
                P.op(DVE, lambda: nc.vector.tensor_tensor(out=x_t[:, i, :], in0=x_t[:, i, :], in1=e_b[:], op=ALU.add),
                     reads=[x_u[i], e_u["b"]], writes=[x_u[i]])
            else:
                P.op(GQ, lambda: nc.gpsimd.tensor_tensor(out=x_t[:, i, :], in0=xn_t[xs_][:], in1=e_g[:], op=ALU.mult),
                     reads=[xn_u[xs_], e_u["g"]], writes=[x_u[i]])
                P.op(GQ, lambda: nc.gpsimd.tensor_tensor(out=x_t[:, i, :], in0=x_t[:, i, :], in1=e_b[:], op=ALU.add),
                     reads=[x_u[i], e_u["b"]], writes=[x_u[i]])
            if do_h:
                bks = bank_sets[j % len(bank_sets)]
                for c in range(8):
                    b = bks[c // 4]
                    P.op(PE, lambda c=c, b=b: nc.tensor.transpose(banks[b][:, (c % 4) * 128:(c % 4 + 1) * 128],
                                                                  xn_t[xs_][:, c * 128:(c + 1) * 128], ident_f[:]),
                         reads=[xn_u[xs_], const_u], writes=[bank_u[b]])

        def stF(j, i):
            if not do_h:
                return
            bks = bank_sets[j % len(bank_sets)]
            for c in range(8):
                b = bks[c // 4]
                src = banks[b][:, (c % 4) * 128:(c % 4 + 1) * 128]
                dst = hT[:, c, i * 128:(i + 1) * 128]
                if c % 2 == 0:
                    P.op(ACT, lambda: nc.scalar.activation(out=dst, in_=src, func=AF.Identity, bias=g2b2[:, gi, 8 + c:9 + c],
                                                           scale=g2b2[:, gi, c:c + 1]),
                         reads=[bank_u[b], g2b2_u[gi]], writes=[hT_u[i]])
                else:
                    P.op(DVE, lambda: nc.vector.tensor_scalar(out=dst, in0=src, scalar1=g2b2[:, gi, c:c + 1], scalar2=g2b2[:, gi, 8 + c:9 + c],
                                                              op0=ALU.mult, op1=ALU.add),
                         reads=[bank_u[b], g2b2_u[gi]], writes=[hT_u[i]])

        stages = [stA, stB, stC, stE, stF]
        for step in range(n + len(stages) - 1):
            for k in reversed(range(len(stages))):
                j = step - k
                if 0 <= j < n:
                    stages[k](j, tiles[j])

    def ffn_sublayer(l, j, nxt):
        jj = 0 if j == 0 else 1
        fz = P.fence()
        act_u = [[Unit(fz), Unit(fz)] for _ in range(NFC)]
        actT = aview(0, NFC * TOK, BF16).rearrange("p (f t) -> p f t", t=TOK)
        wd_t = [aview(45056 + 2048 * i, 1024, BF16).rearrange("p (a n) -> p a n", n=512) for i in range(4)]
        wd_u = [Unit(fz) for _ in range(4)]
        sg_t = [aview(45056 + 8192 + 2048 * i, 512, F32) for i in range(2)]
        sg_u = [Unit(fz) for _ in range(2)]
        wup = w_ffn_up[l, jj]
        wdn = w_ffn_down[l, jj]
        sgc = 0
        for fp in range(NFC // 2):
            if fp == 2:
                emit_gm_gb(l, j)
            if fp == 6 and nxt is not None:
                emit_st(nxt[0], nxt[1], (l, j))
            if ("wup", fp) in prefetched:
                wi, wv = prefetched.pop(("wup", fp))
            else:
                wi, wv = load_wup(l, jj, fp)
            for sub in range(2):
                fc = 2 * fp + sub
                for qb in range(2):
                    bg, bu = nextbank("A"), nextbank("B")
                    hr = [hT_u[4 * qb + t] for t in range(4)]
                    for k in range(8):
                        P.op(PE, lambda k=k: nc.tensor.matmul(banks[bg][:], lhsT=wv[:, k, sub * 128:(sub + 1) * 128],
                                                                rhs=hT[:, k, qb * 512:(qb + 1) * 512], start=(k == 0), stop=(k == 7)),
                             reads=[wbuf_u[wi]] + hr, writes=[bank_u[bg]])
                    for k in range(8):
                        P.op(PE, lambda k=k: nc.tensor.matmul(banks[bu][:], lhsT=wv[:, k, 256 + sub * 128:256 + (sub + 1) * 128],
                                                                rhs=hT[:, k, qb * 512:(qb + 1) * 512], start=(k == 0), stop=(k == 7)),
                             reads=[wbuf_u[wi]] + hr, writes=[bank_u[bu]])
                    si = sgc % 2
                    sgc += 1
                    P.op(ACT, lambda: nc.scalar.activation(out=sg_t[si], in_=banks[bg][:], func=AF.Silu),
                         reads=[bank_u[bg]], writes=[sg_u[si]])
                    P.op(DVE, lambda: nc.vector.tensor_tensor(out=actT[:, fc, qb * 512:(qb + 1) * 512], in0=banks[bu][:],
                                                              in1=sg_t[si], op=ALU.mult),
                         reads=[bank_u[bu], sg_u[si]], writes=[act_u[fc][qb]])
        do_prefetch(nxt)
        wdc = 0
        for dh in range(2):
            for f2 in range(NFC // 2):
                wi = wdc % 4
                wdc += 1
                P.dma(GQ, arena_dma_sems[wi], wd_t[wi],
                      wdn[f2 * 256:(f2 + 1) * 256, dh * 512:(dh + 1) * 512].rearrange("(a p) n -> p a n", p=128),
                      writes=[wd_u[wi]])
                for a in range(2):
                    P.op(DVE, lambda a=a: nc.vector.tensor_tensor(out=wd_t[wi][:, a, :], in0=wd_t[wi][:, a, :],
                                                                 in1=e_gm[:, dh * 512:(dh + 1) * 512], op=ALU.mult),
                         reads=[wd_u[wi], e_u["gm"]], writes=[wd_u[wi]])
                for a in range(2):
                    fc = 2 * f2 + a
                    for i in range(NT):
                        P.op(PE, lambda i=i: nc.tensor.matmul(banks[i][:], lhsT=actT[:, fc, i * 128:(i + 1) * 128],
                                                                rhs=wd_t[wi][:, a, :], start=(fc == 0), stop=(fc == NFC - 1)),
                             reads=[act_u[fc][i // 4], wd_u[wi]], writes=[bank_u[i]])
            for i in range(NT):
                residual_half(i, dh, i, 0.5 / ALPHA)
        epilogue_tiles(range(NT), nxt is not None, bank_sets=((0, 1), (2, 3), (4, 5), (6, 7)))

    def mixer_sublayer(l, grp, nxt):
        is_s = grp == "S"
        koff = 256 if is_s else 0
        nkeys = 1280 if is_s else 1024
        nkt = nkeys // 128
        fz = P.fence()
        o = 0
        mixT = aview(o, 8 * TOK, BF16).rearrange("p (e t) -> p e t", t=TOK); o += 16384
        cqT = aview(o, 3 * TOK, BF16).rearrange("p (e t) -> p e t", t=TOK); o += 6144
        ckvT = aview(o, 2 * 1280, BF16).rearrange("p (e t) -> p e t", t=1280); o += 5120
        krT = aview(o, 1280, BF16); o += 2560
        cqn = [aview(o + 768 * i, QL, BF16) for i in range(2)]; o += 1536
        kvo = [aview(o + 1152 * i, 288, F32) for i in range(2)]; o += 2304
        ckvb = [aview(o + 512 * i, KVL, BF16) for i in range(2)]; o += 1024
        ctxc = aview(o, 512, BF16).rearrange("p (t n) -> p t n", n=256); o += 1024
        ctxr = aview(o, 192, BF16).rearrange("p (t n) -> p t n", n=96); o += 384
        sqj = aview(o, QL, BF16); o += 768
        assert o <= 37248, o
        o = 37248
        U0 = o
        xpad = aview(o, 1056, F32); o += 4224
        xc = aview(o, TOK, F32); o += 4096
        xcb = aview(o, TOK, BF16); o += 2048
        gg = aview(o, TOK, F32); o += 4096
        ra = aview(o, TOK, F32); o += 4096
        iu = aview(o, TOK, F32); o += 4096
        ss = aview(o, TOK, F32); o += 4096
        hf_t = aview(o, TOK, F32); o += 4096
        hb2_t = aview(o, TOK, F32); o += 4096
        assert o <= ARENA_BYTES, o
        mix_u = [[Unit(fz) for _ in range(NT)] for _ in range(8)]
        cqT_u = [Unit(fz) for _ in range(NT)]
        ckvT_u = [Unit(fz) for _ in range(nkt)]
        krT_u = Unit(fz)
        m1_u = {k: [Unit(fz), Unit(fz)] for k in ("cqn", "kvo", "ckvb", "sm")}
        ctx_u = Unit(fz)
        sqj_u = Unit(fz)
        lru_u = {k: Unit(fz) for k in ("xpad", "xc", "xcb", "gg", "ra", "iu", "ss", "hf", "hb")}

        P.dma(SP, norm_sem, qg_bc[:], q_norm_g[l].partition_broadcast(128), writes=[norm_u])
        P.dma(SP, norm_sem, kg_bc[:], kv_norm_g[l].partition_broadcast(128), writes=[norm_u])
        for mat, wsrc in enumerate((lru_w_a, lru_w_x)):
            for d in range(2):
                for par in range(2):
                    src = wsrc[l, d].rearrange("(c q) a e -> q a c e", q=2)[par]
                    P.dma(GQ, bd_sem, bd_t[par * 64:(par + 1) * 64, mat, d, :, par * 64:(par + 1) * 64], src,
                          writes=[bd_u], nc_ok=True)

        if "wA" in prefetched:
            wA_i, wA = prefetched.pop("wA")
            wB_i, wB = prefetched.pop("wB")
        else:
            wA_i, wA = load_w_piece(w_in[l][:, 0:QL], QL)
            wB_i, wB = load_w_piece(w_in[l][:, OFF_KV:OFF_KV + 288], 288)
        if is_s:
            P.dma(GQ, arena_dma_sems[4], ctxc, cache_ckv[l].rearrange("(t p) c -> p t c", p=128), writes=[ctx_u])
            P.op(DVE, lambda: nc.vector.memset(ctxr.rearrange("p t n -> p (t n)"), 0.0), writes=[ctx_u])
            P.dma(GQ, arena_dma_sems[4], ctxr[:, :, 64:96], cache_krope[l].rearrange("(t p) c -> p t c", p=128),
                  writes=[ctx_u], nc_ok=True)

        m1b = {}

        def m1_s0(i):
            p = i % 3
            bq, bk = 2 * p, 2 * p + 1
            m1b[i] = (bq, bk)
            for k in range(8):
                P.op(PE, lambda k=k: nc.tensor.matmul(banks[bq][:, 0:QL], lhsT=hT[:, k, i * 128:(i + 1) * 128], rhs=wA[:, k, :],
                                                        start=(k == 0), stop=(k == 7)),
                     reads=[hT_u[i], wbuf_u[wA_i]], writes=[bank_u[bq]])
            for k in range(8):
                P.op(PE, lambda k=k: nc.tensor.matmul(banks[bk][:, 0:288], lhsT=hT[:, k, i * 128:(i + 1) * 128], rhs=wB[:, k, :],
                                                        start=(k == 0), stop=(k == 7)),
                     reads=[hT_u[i], wbuf_u[wB_i]], writes=[bank_u[bk]])

        def m1_s1(i):
            s = i % 2
            bq, bk = m1b[i]
            sm, smu = sm_t[s], m1_u["sm"][s]
            P.op(ACT, lambda: nc.scalar.activation(out=sqj, in_=banks[bq][:, 0:QL], func=AF.Square, accum_out=sm[:, 3:4]),
                 reads=[bank_u[bq]], writes=[sqj_u, smu])
            P.op(ACT, lambda: nc.scalar.activation(out=sqj[:, 0:KVL], in_=banks[bk][:, 0:KVL], func=AF.Square, accum_out=sm[:, 5:6]),
                 reads=[bank_u[bk]], writes=[sqj_u, smu])
            P.op(ACT, lambda: nc.scalar.activation(out=sm[:, 4:5], in_=sm[:, 3:4], func=AF.Sqrt, bias=RMS_EPS, scale=1.0 / QL),
                 reads=[smu], writes=[smu])
            P.op(ACT, lambda: nc.scalar.activation(out=sm[:, 6:7], in_=sm[:, 5:6], func=AF.Sqrt, bias=RMS_EPS, scale=1.0 / KVL),
                 reads=[smu], writes=[smu])

        def m1_s2(i):
            s = i % 2
            bq, bk = m1b[i]
            sm, smu = sm_t[s], m1_u["sm"][s]
            P.op(DVE, lambda: nc.vector.reciprocal(out=sm[:, 4:5], in_=sm[:, 4:5]), reads=[smu], writes=[smu])
            P.op(DVE, lambda: nc.vector.reciprocal(out=sm[:, 6:7], in_=sm[:, 6:7]), reads=[smu], writes=[smu])
            P.op(DVE, lambda: nc.vector.scalar_tensor_tensor(out=cqn[s], in0=banks[bq][:, 0:QL], scalar=sm[:, 4:5], in1=qg_bc[:],
                                                              op0=ALU.mult, op1=ALU.mult),
                 reads=[bank_u[bq], smu, norm_u], writes=[m1_u["cqn"][s]])
            P.op(DVE, lambda: nc.vector.scalar_tensor_tensor(out=kvo[s][:, 0:KVL], in0=banks[bk][:, 0:KVL], scalar=sm[:, 6:7],
                                                              in1=kg_bc[:], op0=ALU.mult, op1=ALU.mult),
                 reads=[bank_u[bk], smu, norm_u], writes=[m1_u["kvo"][s]])
            if not is_s:
                P.op(DVE, lambda: nc.vector.tensor_copy(out=kvo[s][:, KVL:288], in_=banks[bk][:, KVL:288]),
                     reads=[bank_u[bk]], writes=[m1_u["kvo"][s]])

        def m1_s3(i):
            s = i % 2
            if not is_s:
                seq, t0 = i // 2, (i % 2) * 128
                P.dma(SP, arena_dma_sems[5 + s], new_ckv[seq, l, t0:t0 + 128, :], kvo[s][:, 0:KVL], reads=[m1_u["kvo"][s]])
                P.dma(SP, arena_dma_sems[5 + s], new_krope[seq, l, t0:t0 + 128, :], kvo[s][:, KVL:288], reads=[m1_u["kvo"][s]])
            P.op(ACT, lambda: nc.scalar.copy(out=ckvb[s], in_=kvo[s][:, 0:KVL]), reads=[m1_u["kvo"][s]], writes=[m1_u["ckvb"][s]])
            pT = banks[6][:].bitcast(BF16)
            for c in range(3):
                P.op(PE, lambda c=c: nc.tensor.transpose(pT[:, c * 128:(c + 1) * 128], cqn[s][:, c * 128:(c + 1) * 128], ident_b[:]),
                     reads=[m1_u["cqn"][s], const_u], writes=[bank_u[6]])

        def m1_s4(i):
            s = i % 2
            pT6 = banks[6][:].bitcast(BF16)
            P.op(ACT, lambda: nc.scalar.copy(out=cqT[:, :, i * 128:(i + 1) * 128], in_=pT6[:, 0:384].rearrange("p (c n) -> p c n", n=128)),
                 reads=[bank_u[6]], writes=[cqT_u[i]])
            pT = banks[7][:].bitcast(BF16)
            for c in range(2):
                P.op(PE, lambda c=c: nc.tensor.transpose(pT[:, c * 128:(c + 1) * 128], ckvb[s][:, c * 128:(c + 1) * 128], ident_b[:]),
                     reads=[m1_u["ckvb"][s], const_u], writes=[bank_u[7]])

        def m1_s5(i):
            pT = banks[7][:].bitcast(BF16)
            kt = (koff // 128) + i
            P.op(ACT, lambda: nc.scalar.copy(out=ckvT[:, :, koff + i * 128:koff + (i + 1) * 128],
                                             in_=pT[:, 0:256].rearrange("p (c n) -> p c n", n=128)),
                 reads=[bank_u[7]], writes=[ckvT_u[kt]])

        m1_stages = [m1_s0, m1_s1, m1_s2, m1_s3, m1_s4, m1_s5]
        for step in range(NT + len(m1_stages) - 1):
            for k in reversed(range(len(m1_stages))):
                i = step - k
                if 0 <= i < NT:
                    m1_stages[k](i)

        if is_s:
            b = nextbank("C")
            pT = banks[b][:].bitcast(BF16)
            for c in range(2):
                for t in range(2):
                    P.op(PE, lambda c=c, t=t: nc.tensor.transpose(pT[:, (c * 2 + t) * 128:(c * 2 + t + 1) * 128],
                                                                  ctxc[:, t, c * 128:(c + 1) * 128], ident_b[:]),
                         reads=[ctx_u, const_u], writes=[bank_u[b]])
            P.op(ACT, lambda: nc.scalar.copy(out=ckvT[:, :, 0:256], in_=pT[:, 0:512].rearrange("p (c n) -> p c n", n=256)),
                 reads=[bank_u[b]], writes=[ckvT_u[0], ckvT_u[1]])
            b = nextbank("C")
            pT = banks[b][:].bitcast(BF16)
            for t in range(2):
                P.op(PE, lambda t=t: nc.tensor.transpose(pT[0:96, t * 128:(t + 1) * 128], ctxr[:, t, :], ident_b[:]),
                     reads=[ctx_u, const_u], writes=[bank_u[b]])
            P.op(ACT, lambda: nc.scalar.copy(out=krT[64:96, 0:256], in_=pT[64:96, 0:256]),
                 reads=[bank_u[b]], writes=[krT_u])

        if is_s:
            wS_i = next_wbuf()
            wS = wbuf[wS_i][:, 0:8 * 96].rearrange("p (k n) -> p k n", n=96)
            P.op(DVE, lambda: nc.vector.memset(wbuf[wS_i][:, 0:8 * 96], 0.0), writes=[wbuf_u[wS_i]])
            P.dma(GQ, wbuf_sem[wS_i], wS[:, :, 64:80], w_in[l][:, OFF_KR + 16:OFF_KR + 32].rearrange("(k p) n -> p k n", p=128),
                  writes=[wbuf_u[wS_i]], nc_ok=True)
            P.dma(GQ, wbuf_sem[wS_i], wS[:, :, 80:96], w_in[l][:, OFF_KR:OFF_KR + 16].rearrange("(k p) n -> p k n", p=128),
                  writes=[wbuf_u[wS_i]], nc_ok=True)
        for qb in range(2):
            hr = [hT_u[4 * qb + t] for t in range(4)]
            ba_ = nextbank("A")
            for k in range(8):
                P.op(PE, lambda k=k: nc.tensor.matmul(banks[ba_][0:96, :], lhsT=wB[:, k, 192:288], rhs=hT[:, k, qb * 512:(qb + 1) * 512],
                                                        start=(k == 0), stop=(k == 7)),
                     reads=[wbuf_u[wB_i]] + hr, writes=[bank_u[ba_]])
            if not is_s:
                P.op(ACT, lambda: nc.scalar.copy(out=krT[64:96, qb * 512:(qb + 1) * 512], in_=banks[ba_][64:96, :]),
                     reads=[bank_u[ba_]], writes=[krT_u])
            else:
                bb_ = nextbank("B")
                for k in range(8):
                    P.op(PE, lambda k=k: nc.tensor.matmul(banks[bb_][0:96, :], lhsT=wS[:, k, :], rhs=hT[:, k, qb * 512:(qb + 1) * 512],
                                                            start=(k == 0), stop=(k == 7)),
                         reads=[wbuf_u[wS_i]] + hr, writes=[bank_u[bb_]])
                t1 = ra[64:96, 0:512]
                t2 = iu[64:96, 0:512]
                P.op(DVE, lambda: nc.vector.tensor_tensor(out=t1, in0=banks[ba_][64:96, :], in1=rope_t[64:96, 0, qb * 512:(qb + 1) * 512],
                                                          op=ALU.mult), reads=[bank_u[ba_], const_u], writes=[lru_u["ra"]])
                P.op(DVE, lambda: nc.vector.tensor_tensor(out=t2, in0=banks[bb_][64:96, :], in1=rope_t[64:96, 1, qb * 512:(qb + 1) * 512],
                                                          op=ALU.mult), reads=[bank_u[bb_], const_u], writes=[lru_u["iu"]])
                P.op(DVE, lambda: nc.vector.tensor_tensor(out=krT[64:96, 256 + qb * 512:256 + (qb + 1) * 512], in0=t1, in1=t2, op=ALU.add),
                     reads=[lru_u["ra"], lru_u["iu"]], writes=[krT_u])

        def fetch_attn_weights():
            wq_i = next_wbuf()
            wq = wbuf[wq_i][:, 0:3 * 800].rearrange("p (k n) -> p k n", n=800)
            P.op(DVE, lambda: nc.vector.memset(wq[:, :, 768:800], 0.0), writes=[wbuf_u[wq_i]])
            P.dma(GQ, wbuf_sem[wq_i], wq[:, :, 0:768], w_uq[l].rearrange("(k p) n -> p k n", p=128), writes=[wbuf_u[wq_i]])
            wkv_i = next_wbuf()
            wkv = wbuf[wkv_i][:, 0:2048].rearrange("p (k n) -> p k n", n=1024)
            P.dma(GQ, wbuf_sem[wkv_i], wkv, w_ukv[l].rearrange("(k p) n -> p k n", p=128), writes=[wbuf_u[wkv_i]])
            if is_s:
                wqs_i = next_wbuf()
                wqs_f = wbuf[wqs_i][:, 0:3 * 800].rearrange("p (k n) -> p k n", n=800)
                wqs = wqs_f[:, :, 0:768].rearrange("p k (h e) -> p k h e", e=96)
                P.op(DVE, lambda: nc.vector.memset(wbuf[wqs_i][:, 0:3 * 800], 0.0), writes=[wbuf_u[wqs_i]])
                src4 = w_uq[l].rearrange("(k p) (h e) -> p k h e", p=128, e=96)
                for k in range(3):
                    P.dma(GQ, wbuf_sem[wqs_i], wqs[:, k, :, 64:80], src4[:, k, :, 80:96], writes=[wbuf_u[wqs_i]], nc_ok=True)
                    P.dma(GQ, wbuf_sem[wqs_i], wqs[:, k, :, 80:96], src4[:, k, :, 64:80], writes=[wbuf_u[wqs_i]], nc_ok=True)
            wb_reserved.update({wq_i, wkv_i} | ({wqs_i} if is_s else set()))
            return (wq_i, wq, wkv_i, wkv) + ((wqs_i, wqs_f, wqs) if is_s else (None, None, None))

        segs = [(0, TOK)] if is_s else [(s_ * 256, 256) for s_ in range(4)]
        nseg = len(segs)
        L = segs[0][1]
        LP = L + 3
        xpad_v = xpad[:, 0:nseg * LP].rearrange("p (s t) -> p s t", t=LP)
        P.op(DVE, lambda: nc.vector.memset(xpad, 0.0), writes=[lru_u["xpad"]])
        def load_lru_w(c):
            wi = next_wbuf()
            wv = wbuf[wi][:, 0:2048].rearrange("p (k n) -> p k n", n=256)
            P.dma(GQ, wbuf_sem[wi], wv[:, :, 0:128], w_in[l][:, OFF_UX + c * 128:OFF_UX + (c + 1) * 128].rearrange("(k p) n -> p k n", p=128),
                  writes=[wbuf_u[wi]])
            P.dma(GQ, wbuf_sem[wi], wv[:, :, 128:256], w_in[l][:, OFF_UG + c * 128:OFF_UG + (c + 1) * 128].rearrange("(k p) n -> p k n", p=128),
                  writes=[wbuf_u[wi]])
            return wi, wv

        lru_w0 = load_lru_w(0)
        wq_i, wq, wkv_i, wkv, wqs_i, wqs_f, wqs = fetch_attn_weights()
        for c in range(4):
            wi, wv = lru_w0 if c == 0 else load_lru_w(c)
            for qb in range(2):
                hr = [hT_u[4 * qb + t] for t in range(4)]
                bx_, bg_ = nextbank("A"), nextbank("B")
                for k in range(8):
                    P.op(PE, lambda k=k: nc.tensor.matmul(banks[bx_][:], lhsT=wv[:, k, 0:128], rhs=hT[:, k, qb * 512:(qb + 1) * 512],
                                                            start=(k == 0), stop=(k == 7)),
                         reads=[wbuf_u[wi]] + hr, writes=[bank_u[bx_]])
                for k in range(8):
                    P.op(PE, lambda k=k: nc.tensor.matmul(banks[bg_][:], lhsT=wv[:, k, 128:256], rhs=hT[:, k, qb * 512:(qb + 1) * 512],
                                                            start=(k == 0), stop=(k == 7)),
                         reads=[wbuf_u[wi]] + hr, writes=[bank_u[bg_]])
                if is_s:
                    P.op(ACT, lambda: nc.scalar.copy(out=xpad_v[:, 0, 2 + qb * 512:2 + (qb + 1) * 512], in_=banks[bx_][:]),
                         reads=[bank_u[bx_]], writes=[lru_u["xpad"]])
                else:
                    P.op(ACT, lambda: nc.scalar.copy(out=xpad_v[:, 2 * qb:2 * qb + 2, 2:2 + L],
                                                     in_=banks[bx_][:].rearrange("p (s t) -> p s t", t=L)),
                         reads=[bank_u[bx_]], writes=[lru_u["xpad"]])
                P.op(ACT, lambda: nc.scalar.activation(out=gg[:, qb * 512:(qb + 1) * 512], in_=banks[bg_][:], func=AF.Gelu_apprx_tanh),
                     reads=[bank_u[bg_]], writes=[lru_u["gg"]])
            xc_v = xc.rearrange("p (s t) -> p s t", t=L)
            P.op(DVE, lambda: nc.vector.tensor_scalar(out=xc_v, in0=xpad_v[:, :, 0:L], scalar1=cw_t[:, l, 0, c:c + 1],
                                                      scalar2=cb_t[:, l, c:c + 1], op0=ALU.mult, op1=ALU.add),
                 reads=[lru_u["xpad"], const_u], writes=[lru_u["xc"]])
            for kk in range(1, 4):
                P.op(DVE, lambda kk=kk: nc.vector.scalar_tensor_tensor(out=xc_v, in0=xpad_v[:, :, kk:kk + L], scalar=cw_t[:, l, kk, c:c + 1],
                                                                        in1=xc_v, op0=ALU.mult, op1=ALU.add),
                     reads=[lru_u["xpad"], lru_u["xc"], const_u], writes=[lru_u["xc"]])
            P.op(ACT, lambda: nc.scalar.copy(out=xcb, in_=xc), reads=[lru_u["xc"]], writes=[lru_u["xcb"]])
            for d in range(2):
                hdst, hdu = (hf_t, lru_u["hf"]) if d == 0 else (hb2_t, lru_u["hb"])
                for qb in range(2):
                    br_, bi_ = nextbank("A"), nextbank("B")
                    sl = slice(qb * 512, (qb + 1) * 512)
                    P.op(PE, lambda: nc.tensor.matmul(banks[br_][:], lhsT=bd_t[:, 0, d, c, :], rhs=xcb[:, sl], start=True, stop=True),
                         reads=[bd_u, lru_u["xcb"]], writes=[bank_u[br_]])
                    P.op(PE, lambda: nc.tensor.matmul(banks[bi_][:], lhsT=bd_t[:, 1, d, c, :], rhs=xcb[:, sl], start=True, stop=True),
                         reads=[bd_u, lru_u["xcb"]], writes=[bank_u[bi_]])
                    P.op(ACT, lambda: nc.scalar.activation(out=ra[:, sl], in_=banks[br_][:], func=AF.Sigmoid, bias=ba_t[:, l, d, c:c + 1], scale=1.0),
                         reads=[bank_u[br_], const_u], writes=[lru_u["ra"]])
                    P.op(ACT, lambda: nc.scalar.activation(out=iu[:, sl], in_=banks[bi_][:], func=AF.Sigmoid, bias=bx_t[:, l, d, c:c + 1], scale=1.0),
                         reads=[bank_u[bi_], const_u], writes=[lru_u["iu"]])
                P.op(ACT, lambda: nc.scalar.activation(out=ss, in_=ra, func=AF.Exp, scale=neg2c_t[:, l, d, c:c + 1]),
                     reads=[lru_u["ra"], lru_c_u, fso_u], writes=[lru_u["ss"]])
                P.op(ACT, lambda: nc.scalar.activation(out=ra, in_=ra, func=AF.Exp, scale=negc_t[:, l, d, c:c + 1]),
                     reads=[lru_u["ra"], lru_c_u], writes=[lru_u["ra"]])
                P.op(ACT, lambda: nc.scalar.activation(out=ss, in_=ss, func=AF.Sqrt, bias=1.0, scale=-1.0),
                     reads=[lru_u["ss"]], writes=[lru_u["ss"]])
                P.op(DVE, lambda: nc.vector.tensor_tensor(out=iu, in0=iu, in1=xc, op=ALU.mult),
                     reads=[lru_u["iu"], lru_u["xc"]], writes=[lru_u["iu"]])
                P.op(DVE, lambda: nc.vector.tensor_tensor(out=iu, in0=iu, in1=ss, op=ALU.mult),
                     reads=[lru_u["iu"], lru_u["ss"]], writes=[lru_u["iu"]])
                for (s0, ln) in segs:
                    if d == 0:
                        o_ap, a_ap, u_ap = hdst[:, s0:s0 + ln], ra[:, s0:s0 + ln], iu[:, s0:s0 + ln]
                    else:
                        o_ap, a_ap, u_ap = hdst[:, s0:s0 + ln][:, ::-1], ra[:, s0:s0 + ln][:, ::-1], iu[:, s0:s0 + ln][:, ::-1]
                    init = h0_t[:, l, d, c:c + 1] if is_s else 0.0
                    P.op(DVE, lambda o_ap=o_ap, a_ap=a_ap, u_ap=u_ap, init=init: nc.vector.tensor_tensor_scan(
                        out=o_ap, data0=a_ap, data1=u_ap, initial=init, op0=ALU.mult, op1=ALU.add),
                         reads=[lru_u["ra"], lru_u["iu"], const_u], writes=[hdu])
            if not is_s:
                hfv = hf_t.rearrange("p (s t) -> p s t", t=L)
                hbv = hb2_t.rearrange("p (s t) -> p s t", t=L)
                P.op(ACT, lambda: nc.scalar.copy(out=fs_t[:, :, l, 0, c:c + 1], in_=hfv[:, :, L - 1:L]),
                     reads=[lru_u["hf"]], writes=[fs_u])
                P.op(ACT, lambda: nc.scalar.copy(out=fs_t[:, :, l, 1, c:c + 1], in_=hbv[:, :, 0:1]),
                     reads=[lru_u["hb"]], writes=[fs_u])
            P.op(GQ, lambda: nc.gpsimd.tensor_tensor(out=hf_t, in0=hf_t, in1=hb2_t, op=ALU.add),
                 reads=[lru_u["hf"], lru_u["hb"]], writes=[lru_u["hf"]])
            P.op(DVE, lambda: nc.vector.tensor_tensor(out=mixT[:, 4 + c, :], in0=hf_t, in1=gg, op=ALU.mult),
                 reads=[lru_u["hf"], lru_u["gg"]], writes=mix_u[4 + c])

        fz2 = P.fence()
        o = U0
        hd = []
        for s_ in range(2):
            qh = aview(o, TOK, BF16); o += 2048
            Kh = aview(o, 1280, BF16); o += 2560
            Vh = aview(o, 1280, BF16).rearrange("p (t n) -> p t n", n=128); o += 2560
            hd.append((qh, Kh, Vh, Unit(fz2), Unit(fz2), Unit(fz2)))
        pT_t = []
        for s_ in range(2):
            pT_t.append((aview(o, 1024, BF16), Unit(fz2))); o += 2048
        rc_t = []
        for s_ in range(2):
            rc_t.append((aview(o, 512, F32), Unit(fz2))); o += 2048
        rp_t = []
        for s_ in range(2):
            rp_t.append((aview(o, 512, F32), Unit(fz2))); o += 2048
        assert o <= ARENA_BYTES
        for s_ in range(2):
            P.op(DVE, lambda s_=s_: nc.vector.memset(hd[s_][2][:, :, 64:128], 1.0), writes=[hd[s_][5]])
        if is_s:
            asegs = [(0, 512, list(range(10))), (512, 512, list(range(10)))]
        else:
            asegs = [(s_ * 256, 256, [2 * s_, 2 * s_ + 1]) for s_ in range(4)]
        ptc = [0]
        rcc = [0]

        def prep_head(h):
            qh, Kh, Vh, qh_u, Kh_u, Vh_u = hd[h % 2]
            pieces = []

            def q_piece(qb):
                sl = slice(qb * 512, (qb + 1) * 512)
                cr = [cqT_u[4 * qb + t] for t in range(4)]
                ba_ = nextbank("A")
                for k in range(3):
                    P.op(PE, lambda k=k: nc.tensor.matmul(banks[ba_][:, :], lhsT=wq[:, k, h * 96:h * 96 + 128], rhs=cqT[:, k, sl],
                                                            start=(k == 0), stop=(k == 2)),
                         reads=[wbuf_u[wq_i]] + cr, writes=[bank_u[ba_]])
                if not is_s:
                    P.op(ACT, lambda: nc.scalar.copy(out=qh[0:96, sl], in_=banks[ba_][0:96, :]), reads=[bank_u[ba_]], writes=[qh_u])
                else:
                    bb_ = nextbank("B")
                    for k in range(3):
                        P.op(PE, lambda k=k: nc.tensor.matmul(banks[bb_][:, :], lhsT=wqs_f[:, k, h * 96:h * 96 + 128], rhs=cqT[:, k, sl],
                                                                start=(k == 0), stop=(k == 2)),
                             reads=[wbuf_u[wqs_i]] + cr, writes=[bank_u[bb_]])
                    P.op(ACT, lambda: nc.scalar.copy(out=qh[0:64, sl], in_=banks[ba_][0:64, :]), reads=[bank_u[ba_]], writes=[qh_u])
                    (t1, t1u), (t2, t2u) = rp_t[0], rp_t[1]
                    P.op(DVE, lambda: nc.vector.tensor_tensor(out=t1[64:96, :], in0=banks[ba_][64:96, :], in1=rope_t[64:96, 0, sl], op=ALU.mult),
                         reads=[bank_u[ba_], const_u], writes=[t1u])
                    P.op(DVE, lambda: nc.vector.tensor_tensor(out=t2[64:96, :], in0=banks[bb_][64:96, :], in1=rope_t[64:96, 1, sl], op=ALU.mult),
                         reads=[bank_u[bb_], const_u], writes=[t2u])
                    P.op(DVE, lambda: nc.vector.tensor_tensor(out=qh[64:96, sl], in0=t1[64:96, :], in1=t2[64:96, :], op=ALU.add),
                         reads=[t1u, t2u], writes=[qh_u])
            for qb in range(2):
                pieces.append(lambda qb=qb: q_piece(qb))

            def k_piece(k0):
                n = min(512, nkeys - k0)
                kr_ = [ckvT_u[t] for t in range(k0 // 128, (k0 + n) // 128)]
                ba_ = nextbank("A")
                for k in range(2):
                    P.op(PE, lambda k=k: nc.tensor.matmul(banks[ba_][:, 0:n], lhsT=wkv[:, k, h * 128:h * 128 + 128], rhs=ckvT[:, k, k0:k0 + n],
                                                            start=(k == 0), stop=(k == 1)),
                         reads=[wbuf_u[wkv_i]] + kr_, writes=[bank_u[ba_]])
                P.op(DVE, lambda: nc.vector.tensor_copy(out=Kh[0:64, k0:k0 + n], in_=banks[ba_][0:64, 0:n]), reads=[bank_u[ba_]], writes=[Kh_u])

            for k0 in range(0, nkeys, 512):
                pieces.append(lambda k0=k0: k_piece(k0))
            pieces.append(lambda: P.op(DVE, lambda: nc.vector.tensor_copy(out=Kh[64:96, 0:nkeys], in_=krT[64:96, 0:nkeys]),
                                       reads=[krT_u], writes=[Kh_u]))

            def v_piece(t0):
                nt_ = min(8, nkt - t0)
                bb_ = nextbank("B")
                pv = banks[bb_][:, 0:nt_ * 64].rearrange("p (t n) -> p t n", n=64)
                for t in range(nt_):
                    for k in range(2):
                        P.op(PE, lambda k=k, t=t: nc.tensor.matmul(pv[:, t, :], lhsT=ckvT[:, k, (t0 + t) * 128:(t0 + t + 1) * 128],
                                                                     rhs=wkv[:, k, h * 128 + 64:h * 128 + 128], start=(k == 0), stop=(k == 1)),
                             reads=[wbuf_u[wkv_i], ckvT_u[t0 + t]], writes=[bank_u[bb_]])
                P.op(DVE, lambda: nc.vector.tensor_copy(out=Vh[:, t0:t0 + nt_, 0:64], in_=pv), reads=[bank_u[bb_]], writes=[Vh_u])

            for t0 in range(0, nkt, 8):
                pieces.append(lambda t0=t0: v_piece(t0))
            return pieces

        sc_cnt = [0]
        acc_cnt = [0]
        pending_norm = []

        def attn_head(h, pend):
            qh, Kh, Vh, qh_u, Kh_u, Vh_u = hd[h % 2]
            for (q0, nq, kts) in asegs:
                bo_ = (2, 3)[acc_cnt[0] % 2] if is_s else (3, 6, 7)[acc_cnt[0] % 3]
                acc_cnt[0] += 1
                groups_ = [(kts[2 * p], kts[2 * p + 1]) for p in range(len(kts) // 2)]
                sb = {}

                def score(gi):
                    if is_s:
                        b0 = (4, 6)[sc_cnt[0] % 2]
                        sc_cnt[0] += 1
                        outs = [(banks[b0][:, 0:nq], bank_u[b0]), (banks[b0 + 1][:, 0:nq], bank_u[b0 + 1])]
                        us = [bank_u[b0], bank_u[b0 + 1]]
                    else:
                        b0 = nextbank("C")
                        outs = [(banks[b0][:, 0:nq], bank_u[b0]), (banks[b0][:, nq:2 * nq], bank_u[b0])]
                        us = [bank_u[b0]]
                    for j, kt in enumerate(groups_[gi]):
                        o_ap, o_u = outs[j]
                        P.op(PE, lambda: nc.tensor.matmul(o_ap, lhsT=Kh[0:96, kt * 128:(kt + 1) * 128], rhs=qh[0:96, q0:q0 + nq],
                                                           start=True, stop=True),
                             reads=[Kh_u, qh_u], writes=[o_u])
                    sb[gi] = (b0, us)

                score(0)
                for gi, grp_ in enumerate(groups_):
                    if gi + 1 < len(groups_):
                        score(gi + 1)
                    if pend:
                        pend.pop(0)()
                    if pend and not is_s:
                        pend.pop(0)()
                    b0, us = sb[gi]
                    pt, ptu = pT_t[ptc[0] % 2]
                    ptc[0] += 1
                    P.op(ACT, lambda: nc.scalar.activation(out=pt[:, 0:2 * nq], in_=bigps[:, b0 * 512:b0 * 512 + 2 * nq], func=AF.Exp,
                                                           scale=ATTN_SCALE),
                         reads=us, writes=[ptu])
                    for j, kt in enumerate(grp_):
                        P.op(PE, lambda: nc.tensor.matmul(banks[bo_][:, 0:nq], lhsT=Vh[:, kt, :], rhs=pt[:, j * nq:(j + 1) * nq],
                                                           start=(gi == 0 and j == 0), stop=(gi == len(groups_) - 1 and j == 1)),
                             reads=[Vh_u, ptu], writes=[bank_u[bo_]])
                    if gi == 0:
                        while pending_norm:
                            pending_norm.pop(0)()

                def norm(bo_=bo_, q0=q0, nq=nq, h=h):
                    rc, rcu = rc_t[rcc[0] % 2]
                    rcc[0] += 1
                    P.op(DVE, lambda: nc.vector.reciprocal(out=rc[64:128, 0:nq], in_=banks[bo_][64:128, 0:nq]), reads=[bank_u[bo_]], writes=[rcu])
                    pb = (h % 2) * 64
                    mu = [mix_u[h // 2][t] for t in range(q0 // 128, (q0 + nq) // 128)]
                    P.op(DVE, lambda: nc.vector.tensor_tensor(out=mixT[pb:pb + 64, h // 2, q0:q0 + nq], in0=banks[bo_][0:64, 0:nq],
                                                              in1=rc[64:128, 0:nq], op=ALU.mult),
                         reads=[bank_u[bo_], rcu], writes=mu)

                pending_norm.append(norm)

        rot_a_saved, rot_b_saved = rot["A"], rot["B"]
        if is_s:
            rot["A"], rot["B"] = [0], [1]
        else:
            rot["B"] = [2]
        for pc in prep_head(0):
            pc()
        for h in range(NH):
            pend = prep_head(h + 1) if h + 1 < NH else []
            attn_head(h, pend)
            while pend:
                pend.pop(0)()
        while pending_norm:
            pending_norm.pop(0)()
        rot["A"], rot["B"] = rot_a_saved, rot_b_saved
        wb_reserved.clear()

        emit_gm_gb(l, 1)
        if nxt is not None:
            emit_st(nxt[0], nxt[1], (l, 1))
        wo_v = []
        for dh in range(2):
            wo_v.append(load_w_piece(w_o[l][:, dh * 512:(dh + 1) * 512], 512))
        for dh in range(2):
            wi, wv = wo_v[dh]
            for e in range(8):
                P.op(DVE, lambda e=e: nc.vector.tensor_tensor(out=wv[:, e, :], in0=wv[:, e, :], in1=e_gm[:, dh * 512:(dh + 1) * 512], op=ALU.mult),
                     reads=[wbuf_u[wi], e_u["gm"]], writes=[wbuf_u[wi]])
        do_prefetch(nxt)

        def wo_tile(i):
            for dh in range(2):
                wi, wv = wo_v[dh]
                b = nextbank("A") if dh == 0 else nextbank("B")
                for e in range(8):
                    P.op(PE, lambda e=e: nc.tensor.matmul(banks[b][:], lhsT=mixT[:, e, i * 128:(i + 1) * 128], rhs=wv[:, e, :],
                                                            start=(e == 0), stop=(e == 7)),
                         reads=[mix_u[e][i], wbuf_u[wi]], writes=[bank_u[b]])
                residual_half(i, dh, b, 1.0 / ALPHA)

        epilogue_tiles(range(NT), nxt is not None, pre=wo_tile)

    lru_c_u = rope_u
    def load_consts():
        P.dma(SP, const_sem, cw_t[:], conv_w.rearrange("l k (c p) -> p l k c", p=128), writes=[const_u], nc_ok=True)
        P.dma(SP, const_sem, cb_t[:], conv_b.rearrange("l (c p) -> p l c", p=128), writes=[const_u], nc_ok=True)
        P.dma(SP, const_sem, ba_t[:], lru_b_a.rearrange("l d (c p) -> p l d c", p=128), writes=[const_u], nc_ok=True)
        P.dma(SP, const_sem, bx_t[:], lru_b_x.rearrange("l d (c p) -> p l d c", p=128), writes=[const_u], nc_ok=True)
        P.dma(SP, const_sem, lam_t[:], lru_lambda.rearrange("l d (c p) -> p l d c", p=128), writes=[const_u], nc_ok=True)
        P.dma(SP, const_sem, h0_t[:], state_lru.rearrange("l d (c p) -> p l d c", p=128), writes=[const_u], nc_ok=True)
        P.dma(SP, const_sem, rope_t[64:96, :, :], rope_cs.rearrange("a r t -> r a t"), writes=[const_u])
        P.dma(SP, const_sem, bmodT[:], b_mod.rearrange("l (v p) -> p l v", p=128), writes=[const_u], nc_ok=True)
        P.dma(SP, const_sem, lngT[:], ln_g.rearrange("l j (c p) -> p l j c", p=128), writes=[const_u], nc_ok=True)
        P.dma(SP, const_sem, lnbT[:], ln_b.rearrange("l j (c p) -> p l j c", p=128), writes=[const_u], nc_ok=True)
        lamf = lam_t[:].rearrange("p l d c -> p (l d c)")
        negcf = negc_t[:].rearrange("p l d c -> p (l d c)")
        neg2cf = neg2c_t[:].rearrange("p l d c -> p (l d c)")
        P.op(ACT, lambda: nc.scalar.activation(out=negcf, in_=lamf, func=AF.Exp, scale=-1.0), reads=[const_u], writes=[rope_u])
        P.op(ACT, lambda: nc.scalar.activation(out=negcf, in_=negcf, func=AF.Ln, bias=1.0, scale=1.0), reads=[rope_u], writes=[rope_u])
        P.op(ACT, lambda: nc.scalar.mul(out=neg2cf, in_=negcf, mul=-16.0), reads=[rope_u], writes=[fso_u])
        P.op(ACT, lambda: nc.scalar.mul(out=negcf, in_=negcf, mul=-8.0), reads=[rope_u, fso_u], writes=[rope_u])
        P.op(DVE, lambda: nc.vector.memset(bd_t[:].rearrange("p a b c n -> p (a b c n)"), 0.0), writes=[bd_u])
        P.op(DVE, lambda: nc.vector.memset(fs_t[:].rearrange("p a b c n -> p (a b c n)"), 0.0), writes=[fs_u])


    P.dma(SP, const_sem, ident_f[:], ident_in, writes=[const_u])
    const2_sem = P.new_sem("d_const2")
    P.dma(GQ, const2_sem, ident_b[:], ident_in, writes=[const_u])
    P.op(DVE, lambda: nc.vector.memset(ones_f[:], 1.0), writes=[screp_u])
    consts_loaded = [False]
    for grp in groups:
        is_s = grp == "S"
        for i in range(NT):
            src = x_sample[i * 128:(i + 1) * 128, :] if is_s else x_prompt[i // 2, (i % 2) * 128:(i % 2 + 1) * 128, :]
            P.dma(SP, x_sem[i], x_t[:, i, :], src, writes=[x_u[i]])
        csrc = c_in if is_s else c_ctx
        P.dma(SP, cond_sem, cond_f[:], csrc.rearrange("(k p) -> p k", p=128), writes=[cond_u], nc_ok=True)
        P.op(ACT, lambda: nc.scalar.activation(out=cond_s[:], in_=cond_f[:], func=AF.Silu), reads=[cond_u], writes=[cond_u])
        for k in range(8):
            P.op(DVE, lambda k=k: nc.vector.tensor_scalar(out=sc_rep[:, k, :], in0=ones_f[:], scalar1=cond_s[:, k:k + 1], scalar2=None,
                                                          op0=ALU.mult),
                 reads=[cond_u, screp_u], writes=[screp_u])
        emit_st_first = True
        if not consts_loaded[0]:
            P.dma(SP, const_sem, bmodT[:, 0, 0:16], b_mod[0, 0:2048].rearrange("(v p) -> p v", p=128), writes=[const_u], nc_ok=True)
        emit_st(0, 0, None)
        for i in range(NT):
            make_h(i, x_t[:, i, :], x_u[i])
        if not consts_loaded[0]:
            load_consts()
            consts_loaded[0] = True
        subs = [(l, j) for l in range(n_layers) for j in range(3)]
        for idx, (l, j) in enumerate(subs):
            nxt = subs[idx + 1] if idx + 1 < len(subs) else None
            if j == 1:
                mixer_sublayer(l, grp, nxt)
            else:
                ffn_sublayer(l, j, nxt)
            if debug and idx == 0:
                dbg_x = nc.dram_tensor("dbg_x", [128, NT, D], F32, kind="ExternalOutput").ap()
                dbg_h = nc.dram_tensor("dbg_h", [128, 8, TOK], BF16, kind="ExternalOutput").ap()
                dbg_e = nc.dram_tensor("dbg_e", [5, 128, D], F32, kind="ExternalOutput").ap()
                dbg_sems = [P.new_sem(f"d_dbg{q}") for q in range(7)]
                P.dma(SP, dbg_sems[5], dbg_x, x_t[:], reads=x_u)
                P.dma(SP, dbg_sems[6], dbg_h, hT[:], reads=hT_u)
                for q, (tl, k_) in enumerate(((e_gm, "gm"), (e_g, "g"), (e_b, "b"))):
                    P.dma(SP, dbg_sems[q], dbg_e[q], tl[:], reads=[e_u[k_]])
        for i in range(NT):
            dst = y_sample[i * 128:(i + 1) * 128, :] if is_s else y_prompt[i // 2, (i % 2) * 128:(i % 2 + 1) * 128, :]
            P.dma(SP, x_sem[i], dst, x_t[:, i, :], reads=[x_u[i]])
        if not is_s:
            b = nextbank("C")
            P.op(PE, lambda: nc.tensor.transpose(banks[b][:, 0:128], fs_t[:].rearrange("p a b c n -> p (a b c n)"), ident_f[:]),
                 reads=[fs_u, const_u], writes=[bank_u[b]])
            P.op(ACT, lambda: nc.scalar.copy(out=fs_o[:], in_=banks[b][:, 0:128]), reads=[bank_u[b]], writes=[fso_u])
            P.dma(SP, fso_sem, new_state, fs_o[:], reads=[fso_u])

    for s in P.sems:
        if s.name.startswith("d_") and s.cnt > 0:
            nc.sync.wait_ge(s.h, s.cnt)
    return P


_CACHE = {}


def _rope_tables():
    n_freq = ROPE // 4
    inv = (10000.0 ** (-np.arange(n_freq, dtype=np.float32) / n_freq)).astype(np.float32)
    t = np.arange(TOK)
    row = (t // 64).astype(np.float32)
    col = (t % 64).astype(np.float32)
    ang = np.concatenate([row[:, None] * inv, col[:, None] * inv], axis=-1).astype(np.float32)
    cos = np.cos(ang).astype(np.float32).T
    sin = np.sin(ang).astype(np.float32).T
    C = np.concatenate([cos, cos], axis=0)
    S = np.concatenate([-sin, sin], axis=0)
    return np.ascontiguousarray(np.stack([C, S], axis=0)).astype(np.float32)


def kernel(**inputs):
    if "prog" not in _CACHE:
        _CACHE["prog"] = build_program()
    P = _CACHE["prog"]
    f = lambda a: np.ascontiguousarray(np.asarray(a, dtype=np.float32))
    shared = {k: f(inputs[k]) for k in (
        "c_ctx", "w_mod", "b_mod", "ln_g", "ln_b", "w_ffn_up", "w_ffn_down", "w_in", "q_norm_g", "kv_norm_g",
        "w_uq", "w_ukv", "conv_w", "conv_b", "lru_w_a", "lru_b_a", "lru_w_x", "lru_b_x", "lru_lambda", "w_o")}
    shared["ident"] = np.eye(128, dtype=np.float32)
    shared["rope_cs"] = _rope_tables()
    xp = f(inputs["x_prompt"]); xs = f(inputs["x_sample"])
    ckv = f(inputs["cache_ckv"]); ckr = f(inputs["cache_krope"]); stl = f(inputs["state_lru"]); cc = f(inputs["c"])
    in_maps = []
    for core in range(8):
        b = core % 4
        m = dict(shared)
        m["x_prompt"] = np.ascontiguousarray(xp[4 * core:4 * core + 4])
        m["x_sample"] = np.ascontiguousarray(xs[b])
        m["cache_ckv"] = np.ascontiguousarray(ckv[b])
        m["cache_krope"] = np.ascontiguousarray(ckr[b])
        m["state_lru"] = np.ascontiguousarray(stl[b])
        m["c"] = np.ascontiguousarray(cc[b])
        in_maps.append(m)
    res = run_bass_kernel_spmd(P.nc, in_maps, core_ids=list(range(8)))
    r = res.results
    y_prompt = np.concatenate([r[c]["y_prompt"] for c in range(8)], axis=0)
    y_sample = np.stack([r[c]["y_sample"] for c in range(4)], axis=0)
    new_ckv = np.concatenate([r[c]["new_ckv"] for c in range(8)], axis=0)
    new_krope = np.concatenate([r[c]["new_krope"] for c in range(8)], axis=0)
    new_state = np.concatenate([r[c]["new_state"].reshape(4, DEPTH, 2, LRUW) for c in range(8)], axis=0)
    return (y_prompt.astype(np.float32), y_sample.astype(np.float32), new_ckv.astype(np.float32),
            new_krope.astype(np.float32), new_state.astype(np.float32))
```

```python
import math
from contextlib import ExitStack

import numpy as np
import concourse.bass as bass
import concourse.mybir as mybir
from concourse.bass_utils import run_bass_kernel_spmd

F32 = mybir.dt.float32
BF16 = mybir.dt.bfloat16
AF = mybir.ActivationFunctionType
ALU = mybir.AluOpType

D = 1024
DEPTH = 4
DFF = 2816
NFC = DFF // 128
TOK = 1024
NT = 8
QL, KVL, ROPE = 384, 256, 32
LRUW = 512
INW = QL + KVL + ROPE + 2 * LRUW
OFF_KV = QL
OFF_KR = QL + KVL
OFF_UX = QL + KVL + ROPE
OFF_UG = OFF_UX + LRUW
NH = 8
ALPHA = (2.0 * DEPTH) ** 0.25
LN_EPS_S = 1e-5 / (ALPHA * ALPHA)
RMS_EPS = 1e-6
ATTN_SCALE = 1.0 / math.sqrt(96.0)
N_LAYERS = DEPTH
GROUPS = ("P", "S")

ARENA_BYTES = 72704


class Sem:
    def __init__(self, handle, name):
        self.h = handle
        self.name = name
        self.cnt = 0


class Unit:
    __slots__ = ("w", "r")

    def __init__(self, fence=None):
        self.w = None
        self.r = dict(fence) if fence else {}


class Eng:
    def __init__(self, h, sem):
        self.h = h
        self.sem = sem
        self.known = {}
        self.relaxed = False


class Prog:
    def __init__(self):
        self.nc = bass.Bass("TRN2", target_bir_lowering=False)
        self.es = ExitStack()
        self.sems = []
        self.n_inst = 0

    def new_sem(self, name):
        s = Sem(self.es.enter_context(self.nc.semaphore(name)), name)
        self.sems.append(s)
        return s

    def fence(self):
        return {s: s.cnt for s in self.sems if s.cnt > 0}

    def sbuf(self, name, shape, dt):
        return self.es.enter_context(self.nc.sbuf_tensor(name, shape, dt))

    def psum(self, name, shape, dt):
        return self.es.enter_context(self.nc.psum_tensor(name, shape, dt))

    def _waits(self, eng, reads, writes, is_dma):
        need = {}

        def add(s, v):
            if need.get(s, 0) < v:
                need[s] = v

        for u in reads:
            if u.w is not None:
                add(*u.w)
        relaxed = eng.relaxed and not is_dma
        for u in writes:
            if u.w is not None and not (relaxed and u.w[0] is eng.sem):
                add(*u.w)
            for s, v in u.r.items():
                if not (relaxed and s is eng.sem):
                    add(s, v)
        for s, v in need.items():
            if eng.known.get(s, 0) < v:
                eng.h.wait_ge(s.h, v)
                eng.known[s] = v
                self.n_inst += 1

    def op(self, eng, fn, reads=(), writes=()):
        self._waits(eng, reads, writes, False)
        inst = fn()
        eng.sem.cnt += 1
        inst.then_inc(eng.sem.h, 1)
        mark = (eng.sem, eng.sem.cnt)
        self.n_inst += 1
        for u in reads:
            u.r[mark[0]] = mark[1]
        for u in writes:
            u.w = mark
            u.r = {}

    def dma(self, eng, sem, out, in_, reads=(), writes=(), nc_ok=False):
        self._waits(eng, reads, writes, True)
        if nc_ok:
            with self.nc.allow_non_contiguous_dma(reason="small strided parameter load"):
                inst = eng.h.dma_start(out=out, in_=in_)
        else:
            inst = eng.h.dma_start(out=out, in_=in_)
        sem.cnt += 16
        inst.then_inc(sem.h, 16)
        mark = (sem, sem.cnt)
        self.n_inst += 1
        for u in reads:
            u.r[mark[0]] = mark[1]
        for u in writes:
            u.w = mark
            u.r = {}


def build_program(n_layers=N_LAYERS, groups=GROUPS, debug=False):
    P = Prog()
    nc = P.nc

    def din(name, shape):
        return nc.dram_tensor(name, list(shape), F32, kind="ExternalInput").ap()

    def dout(name, shape):
        return nc.dram_tensor(name, list(shape), F32, kind="ExternalOutput").ap()

    x_prompt = din("x_prompt", [4, 256, D])
    x_sample = din("x_sample", [TOK, D])
    cache_ckv = din("cache_ckv", [DEPTH, 256, KVL])
    cache_krope = din("cache_krope", [DEPTH, 256, ROPE])
    state_lru = din("state_lru", [DEPTH, 2, LRUW])
    c_in = din("c", [D])
    c_ctx = din("c_ctx", [D])
    w_mod = din("w_mod", [DEPTH, D, 9 * D])
    b_mod = din("b_mod", [DEPTH, 9 * D])
    ln_g = din("ln_g", [DEPTH, 3, D])
    ln_b = din("ln_b", [DEPTH, 3, D])
    w_ffn_up = din("w_ffn_up", [DEPTH, 2, D, 2 * DFF])
    w_ffn_down = din("w_ffn_down", [DEPTH, 2, DFF, D])
    w_in = din("w_in", [DEPTH, D, INW])
    q_norm_g = din("q_norm_g", [DEPTH, QL])
    kv_norm_g = din("kv_norm_g", [DEPTH, KVL])
    w_uq = din("w_uq", [DEPTH, QL, NH * 96])
    w_ukv = din("w_ukv", [DEPTH, KVL, NH * 128])
    conv_w = din("conv_w", [DEPTH, 4, LRUW])
    conv_b = din("conv_b", [DEPTH, LRUW])
    lru_w_a = din("lru_w_a", [DEPTH, 2, 8, 64, 64])
    lru_b_a = din("lru_b_a", [DEPTH, 2, LRUW])
    lru_w_x = din("lru_w_x", [DEPTH, 2, 8, 64, 64])
    lru_b_x = din("lru_b_x", [DEPTH, 2, LRUW])
    lru_lambda = din("lru_lambda", [DEPTH, 2, LRUW])
    w_o = din("w_o", [DEPTH, D, D])
    ident_in = din("ident", [128, 128])
    rope_cs = din("rope_cs", [2, 32, TOK])

    y_prompt = dout("y_prompt", [4, 256, D])
    y_sample = dout("y_sample", [TOK, D])
    new_ckv = dout("new_ckv", [4, DEPTH, 256, KVL])
    new_krope = dout("new_krope", [4, DEPTH, 256, ROPE])
    new_state = dout("new_state", [128, 128])

    PE = Eng(nc.tensor, P.new_sem("s_pe"))
    PE.relaxed = True
    ACT = Eng(nc.scalar, P.new_sem("s_act"))
    DVE = Eng(nc.vector, P.new_sem("s_dve"))
    GQ = Eng(nc.gpsimd, P.new_sem("s_pool"))
    SP = Eng(nc.sync, P.new_sem("s_sp"))

    x_t = P.sbuf("x_t", [128, NT, D], F32)
    hT = P.sbuf("hT", [128, 8, TOK], BF16)
    arena = P.sbuf("arena", [128, ARENA_BYTES // 2], BF16)
    NWB = 4
    wbuf = [P.sbuf(f"wbuf{i}", [128, 4096], BF16) for i in range(NWB)]
    e_gm = P.sbuf("e_gm", [128, D], F32)
    e_g = P.sbuf("e_g", [128, D], F32)
    e_b = P.sbuf("e_b", [128, D], F32)
    NXN = 4
    xn_t = [P.sbuf(f"xn{i}", [128, D], F32) for i in range(NXN)]
    bmodT = P.sbuf("bmodT", [128, DEPTH, 72], F32)
    lngT = P.sbuf("lngT", [128, DEPTH, 3, 8], F32)
    lnbT = P.sbuf("lnbT", [128, DEPTH, 3, 8], F32)
    g2b2 = P.sbuf("g2b2", [128, 2, 16], F32)
    mT_t = P.sbuf("mT_t", [128, 24], F32)
    rope_t = P.sbuf("rope_t", [128, 2, TOK], F32)
    ident_b = P.sbuf("ident_b", [128, 128], BF16)
    ident_f = P.sbuf("ident_f", [128, 128], F32)
    ones_f = P.sbuf("ones_f", [128, 128], F32)
    sc_rep = P.sbuf("sc_rep", [128, 8, 128], BF16)
    cond_f = P.sbuf("cond_f", [128, 8], F32)
    cond_s = P.sbuf("cond_s", [128, 8], F32)
    qg_bc = P.sbuf("qg_bc", [128, QL], F32)
    kg_bc = P.sbuf("kg_bc", [128, KVL], F32)
    cw_t = P.sbuf("cw_t", [128, DEPTH, 4, 4], F32)
    cb_t = P.sbuf("cb_t", [128, DEPTH, 4], F32)
    ba_t = P.sbuf("ba_t", [128, DEPTH, 2, 4], F32)
    bx_t = P.sbuf("bx_t", [128, DEPTH, 2, 4], F32)
    lam_t = P.sbuf("lam_t", [128, DEPTH, 2, 4], F32)
    negc_t = P.sbuf("negc_t", [128, DEPTH, 2, 4], F32)
    neg2c_t = P.sbuf("neg2c_t", [128, DEPTH, 2, 4], F32)
    h0_t = P.sbuf("h0_t", [128, DEPTH, 2, 4], F32)
    fs_t = P.sbuf("fs_t", [128, 4, DEPTH, 2, 4], F32)
    fs_o = P.sbuf("fs_o", [128, 128], F32)
    bd_t = P.sbuf("bd_t", [128, 2, 2, 4, 128], BF16)
    NST = 6
    st_t = [P.sbuf(f"st{i}", [128, 2, 6], F32) for i in range(NST)]
    mv_t = [P.sbuf(f"mv{i}", [128, 2], F32) for i in range(NST)]
    lsm_t = [P.sbuf(f"lsm{i}", [128, 4], F32) for i in range(NST)]
    sm_t = [P.sbuf(f"sm{i}", [128, 8], F32) for i in range(2)]

    bigps = P.psum("bigps", [128, 8 * 512], F32)
    banks = [bigps[:, i * 512:(i + 1) * 512] for i in range(8)]
    bank_u = [Unit() for _ in range(8)]
    rot = {"A": [0, 1], "B": [2, 3], "C": [4, 5], "D": [6, 7]}
    rot_i = {k: 0 for k in rot}

    def nextbank(cls):
        b = rot[cls][rot_i[cls] % len(rot[cls])]
        rot_i[cls] += 1
        return b

    x_u = [Unit() for _ in range(NT)]
    hT_u = [Unit() for _ in range(NT)]
    wbuf_u = [Unit() for _ in range(NWB)]
    wbuf_sem = [P.new_sem(f"d_wb{i}") for i in range(NWB)]
    wb_i = [0]
    e_u = {k: Unit() for k in ("gm", "g", "b")}
    g2b2_u = [Unit(), Unit()]
    mT_u = Unit()
    g2_i = [0]
    e_sem = {k: P.new_sem(f"d_e_{k}") for k in ("g", "b", "bm")}
    xn_u = [Unit() for _ in range(NXN)]
    st_u = [Unit() for _ in range(NST)]
    x_sem = [P.new_sem(f"d_x{i}") for i in range(NT)]
    const_u = Unit()
    const_sem = P.new_sem("d_const")
    cond_u = Unit()
    cond_sem = P.new_sem("d_cond")
    screp_u = Unit()
    norm_u = Unit()
    norm_sem = P.new_sem("d_norm")
    bd_u = Unit()
    bd_sem = P.new_sem("d_bd")
    rope_u = Unit()
    fs_u = Unit()
    fso_u = Unit()
    fso_sem = P.new_sem("d_fso")
    arena_dma_sems = [P.new_sem(f"d_ar{i}") for i in range(8)]

    def aview(off, n_elem, dt):
        if dt == BF16:
            return arena[:, off // 2: off // 2 + n_elem]
        return arena[:, off // 2: off // 2 + 2 * n_elem].bitcast(F32)

    wb_reserved = set()

    def next_wbuf():
        while True:
            i = wb_i[0] % NWB
            wb_i[0] += 1
            if i not in wb_reserved:
                return i

    prefetched = {}

    def load_wup(l, jj, fp):
        wup = w_ffn_up[l, jj]
        wi = next_wbuf()
        wv = wbuf[wi][:, 0:4096].rearrange("p (k n) -> p k n", n=512)
        P.dma(GQ, wbuf_sem[wi], wv[:, :, 0:256], wup[:, fp * 256:(fp + 1) * 256].rearrange("(k p) n -> p k n", p=128),
              writes=[wbuf_u[wi]])
        P.dma(GQ, wbuf_sem[wi], wv[:, :, 256:512],
              wup[:, DFF + fp * 256:DFF + (fp + 1) * 256].rearrange("(k p) n -> p k n", p=128), writes=[wbuf_u[wi]])
        return wi, wv

    def do_prefetch(nxt):
        if nxt is None:
            return
        l2, j2 = nxt
        if j2 == 1:
            prefetched["wA"] = load_w_piece(w_in[l2][:, 0:QL], QL)
            prefetched["wB"] = load_w_piece(w_in[l2][:, OFF_KV:OFF_KV + 288], 288)
        else:
            for fp in (0, 1):
                prefetched[("wup", fp)] = load_wup(l2, 0 if j2 == 0 else 1, fp)

    def load_w_piece(src_ap, ncols, extra=None):
        i = next_wbuf()
        dst = wbuf[i][:, 0:8 * ncols].rearrange("p (k n) -> p k n", n=ncols)
        P.dma(GQ, wbuf_sem[i], dst, src_ap.rearrange("(k p) n -> p k n", p=128), writes=[wbuf_u[i]])
        return i, dst

    def mod_vector(l, v, dest, dest_u, kind):
        P.dma(SP, e_sem["bm"], dest[:], b_mod[l, v * D:(v + 1) * D].partition_broadcast(128), writes=[dest_u])
        for half in range(2):
            c0 = v * D + half * 512
            wi, wv = load_w_piece(w_mod[l][:, c0:c0 + 512], 512)
            b = nextbank("D")
            for k in range(8):
                P.op(PE, lambda k=k: nc.tensor.matmul(banks[b][:], lhsT=sc_rep[:, k, :], rhs=wv[:, k, :],
                                                        start=(k == 0), stop=(k == 7)),
                     reads=[screp_u, wbuf_u[wi]], writes=[bank_u[b]])
            dsl = dest[:, half * 512:(half + 1) * 512]
            P.op(DVE, lambda: nc.vector.tensor_tensor(out=dsl, in0=banks[b][:], in1=dsl, op=ALU.add),
                 reads=[bank_u[b], dest_u], writes=[dest_u])

    def emit_st(l, j, prev):
        b = nextbank("D")
        ps = banks[b][:, 0:32]
        for vi in range(2):
            v = 3 * j + vi
            for half in range(2):
                c0 = v * D + half * 512
                wi, wv = load_w_piece(w_mod[l][:, c0:c0 + 512], 512)
                for cc in range(4):
                    col = 2 * (vi * 8 + half * 4 + cc)
                    for k in range(8):
                        P.op(PE, lambda k=k: nc.tensor.matmul(ps[:, col:col + 2], lhsT=wv[:, k, cc * 128:(cc + 1) * 128],
                                                                rhs=sc_rep[:, k, 0:2], start=(k == 0), stop=(k == 7)),
                             reads=[screp_u, wbuf_u[wi]], writes=[bank_u[b]])
        psv = ps.rearrange("p (c t) -> p c t", t=2)[:, :, 0]
        P.op(DVE, lambda: nc.vector.tensor_tensor(out=mT_t[:, 0:16], in0=psv, in1=bmodT[:, l, 3 * j * 8:3 * j * 8 + 16], op=ALU.add),
             reads=[bank_u[b], const_u], writes=[mT_u])
        gi = g2_i[0] % 2
        g2_i[0] += 1
        G2 = g2b2[:, gi, 0:8]
        B2 = g2b2[:, gi, 8:16]
        if prev is None:
            P.op(DVE, lambda: nc.vector.tensor_scalar(out=G2, in0=mT_t[:, 8:16], scalar1=1.0, scalar2=None, op0=ALU.add),
                 reads=[mT_u], writes=[g2b2_u[gi]])
            P.op(DVE, lambda: nc.vector.tensor_copy(out=B2, in_=mT_t[:, 0:8]), reads=[mT_u], writes=[g2b2_u[gi]])
        else:
            lp, jp = prev
            P.op(DVE, lambda: nc.vector.scalar_tensor_tensor(out=G2, in0=mT_t[:, 8:16], scalar=1.0, in1=lngT[:, lp, jp, :],
                                                              op0=ALU.add, op1=ALU.mult),
                 reads=[mT_u, const_u], writes=[g2b2_u[gi]])
            P.op(DVE, lambda: nc.vector.scalar_tensor_tensor(out=mT_t[:, 16:24], in0=mT_t[:, 8:16], scalar=1.0, in1=lnbT[:, lp, jp, :],
                                                              op0=ALU.add, op1=ALU.mult),
                 reads=[mT_u, const_u], writes=[mT_u])
            P.op(DVE, lambda: nc.vector.tensor_tensor(out=B2, in0=mT_t[:, 16:24], in1=mT_t[:, 0:8], op=ALU.add),
                 reads=[mT_u], writes=[g2b2_u[gi]])

    def emit_gm_gb(l, j):
        mod_vector(l, 3 * j + 2, e_gm, e_u["gm"], "T")
        P.dma(SP, e_sem["g"], e_g[:], ln_g[l, j].partition_broadcast(128), writes=[e_u["g"]])
        P.dma(SP, e_sem["b"], e_b[:], ln_b[l, j].partition_broadcast(128), writes=[e_u["b"]])

    tp_cnt = [0]

    def make_h(i, src, src_u, hi_banks=False):
        gi = (g2_i[0] - 1) % 2
        par = tp_cnt[0] % 2
        tp_cnt[0] += 1
        bks = (6, 7) if (hi_banks or par == 1) else (4, 5)
        for c in range(8):
            b = bks[c // 4]
            P.op(PE, lambda c=c, b=b: nc.tensor.transpose(banks[b][:, (c % 4) * 128:(c % 4 + 1) * 128], src[:, c * 128:(c + 1) * 128], ident_f[:]),
                 reads=[src_u, const_u], writes=[bank_u[b]])
        for c in range(8):
            b = bks[c // 4]
            if c % 2 == 0:
                P.op(ACT, lambda c=c, b=b: nc.scalar.activation(out=hT[:, c, i * 128:(i + 1) * 128], in_=banks[b][:, (c % 4) * 128:(c % 4 + 1) * 128],
                                                               func=AF.Identity, bias=g2b2[:, gi, 8 + c:9 + c], scale=g2b2[:, gi, c:c + 1]),
                     reads=[bank_u[b], g2b2_u[gi]], writes=[hT_u[i]])
            else:
                P.op(DVE, lambda c=c, b=b: nc.vector.tensor_scalar(out=hT[:, c, i * 128:(i + 1) * 128], in0=banks[b][:, (c % 4) * 128:(c % 4 + 1) * 128],
                                                                  scalar1=g2b2[:, gi, c:c + 1], scalar2=g2b2[:, gi, 8 + c:9 + c],
                                                                  op0=ALU.mult, op1=ALU.add),
                     reads=[bank_u[b], g2b2_u[gi]], writes=[hT_u[i]])

    def residual_half(i, dh, b, kfac):
        xs = x_t[:, i, dh * 512:(dh + 1) * 512]
        P.op(DVE, lambda: nc.vector.scalar_tensor_tensor(out=xs, in0=banks[b][:], scalar=kfac, in1=xs, op0=ALU.mult, op1=ALU.add),
             reads=[bank_u[b], x_u[i]], writes=[x_u[i]])

    ln_cnt = [0]

    def epilogue_tiles(tiles, do_h, pre=None, bank_sets=((4, 5), (6, 7))):
        tiles = list(tiles)
        n = len(tiles)
        slot = {}
        gi = (g2_i[0] - 1) % 2

        def stA(j, i):
            sl_ = ln_cnt[0] % NST
            xs_ = ln_cnt[0] % NXN
            ln_cnt[0] += 1
            slot[j] = (sl_, xs_)
            if pre is not None:
                pre(i)
            st, mv = st_t[sl_], mv_t[sl_]
            for hf in range(2):
                P.op(DVE, lambda hf=hf: nc.vector.bn_stats(out=st[:, hf, :], in_=x_t[:, i, hf * 512:(hf + 1) * 512]),
                     reads=[x_u[i]], writes=[st_u[sl_]])
            P.op(DVE, lambda: nc.vector.bn_aggr(out=mv[:], in_=st[:].rearrange("p a b -> p (a b)")),
                 reads=[st_u[sl_]], writes=[st_u[sl_]])

        def stB(j, i):
            sl_, xs_ = slot[j]
            P.op(ACT, lambda: nc.scalar.activation(out=lsm_t[sl_][:, 0:1], in_=mv_t[sl_][:, 1:2], func=AF.Sqrt, bias=LN_EPS_S, scale=1.0),
                 reads=[st_u[sl_]], writes=[st_u[sl_]])

        def stC(j, i):
            sl_, xs_ = slot[j]
            sm = lsm_t[sl_]
            P.op(DVE, lambda: nc.vector.reciprocal(out=sm[:, 1:2], in_=sm[:, 0:1]), reads=[st_u[sl_]], writes=[st_u[sl_]])
            P.op(DVE, lambda: nc.vector.tensor_scalar(out=xn_t[xs_][:], in0=x_t[:, i, :], scalar1=mv_t[sl_][:, 0:1], scalar2=sm[:, 1:2],
                                                      op0=ALU.subtract, op1=ALU.mult),
                 reads=[x_u[i], st_u[sl_]], writes=[xn_u[xs_]])

        def stE(j, i):
            sl_, xs_ = slot[j]
            if j % 4 == 3:
                P.op(DVE, lambda: nc.vector.tensor_tensor(out=x_t[:, i, :], in0=xn_t[xs_][:], in1=e_g[:], op=ALU.mult),
                     reads=[xn_u[xs_], e_u["g"]], writes=[x_u[i]])
                P.op(DVE, lambda: nc.vector.tensor_tensor(out=x_t[:, i, :], in0=x_t[:, i, :], in1=e_b[:], op=ALU.add),
                     reads=[x_u[i], e_u["b"]], writes=[x_u[i]])
            else:
                P.op(GQ, lambda: nc.gpsimd.tensor_tensor(out=x_t[:, i, :], in0=xn_t[xs_][:], in1=e_g[:], op=ALU.mult),
                     reads=[xn_u[xs_], e_u["g"]], writes=[x_u[i]])
                P.op(GQ, lambda: nc.gpsimd.tensor_tensor(out=x_t[:, i, :], in0=x_t[:, i, :], in1=e_b[:], op=ALU.add),
                     reads=[x_u[i], e_u["b"]], writes=[x_u[i]])
            if do_h:
                bks = bank_sets[j % len(bank_sets)]
                for c in range(8):
                    b = bks[c // 4]
                    P.op(PE, lambda c=c, b=b: nc.tensor.transpose(banks[b][:, (c % 4) * 128:(c % 4 + 1) * 128],
                                                                  xn_t[xs_][:, c * 128:(c + 1) * 128], ident_f[:]),
                         reads=[xn_u[xs_], const_u], writes=[bank_u[b]])

        def stF(j, i):
            if not do_h:
                return
            bks = bank_sets[j % len(bank_sets)]
            for c in range(8):
                b = bks[c // 4]
                src = banks[b][:, (c % 4) * 128:(c % 4 + 1) * 128]
                dst = hT[:, c, i * 128:(i + 1) * 128]
                if c % 2 == 0:
                    P.op(ACT, lambda: nc.scalar.activation(out=dst, in_=src, func=AF.Identity, bias=g2b2[:, gi, 8 + c:9 + c],
                                                           scale=g2b2[:, gi, c:c + 1]),
                         reads=[bank_u[b], g2b2_u[gi]], writes=[hT_u[i]])
                else:
                    P.op(DVE, lambda: nc.vector.tensor_scalar(out=dst, in0=src, scalar1=g2b2[:, gi, c:c + 1], scalar2=g2b2[:, gi, 8 + c:9 + c],
                                                              op0=ALU.mult, op1=ALU.add),
                         reads=[bank_u[b], g2b2_u[gi]], writes=[hT_u[i]])

        stages = [stA, stB, stC, stE, stF]
        for step in range(n + len(stages) - 1):
            for k in reversed(range(len(stages))):
                j = step - k
                if 0 <= j < n:
                    stages[k](j, tiles[j])

    def ffn_sublayer(l, j, nxt):
        jj = 0 if j == 0 else 1
        fz = P.fence()
        act_u = [[Unit(fz), Unit(fz)] for _ in range(NFC)]
        actT = aview(0, NFC * TOK, BF16).rearrange("p (f t) -> p f t", t=TOK)
        wd_t = [aview(45056 + 2048 * i, 1024, BF16).rearrange("p (a n) -> p a n", n=512) for i in range(4)]
        wd_u = [Unit(fz) for _ in range(4)]
        sg_t = [aview(45056 + 8192 + 2048 * i, 512, F32) for i in range(2)]
        sg_u = [Unit(fz) for _ in range(2)]
        wup = w_ffn_up[l, jj]
        wdn = w_ffn_down[l, jj]
        sgc = 0
        for fp in range(NFC // 2):
            if fp == 2:
                emit_gm_gb(l, j)
            if fp == 6 and nxt is not None:
                emit_st(nxt[0], nxt[1], (l, j))
            if ("wup", fp) in prefetched:
                wi, wv = prefetched.pop(("wup", fp))
            else:
                wi, wv = load_wup(l, jj, fp)
            for sub in range(2):
                fc = 2 * fp + sub
                for qb in range(2):
                    bg, bu = nextbank("A"), nextbank("B")
                    hr = [hT_u[4 * qb + t] for t in range(4)]
                    for k in range(8):
                        P.op(PE, lambda k=k: nc.tensor.matmul(banks[bg][:], lhsT=wv[:, k, sub * 128:(sub + 1) * 128],
                                                                rhs=hT[:, k, qb * 512:(qb + 1) * 512], start=(k == 0), stop=(k == 7)),
                             reads=[wbuf_u[wi]] + hr, writes=[bank_u[bg]])
                    for k in range(8):
                        P.op(PE, lambda k=k: nc.tensor.matmul(banks[bu][:], lhsT=wv[:, k, 256 + sub * 128:256 + (sub + 1) * 128],
                                                                rhs=hT[:, k, qb * 512:(qb + 1) * 512], start=(k == 0), stop=(k == 7)),
                             reads=[wbuf_u[wi]] + hr, writes=[bank_u[bu]])
                    si = sgc % 2
                    sgc += 1
                    P.op(ACT, lambda: nc.scalar.activation(out=sg_t[si], in_=banks[bg][:], func=AF.Silu),
                         reads=[bank_u[bg]], writes=[sg_u[si]])
                    P.op(DVE, lambda: nc.vector.tensor_tensor(out=actT[:, fc, qb * 512:(qb + 1) * 512], in0=banks[bu][:],
                                                              in1=sg_t[si], op=ALU.mult),
                         reads=[bank_u[bu], sg_u[si]], writes=[act_u[fc][qb]])
        do_prefetch(nxt)
        wdc = 0
        for dh in range(2):
            for f2 in range(NFC // 2):
                wi = wdc % 4
                wdc += 1
                P.dma(GQ, arena_dma_sems[wi], wd_t[wi],
                      wdn[f2 * 256:(f2 + 1) * 256, dh * 512:(dh + 1) * 512].rearrange("(a p) n -> p a n", p=128),
                      writes=[wd_u[wi]])
                for a in range(2):
                    P.op(DVE, lambda a=a: nc.vector.tensor_tensor(out=wd_t[wi][:, a, :], in0=wd_t[wi][:, a, :],
                                                                 in1=e_gm[:, dh * 512:(dh + 1) * 512], op=ALU.mult),
                         reads=[wd_u[wi], e_u["gm"]], writes=[wd_u[wi]])
                for a in range(2):
                    fc = 2 * f2 + a
                    for i in range(NT):
                        P.op(PE, lambda i=i: nc.tensor.matmul(banks[i][:], lhsT=actT[:, fc, i * 128:(i + 1) * 128],
                                                                rhs=wd_t[wi][:, a, :], start=(fc == 0), stop=(fc == NFC - 1)),
                             reads=[act_u[fc][i // 4], wd_u[wi]], writes=[bank_u[i]])
            for i in range(NT):
                residual_half(i, dh, i, 0.5 / ALPHA)
        epilogue_tiles(range(NT), nxt is not None, bank_sets=((0, 1), (2, 3), (4, 5), (6, 7)))

    def mixer_sublayer(l, grp, nxt):
        is_s = grp == "S"
        koff = 256 if is_s else 0
        nkeys = 1280 if is_s else 1024
        nkt = nkeys // 128
        fz = P.fence()
        o = 0
        mixT = aview(o, 8 * TOK, BF16).rearrange("p (e t) -> p e t", t=TOK); o += 16384
        cqT = aview(o, 3 * TOK, BF16).rearrange("p (e t) -> p e t", t=TOK); o += 6144
        ckvT = aview(o, 2 * 1280, BF16).rearrange("p (e t) -> p e t", t=1280); o += 5120
        krT = aview(o, 1280, BF16); o += 2560
        cqn = [aview(o + 768 * i, QL, BF16) for i in range(2)]; o += 1536
        kvo = [aview(o + 1152 * i, 288, F32) for i in range(2)]; o += 2304
        ckvb = [aview(o + 512 * i, KVL, BF16) for i in range(2)]; o += 1024
        ctxc = aview(o, 512, BF16).rearrange("p (t n) -> p t n", n=256); o += 1024
        ctxr = aview(o, 192, BF16).rearrange("p (t n) -> p t n", n=96); o += 384
        sqj = aview(o, QL, BF16); o += 768
        assert o <= 37248, o
        o = 37248
        U0 = o
        xpad = aview(o, 1056, F32); o += 4224
        xc = aview(o, TOK, F32); o += 4096
        xcb = aview(o, TOK, BF16); o += 2048
        gg = aview(o, TOK, F32); o += 4096
        ra = aview(o, TOK, F32); o += 4096
        iu = aview(o, TOK, F32); o += 4096
        ss = aview(o, TOK, F32); o += 4096
        hf_t = aview(o, TOK, F32); o += 4096
        hb2_t = aview(o, TOK, F32); o += 4096
        assert o <= ARENA_BYTES, o
        mix_u = [[Unit(fz) for _ in range(NT)] for _ in range(8)]
        cqT_u = [Unit(fz) for _ in range(NT)]
        ckvT_u = [Unit(fz) for _ in range(nkt)]
        krT_u = Unit(fz)
        m1_u = {k: [Unit(fz), Unit(fz)] for k in ("cqn", "kvo", "ckvb", "sm")}
        ctx_u = Unit(fz)
        sqj_u = Unit(fz)
        lru_u = {k: Unit(fz) for k in ("xpad", "xc", "xcb", "gg", "ra", "iu", "ss", "hf", "hb")}

        P.dma(SP, norm_sem, qg_bc[:], q_norm_g[l].partition_broadcast(128), writes=[norm_u])
        P.dma(SP, norm_sem, kg_bc[:], kv_norm_g[l].partition_broadcast(128), writes=[norm_u])
        for mat, wsrc in enumerate((lru_w_a, lru_w_x)):
            for d in range(2):
                for par in range(2):
                    src = wsrc[l, d].rearrange("(c q) a e -> q a c e", q=2)[par]
                    P.dma(GQ, bd_sem, bd_t[par * 64:(par + 1) * 64, mat, d, :, par * 64:(par + 1) * 64], src,
                          writes=[bd_u], nc_ok=True)

        if "wA" in prefetched:
            wA_i, wA = prefetched.pop("wA")
            wB_i, wB = prefetched.pop("wB")
        else:
            wA_i, wA = load_w_piece(w_in[l][:, 0:QL], QL)
            wB_i, wB = load_w_piece(w_in[l][:, OFF_KV:OFF_KV + 288], 288)
        if is_s:
            P.dma(GQ, arena_dma_sems[4], ctxc, cache_ckv[l].rearrange("(t p) c -> p t c", p=128), writes=[ctx_u])
            P.op(DVE, lambda: nc.vector.memset(ctxr.rearrange("p t n -> p (t n)"), 0.0), writes=[ctx_u])
            P.dma(GQ, arena_dma_sems[4], ctxr[:, :, 64:96], cache_krope[l].rearrange("(t p) c -> p t c", p=128),
                  writes=[ctx_u], nc_ok=True)

        m1b = {}

        def m1_s0(i):
            p = i % 3
            bq, bk = 2 * p, 2 * p + 1
            m1b[i] = (bq, bk)
            for k in range(8):
                P.op(PE, lambda k=k: nc.tensor.matmul(banks[bq][:, 0:QL], lhsT=hT[:, k, i * 128:(i + 1) * 128], rhs=wA[:, k, :],
                                                        start=(k == 0), stop=(k == 7)),
                     reads=[hT_u[i], wbuf_u[wA_i]], writes=[bank_u[bq]])
            for k in range(8):
                P.op(PE, lambda k=k: nc.tensor.matmul(banks[bk][:, 0:288], lhsT=hT[:, k, i * 128:(i + 1) * 128], rhs=wB[:, k, :],
                                                        start=(k == 0), stop=(k == 7)),
                     reads=[hT_u[i], wbuf_u[wB_i]], writes=[bank_u[bk]])

        def m1_s1(i):
            s = i % 2
            bq, bk = m1b[i]
            sm, smu = sm_t[s], m1_u["sm"][s]
            P.op(ACT, lambda: nc.scalar.activation(out=sqj, in_=banks[bq][:, 0:QL], func=AF.Square, accum_out=sm[:, 3:4]),
                 reads=[bank_u[bq]], writes=[sqj_u, smu])
            P.op(ACT, lambda: nc.scalar.activation(out=sqj[:, 0:KVL], in_=banks[bk][:, 0:KVL], func=AF.Square, accum_out=sm[:, 5:6]),
                 reads=[bank_u[bk]], writes=[sqj_u, smu])
            P.op(ACT, lambda: nc.scalar.activation(out=sm[:, 4:5], in_=sm[:, 3:4], func=AF.Sqrt, bias=RMS_EPS, scale=1.0 / QL),
                 reads=[smu], writes=[smu])
            P.op(ACT, lambda: nc.scalar.activation(out=sm[:, 6:7], in_=sm[:, 5:6], func=AF.Sqrt, bias=RMS_EPS, scale=1.0 / KVL),
                 reads=[smu], writes=[smu])

        def m1_s2(i):
            s = i % 2
            bq, bk = m1b[i]
            sm, smu = sm_t[s], m1_u["sm"][s]
            P.op(DVE, lambda: nc.vector.reciprocal(out=sm[:, 4:5], in_=sm[:, 4:5]), reads=[smu], writes=[smu])
            P.op(DVE, lambda: nc.vector.reciprocal(out=sm[:, 6:7], in_=sm[:, 6:7]), reads=[smu], writes=[smu])
            P.op(DVE, lambda: nc.vector.scalar_tensor_tensor(out=cqn[s], in0=banks[bq][:, 0:QL], scalar=sm[:, 4:5], in1=qg_bc[:],
                                                              op0=ALU.mult, op1=ALU.mult),
                 reads=[bank_u[bq], smu, norm_u], writes=[m1_u["cqn"][s]])
            P.op(DVE, lambda: nc.vector.scalar_tensor_tensor(out=kvo[s][:, 0:KVL], in0=banks[bk][:, 0:KVL], scalar=sm[:, 6:7],
                                                              in1=kg_bc[:], op0=ALU.mult, op1=ALU.mult),
                 reads=[bank_u[bk], smu, norm_u], writes=[m1_u["kvo"][s]])
            if not is_s:
                P.op(DVE, lambda: nc.vector.tensor_copy(out=kvo[s][:, KVL:288], in_=banks[bk][:, KVL:288]),
                     reads=[bank_u[bk]], writes=[m1_u["kvo"][s]])

        def m1_s3(i):
            s = i % 2
            if not is_s:
                seq, t0 = i // 2, (i % 2) * 128
                P.dma(SP, arena_dma_sems[5 + s], new_ckv[seq, l, t0:t0 + 128, :], kvo[s][:, 0:KVL], reads=[m1_u["kvo"][s]])
                P.dma(SP, arena_dma_sems[5 + s], new_krope[seq, l, t0:t0 + 128, :], kvo[s][:, KVL:288], reads=[m1_u["kvo"][s]])
            P.op(ACT, lambda: nc.scalar.copy(out=ckvb[s], in_=kvo[s][:, 0:KVL]), reads=[m1_u["kvo"][s]], writes=[m1_u["ckvb"][s]])
            pT = banks[6][:].bitcast(BF16)
            for c in range(3):
                P.op(PE, lambda c=c: nc.tensor.transpose(pT[:, c * 128:(c + 1) * 128], cqn[s][:, c * 128:(c + 1) * 128], ident_b[:]),
                     reads=[m1_u["cqn"][s], const_u], writes=[bank_u[6]])

        def m1_s4(i):
            s = i % 2
            pT6 = banks[6][:].bitcast(BF16)
            P.op(ACT, lambda: nc.scalar.copy(out=cqT[:, :, i * 128:(i + 1) * 128], in_=pT6[:, 0:384].rearrange("p (c n) -> p c n", n=128)),
                 reads=[bank_u[6]], writes=[cqT_u[i]])
            pT = banks[7][:].bitcast(BF16)
            for c in range(2):
                P.op(PE, lambda c=c: nc.tensor.transpose(pT[:, c * 128:(c + 1) * 128], ckvb[s][:, c * 128:(c + 1) * 128], ident_b[:]),
                     reads=[m1_u["ckvb"][s], const_u], writes=[bank_u[7]])

        def m1_s5(i):
            pT = banks[7][:].bitcast(BF16)
            kt = (koff // 128) + i
            P.op(ACT, lambda: nc.scalar.copy(out=ckvT[:, :, koff + i * 128:koff + (i + 1) * 128],
                                             in_=pT[:, 0:256].rearrange("p (c n) -> p c n", n=128)),
                 reads=[bank_u[7]], writes=[ckvT_u[kt]])

        m1_stages = [m1_s0, m1_s1, m1_s2, m1_s3, m1_s4, m1_s5]
        for step in range(NT + len(m1_stages) - 1):
            for k in reversed(range(len(m1_stages))):
                i = step - k
                if 0 <= i < NT:
                    m1_stages[k](i)

        if is_s:
            b = nextbank("C")
            pT = banks[b][:].bitcast(BF16)
            for c in range(2):
                for t in range(2):
                    P.op(PE, lambda c=c, t=t: nc.tensor.transpose(pT[:, (c * 2 + t) * 128:(c * 2 + t + 1) * 128],
                                                                  ctxc[:, t, c * 128:(c + 1) * 128], ident_b[:]),
                         reads=[ctx_u, const_u], writes=[bank_u[b]])
            P.op(ACT, lambda: nc.scalar.copy(out=ckvT[:, :, 0:256], in_=pT[:, 0:512].rearrange("p (c n) -> p c n", n=256)),
                 reads=[bank_u[b]], writes=[ckvT_u[0], ckvT_u[1]])
            b = nextbank("C")
            pT = banks[b][:].bitcast(BF16)
            for t in range(2):
                P.op(PE, lambda t=t: nc.tensor.transpose(pT[0:96, t * 128:(t + 1) * 128], ctxr[:, t, :], ident_b[:]),
                     reads=[ctx_u, const_u], writes=[bank_u[b]])
            P.op(ACT, lambda: nc.scalar.copy(out=krT[64:96, 0:256], in_=pT[64:96, 0:256]),
                 reads=[bank_u[b]], writes=[krT_u])

        if is_s:
            wS_i = next_wbuf()
            wS = wbuf[wS_i][:, 0:8 * 96].rearrange("p (k n) -> p k n", n=96)
            P.op(DVE, lambda: nc.vector.memset(wbuf[wS_i][:, 0:8 * 96], 0.0), writes=[wbuf_u[wS_i]])
            P.dma(GQ, wbuf_sem[wS_i], wS[:, :, 64:80], w_in[l][:, OFF_KR + 16:OFF_KR + 32].rearrange("(k p) n -> p k n", p=128),
                  writes=[wbuf_u[wS_i]], nc_ok=True)
            P.dma(GQ, wbuf_sem[wS_i], wS[:, :, 80:96], w_in[l][:, OFF_KR:OFF_KR + 16].rearrange("(k p) n -> p k n", p=128),
                  writes=[wbuf_u[wS_i]], nc_ok=True)
        for qb in range(2):
            hr = [hT_u[4 * qb + t] for t in range(4)]
            ba_ = nextbank("A")
            for k in range(8):
                P.op(PE, lambda k=k: nc.tensor.matmul(banks[ba_][0:96, :], lhsT=wB[:, k, 192:288], rhs=hT[:, k, qb * 512:(qb + 1) * 512],
                                                        start=(k == 0), stop=(k == 7)),
                     reads=[wbuf_u[wB_i]] + hr, writes=[bank_u[ba_]])
            if not is_s:
                P.op(ACT, lambda: nc.scalar.copy(out=krT[64:96, qb * 512:(qb + 1) * 512], in_=banks[ba_][64:96, :]),
                     reads=[bank_u[ba_]], writes=[krT_u])
            else:
                bb_ = nextbank("B")
                for k in range(8):
                    P.op(PE, lambda k=k: nc.tensor.matmul(banks[bb_][0:96, :], lhsT=wS[:, k, :], rhs=hT[:, k, qb * 512:(qb + 1) * 512],
                                                            start=(k == 0), stop=(k == 7)),
                         reads=[wbuf_u[wS_i]] + hr, writes=[bank_u[bb_]])
                t1 = ra[64:96, 0:512]
                t2 = iu[64:96, 0:512]
                P.op(DVE, lambda: nc.vector.tensor_tensor(out=t1, in0=banks[ba_][64:96, :], in1=rope_t[64:96, 0, qb * 512:(qb + 1) * 512],
                                                          op=ALU.mult), reads=[bank_u[ba_], const_u], writes=[lru_u["ra"]])
                P.op(DVE, lambda: nc.vector.tensor_tensor(out=t2, in0=banks[bb_][64:96, :], in1=rope_t[64:96, 1, qb * 512:(qb + 1) * 512],
                                                          op=ALU.mult), reads=[bank_u[bb_], const_u], writes=[lru_u["iu"]])
                P.op(DVE, lambda: nc.vector.tensor_tensor(out=krT[64:96, 256 + qb * 512:256 + (qb + 1) * 512], in0=t1, in1=t2, op=ALU.add),
                     reads=[lru_u["ra"], lru_u["iu"]], writes=[krT_u])

        def fetch_attn_weights():
            wq_i = next_wbuf()
            wq = wbuf[wq_i][:, 0:3 * 800].rearrange("p (k n) -> p k n", n=800)
            P.op(DVE, lambda: nc.vector.memset(wq[:, :, 768:800], 0.0), writes=[wbuf_u[wq_i]])
            P.dma(GQ, wbuf_sem[wq_i], wq[:, :, 0:768], w_uq[l].rearrange("(k p) n -> p k n", p=128), writes=[wbuf_u[wq_i]])
            wkv_i = next_wbuf()
            wkv = wbuf[wkv_i][:, 0:2048].rearrange("p (k n) -> p k n", n=1024)
            P.dma(GQ, wbuf_sem[wkv_i], wkv, w_ukv[l].rearrange("(k p) n -> p k n", p=128), writes=[wbuf_u[wkv_i]])
            if is_s:
                wqs_i = next_wbuf()
                wqs_f = wbuf[wqs_i][:, 0:3 * 800].rearrange("p (k n) -> p k n", n=800)
                wqs = wqs_f[:, :, 0:768].rearrange("p k (h e) -> p k h e", e=96)
                P.op(DVE, lambda: nc.vector.memset(wbuf[wqs_i][:, 0:3 * 800], 0.0), writes=[wbuf_u[wqs_i]])
                src4 = w_uq[l].rearrange("(k p) (h e) -> p k h e", p=128, e=96)
                for k in range(3):
                    P.dma(GQ, wbuf_sem[wqs_i], wqs[:, k, :, 64:80], src4[:, k, :, 80:96], writes=[wbuf_u[wqs_i]], nc_ok=True)
                    P.dma(GQ, wbuf_sem[wqs_i], wqs[:, k, :, 80:96], src4[:, k, :, 64:80], writes=[wbuf_u[wqs_i]], nc_ok=True)
            wb_reserved.update({wq_i, wkv_i} | ({wqs_i} if is_s else set()))
            return (wq_i, wq, wkv_i, wkv) + ((wqs_i, wqs_f, wqs) if is_s else (None, None, None))

        segs = [(0, TOK)] if is_s else [(s_ * 256, 256) for s_ in range(4)]
        nseg = len(segs)
        L = segs[0][1]
        LP = L + 3
        xpad_v = xpad[:, 0:nseg * LP].rearrange("p (s t) -> p s t", t=LP)
        P.op(DVE, lambda: nc.vector.memset(xpad, 0.0), writes=[lru_u["xpad"]])
        def load_lru_w(c):
            wi = next_wbuf()
            wv = wbuf[wi][:, 0:2048].rearrange("p (k n) -> p k n", n=256)
            P.dma(GQ, wbuf_sem[wi], wv[:, :, 0:128], w_in[l][:, OFF_UX + c * 128:OFF_UX + (c + 1) * 128].rearrange("(k p) n -> p k n", p=128),
                  writes=[wbuf_u[wi]])
            P.dma(GQ, wbuf_sem[wi], wv[:, :, 128:256], w_in[l][:, OFF_UG + c * 128:OFF_UG + (c + 1) * 128].rearrange("(k p) n -> p k n", p=128),
                  writes=[wbuf_u[wi]])
            return wi, wv

        lru_w0 = load_lru_w(0)
        wq_i, wq, wkv_i, wkv, wqs_i, wqs_f, wqs = fetch_attn_weights()
        for c in range(4):
            wi, wv = lru_w0 if c == 0 else load_lru_w(c)
            for qb in range(2):
                hr = [hT_u[4 * qb + t] for t in range(4)]
                bx_, bg_ = nextbank("A"), nextbank("B")
                for k in range(8):
                    P.op(PE, lambda k=k: nc.tensor.matmul(banks[bx_][:], lhsT=wv[:, k, 0:128], rhs=hT[:, k, qb * 512:(qb + 1) * 512],
                                                            start=(k == 0), stop=(k == 7)),
                         reads=[wbuf_u[wi]] + hr, writes=[bank_u[bx_]])
                for k in range(8):
                    P.op(PE, lambda k=k: nc.tensor.matmul(banks[bg_][:], lhsT=wv[:, k, 128:256], rhs=hT[:, k, qb * 512:(qb + 1) * 512],
                                                            start=(k == 0), stop=(k == 7)),
                         reads=[wbuf_u[wi]] + hr, writes=[bank_u[bg_]])
                if is_s:
                    P.op(ACT, lambda: nc.scalar.copy(out=xpad_v[:, 0, 2 + qb * 512:2 + (qb + 1) * 512], in_=banks[bx_][:]),
                         reads=[bank_u[bx_]], writes=[lru_u["xpad"]])
                else:
                    P.op(ACT, lambda: nc.scalar.copy(out=xpad_v[:, 2 * qb:2 * qb + 2, 2:2 + L],
                                                     in_=banks[bx_][:].rearrange("p (s t) -> p s t", t=L)),
                         reads=[bank_u[bx_]], writes=[lru_u["xpad"]])
                P.op(ACT, lambda: nc.scalar.activation(out=gg[:, qb * 512:(qb + 1) * 512], in_=banks[bg_][:], func=AF.Gelu_apprx_tanh),
                     reads=[bank_u[bg_]], writes=[lru_u["gg"]])
            xc_v = xc.rearrange("p (s t) -> p s t", t=L)
            P.op(DVE, lambda: nc.vector.tensor_scalar(out=xc_v, in0=xpad_v[:, :, 0:L], scalar1=cw_t[:, l, 0, c:c + 1],
                                                      scalar2=cb_t[:, l, c:c + 1], op0=ALU.mult, op1=ALU.add),
                 reads=[lru_u["xpad"], const_u], writes=[lru_u["xc"]])
            for kk in range(1, 4):
                P.op(DVE, lambda kk=kk: nc.vector.scalar_tensor_tensor(out=xc_v, in0=xpad_v[:, :, kk:kk + L], scalar=cw_t[:, l, kk, c:c + 1],
                                                                        in1=xc_v, op0=ALU.mult, op1=ALU.add),
                     reads=[lru_u["xpad"], lru_u["xc"], const_u], writes=[lru_u["xc"]])
            P.op(ACT, lambda: nc.scalar.copy(out=xcb, in_=xc), reads=[lru_u["xc"]], writes=[lru_u["xcb"]])
            for d in range(2):
                hdst, hdu = (hf_t, lru_u["hf"]) if d == 0 else (hb2_t, lru_u["hb"])
                for qb in range(2):
                    br_, bi_ = nextbank("A"), nextbank("B")
                    sl = slice(qb * 512, (qb + 1) * 512)
                    P.op(PE, lambda: nc.tensor.matmul(banks[br_][:], lhsT=bd_t[:, 0, d, c, :], rhs=xcb[:, sl], start=True, stop=True),
                         reads=[bd_u, lru_u["xcb"]], writes=[bank_u[br_]])
                    P.op(PE, lambda: nc.tensor.matmul(banks[bi_][:], lhsT=bd_t[:, 1, d, c, :], rhs=xcb[:, sl], start=True, stop=True),
                         reads=[bd_u, lru_u["xcb"]], writes=[bank_u[bi_]])
                    P.op(ACT, lambda: nc.scalar.activation(out=ra[:, sl], in_=banks[br_][:], func=AF.Sigmoid, bias=ba_t[:, l, d, c:c + 1], scale=1.0),
                         reads=[bank_u[br_], const_u], writes=[lru_u["ra"]])
                    P.op(ACT, lambda: nc.scalar.activation(out=iu[:, sl], in_=banks[bi_][:], func=AF.Sigmoid, bias=bx_t[:, l, d, c:c + 1], scale=1.0),
                         reads=[bank_u[bi_], const_u], writes=[lru_u["iu"]])
                P.op(ACT, lambda: nc.scalar.activation(out=ss, in_=ra, func=AF.Exp, scale=neg2c_t[:, l, d, c:c + 1]),
                     reads=[lru_u["ra"], lru_c_u, fso_u], writes=[lru_u["ss"]])
                P.op(ACT, lambda: nc.scalar.activation(out=ra, in_=ra, func=AF.Exp, scale=negc_t[:, l, d, c:c + 1]),
                     reads=[lru_u["ra"], lru_c_u], writes=[lru_u["ra"]])
                P.op(ACT, lambda: nc.scalar.activation(out=ss, in_=ss, func=AF.Sqrt, bias=1.0, scale=-1.0),
                     reads=[lru_u["ss"]], writes=[lru_u["ss"]])
                P.op(DVE, lambda: nc.vector.tensor_tensor(out=iu, in0=iu, in1=xc, op=ALU.mult),
                     reads=[lru_u["iu"], lru_u["xc"]], writes=[lru_u["iu"]])
                P.op(DVE, lambda: nc.vector.tensor_tensor(out=iu, in0=iu, in1=ss, op=ALU.mult),
                     reads=[lru_u["iu"], lru_u["ss"]], writes=[lru_u["iu"]])
                for (s0, ln) in segs:
                    if d == 0:
                        o_ap, a_ap, u_ap = hdst[:, s0:s0 + ln], ra[:, s0:s0 + ln], iu[:, s0:s0 + ln]
                    else:
                        o_ap, a_ap, u_ap = hdst[:, s0:s0 + ln][:, ::-1], ra[:, s0:s0 + ln][:, ::-1], iu[:, s0:s0 + ln][:, ::-1]
                    init = h0_t[:, l, d, c:c + 1] if is_s else 0.0
                    P.op(DVE, lambda o_ap=o_ap, a_ap=a_ap, u_ap=u_ap, init=init: nc.vector.tensor_tensor_scan(
                        out=o_ap, data0=a_ap, data1=u_ap, initial=init, op0=ALU.mult, op1=ALU.add),
                         reads=[lru_u["ra"], lru_u["iu"], const_u], writes=[hdu])
            if not is_s:
                hfv = hf_t.rearrange("p (s t) -> p s t", t=L)
                hbv = hb2_t.rearrange("p (s t) -> p s t", t=L)
                P.op(ACT, lambda: nc.scalar.copy(out=fs_t[:, :, l, 0, c:c + 1], in_=hfv[:, :, L - 1:L]),
                     reads=[lru_u["hf"]], writes=[fs_u])
                P.op(ACT, lambda: nc.scalar.copy(out=fs_t[:, :, l, 1, c:c + 1], in_=hbv[:, :, 0:1]),
                     reads=[lru_u["hb"]], writes=[fs_u])
            P.op(DVE, lambda: nc.vector.tensor_tensor(out=hf_t, in0=hf_t, in1=hb2_t, op=ALU.add),
                 reads=[lru_u["hf"], lru_u["hb"]], writes=[lru_u["hf"]])
            P.op(DVE, lambda: nc.vector.tensor_tensor(out=mixT[:, 4 + c, :], in0=hf_t, in1=gg, op=ALU.mult),
                 reads=[lru_u["hf"], lru_u["gg"]], writes=mix_u[4 + c])

        fz2 = P.fence()
        o = U0
        hd = []
        for s_ in range(2):
            qh = aview(o, TOK, BF16); o += 2048
            Kh = aview(o, 1280, BF16); o += 2560
            Vh = aview(o, 1280, BF16).rearrange("p (t n) -> p t n", n=128); o += 2560
            hd.append((qh, Kh, Vh, Unit(fz2), Unit(fz2), Unit(fz2)))
        pT_t = []
        for s_ in range(2):
            pT_t.append((aview(o, 1024, BF16), Unit(fz2))); o += 2048
        rc_t = []
        for s_ in range(2):
            rc_t.append((aview(o, 512, F32), Unit(fz2))); o += 2048
        rp_t = []
        for s_ in range(2):
            rp_t.append((aview(o, 512, F32), Unit(fz2))); o += 2048
        assert o <= ARENA_BYTES
        for s_ in range(2):
            P.op(DVE, lambda s_=s_: nc.vector.memset(hd[s_][2][:, :, 64:128], 1.0), writes=[hd[s_][5]])
        if is_s:
            asegs = [(0, 512, list(range(10))), (512, 512, list(range(10)))]
        else:
            asegs = [(s_ * 256, 256, [2 * s_, 2 * s_ + 1]) for s_ in range(4)]
        ptc = [0]
        rcc = [0]

        def prep_head(h):
            qh, Kh, Vh, qh_u, Kh_u, Vh_u = hd[h % 2]
            pieces = []

            def q_piece(qb):
                sl = slice(qb * 512, (qb + 1) * 512)
                cr = [cqT_u[4 * qb + t] for t in range(4)]
                ba_ = nextbank("A")
                for k in range(3):
                    P.op(PE, lambda k=k: nc.tensor.matmul(banks[ba_][:, :], lhsT=wq[:, k, h * 96:h * 96 + 128], rhs=cqT[:, k, sl],
                                                            start=(k == 0), stop=(k == 2)),
                         reads=[wbuf_u[wq_i]] + cr, writes=[bank_u[ba_]])
                if not is_s:
                    P.op(ACT, lambda: nc.scalar.copy(out=qh[0:96, sl], in_=banks[ba_][0:96, :]), reads=[bank_u[ba_]], writes=[qh_u])
                else:
                    bb_ = nextbank("B")
                    for k in range(3):
                        P.op(PE, lambda k=k: nc.tensor.matmul(banks[bb_][:, :], lhsT=wqs_f[:, k, h * 96:h * 96 + 128], rhs=cqT[:, k, sl],
                                                                start=(k == 0), stop=(k == 2)),
                             reads=[wbuf_u[wqs_i]] + cr, writes=[bank_u[bb_]])
                    P.op(ACT, lambda: nc.scalar.copy(out=qh[0:64, sl], in_=banks[ba_][0:64, :]), reads=[bank_u[ba_]], writes=[qh_u])
                    (t1, t1u), (t2, t2u) = rp_t[0], rp_t[1]
                    P.op(DVE, lambda: nc.vector.tensor_tensor(out=t1[64:96, :], in0=banks[ba_][64:96, :], in1=rope_t[64:96, 0, sl], op=ALU.mult),
                         reads=[bank_u[ba_], const_u], writes=[t1u])
                    P.op(DVE, lambda: nc.vector.tensor_tensor(out=t2[64:96, :], in0=banks[bb_][64:96, :], in1=rope_t[64:96, 1, sl], op=ALU.mult),
                         reads=[bank_u[bb_], const_u], writes=[t2u])
                    P.op(DVE, lambda: nc.vector.tensor_tensor(out=qh[64:96, sl], in0=t1[64:96, :], in1=t2[64:96, :], op=ALU.add),
                         reads=[t1u, t2u], writes=[qh_u])
            for qb in range(2):
                pieces.append(lambda qb=qb: q_piece(qb))

            def k_piece(k0):
                n = min(512, nkeys - k0)
                kr_ = [ckvT_u[t] for t in range(k0 // 128, (k0 + n) // 128)]
                ba_ = nextbank("A")
                for k in range(2):
                    P.op(PE, lambda k=k: nc.tensor.matmul(banks[ba_][:, 0:n], lhsT=wkv[:, k, h * 128:h * 128 + 128], rhs=ckvT[:, k, k0:k0 + n],
                                                            start=(k == 0), stop=(k == 1)),
                         reads=[wbuf_u[wkv_i]] + kr_, writes=[bank_u[ba_]])
                P.op(DVE, lambda: nc.vector.tensor_copy(out=Kh[0:64, k0:k0 + n], in_=banks[ba_][0:64, 0:n]), reads=[bank_u[ba_]], writes=[Kh_u])

            for k0 in range(0, nkeys, 512):
                pieces.append(lambda k0=k0: k_piece(k0))
            pieces.append(lambda: P.op(DVE, lambda: nc.vector.tensor_copy(out=Kh[64:96, 0:nkeys], in_=krT[64:96, 0:nkeys]),
                                       reads=[krT_u], writes=[Kh_u]))

            def v_piece(t0):
                nt_ = min(8, nkt - t0)
                bb_ = nextbank("B")
                pv = banks[bb_][:, 0:nt_ * 64].rearrange("p (t n) -> p t n", n=64)
                for t in range(nt_):
                    for k in range(2):
                        P.op(PE, lambda k=k, t=t: nc.tensor.matmul(pv[:, t, :], lhsT=ckvT[:, k, (t0 + t) * 128:(t0 + t + 1) * 128],
                                                                     rhs=wkv[:, k, h * 128 + 64:h * 128 + 128], start=(k == 0), stop=(k == 1)),
                             reads=[wbuf_u[wkv_i], ckvT_u[t0 + t]], writes=[bank_u[bb_]])
                P.op(DVE, lambda: nc.vector.tensor_copy(out=Vh[:, t0:t0 + nt_, 0:64], in_=pv), reads=[bank_u[bb_]], writes=[Vh_u])

            for t0 in range(0, nkt, 8):
                pieces.append(lambda t0=t0: v_piece(t0))
            return pieces

        sc_cnt = [0]
        acc_cnt = [0]
        pending_norm = []

        def attn_head(h, pend):
            qh, Kh, Vh, qh_u, Kh_u, Vh_u = hd[h % 2]
            for (q0, nq, kts) in asegs:
                bo_ = (2, 3)[acc_cnt[0] % 2] if is_s else (3, 6, 7)[acc_cnt[0] % 3]
                acc_cnt[0] += 1
                groups_ = [(kts[2 * p], kts[2 * p + 1]) for p in range(len(kts) // 2)]
                sb = {}

                def score(gi):
                    if is_s:
                        b0 = (4, 6)[sc_cnt[0] % 2]
                        sc_cnt[0] += 1
                        outs = [(banks[b0][:, 0:nq], bank_u[b0]), (banks[b0 + 1][:, 0:nq], bank_u[b0 + 1])]
                        us = [bank_u[b0], bank_u[b0 + 1]]
                    else:
                        b0 = nextbank("C")
                        outs = [(banks[b0][:, 0:nq], bank_u[b0]), (banks[b0][:, nq:2 * nq], bank_u[b0])]
                        us = [bank_u[b0]]
                    for j, kt in enumerate(groups_[gi]):
                        o_ap, o_u = outs[j]
                        P.op(PE, lambda: nc.tensor.matmul(o_ap, lhsT=Kh[0:96, kt * 128:(kt + 1) * 128], rhs=qh[0:96, q0:q0 + nq],
                                                           start=True, stop=True),
                             reads=[Kh_u, qh_u], writes=[o_u])
                    sb[gi] = (b0, us)

                score(0)
                for gi, grp_ in enumerate(groups_):
                    if gi + 1 < len(groups_):
                        score(gi + 1)
                    if pend:
                        pend.pop(0)()
                    if pend and not is_s:
                        pend.pop(0)()
                    b0, us = sb[gi]
                    pt, ptu = pT_t[ptc[0] % 2]
                    ptc[0] += 1
                    P.op(ACT, lambda: nc.scalar.activation(out=pt[:, 0:2 * nq], in_=bigps[:, b0 * 512:b0 * 512 + 2 * nq], func=AF.Exp,
                                                           scale=ATTN_SCALE),
                         reads=us, writes=[ptu])
                    for j, kt in enumerate(grp_):
                        P.op(PE, lambda: nc.tensor.matmul(banks[bo_][:, 0:nq], lhsT=Vh[:, kt, :], rhs=pt[:, j * nq:(j + 1) * nq],
                                                           start=(gi == 0 and j == 0), stop=(gi == len(groups_) - 1 and j == 1)),
                             reads=[Vh_u, ptu], writes=[bank_u[bo_]])
                    if gi == 0:
                        while pending_norm:
                            pending_norm.pop(0)()

                def norm(bo_=bo_, q0=q0, nq=nq, h=h):
                    rc, rcu = rc_t[rcc[0] % 2]
                    rcc[0] += 1
                    P.op(DVE, lambda: nc.vector.reciprocal(out=rc[64:128, 0:nq], in_=banks[bo_][64:128, 0:nq]), reads=[bank_u[bo_]], writes=[rcu])
                    pb = (h % 2) * 64
                    mu = [mix_u[h // 2][t] for t in range(q0 // 128, (q0 + nq) // 128)]
                    P.op(DVE, lambda: nc.vector.tensor_tensor(out=mixT[pb:pb + 64, h // 2, q0:q0 + nq], in0=banks[bo_][0:64, 0:nq],
                                                              in1=rc[64:128, 0:nq], op=ALU.mult),
                         reads=[bank_u[bo_], rcu], writes=mu)

                pending_norm.append(norm)

        rot_a_saved, rot_b_saved = rot["A"], rot["B"]
        if is_s:
            rot["A"], rot["B"] = [0], [1]
        else:
            rot["B"] = [2]
        for pc in prep_head(0):
            pc()
        for h in range(NH):
            pend = prep_head(h + 1) if h + 1 < NH else []
            attn_head(h, pend)
            while pend:
                pend.pop(0)()
        while pending_norm:
            pending_norm.pop(0)()
        rot["A"], rot["B"] = rot_a_saved, rot_b_saved
        wb_reserved.clear()

        emit_gm_gb(l, 1)
        if nxt is not None:
            emit_st(nxt[0], nxt[1], (l, 1))
        wo_v = []
        for dh in range(2):
            wo_v.append(load_w_piece(w_o[l][:, dh * 512:(dh + 1) * 512], 512))
        for dh in range(2):
            wi, wv = wo_v[dh]
            for e in range(8):
                P.op(DVE, lambda e=e: nc.vector.tensor_tensor(out=wv[:, e, :], in0=wv[:, e, :], in1=e_gm[:, dh * 512:(dh + 1) * 512], op=ALU.mult),
                     reads=[wbuf_u[wi], e_u["gm"]], writes=[wbuf_u[wi]])
        do_prefetch(nxt)

        def wo_tile(i):
            for dh in range(2):
                wi, wv = wo_v[dh]
                b = nextbank("A") if dh == 0 else nextbank("B")
                for e in range(8):
                    P.op(PE, lambda e=e: nc.tensor.matmul(banks[b][:], lhsT=mixT[:, e, i * 128:(i + 1) * 128], rhs=wv[:, e, :],
                                                            start=(e == 0), stop=(e == 7)),
                         reads=[mix_u[e][i], wbuf_u[wi]], writes=[bank_u[b]])
                residual_half(i, dh, b, 1.0 / ALPHA)

        epilogue_tiles(range(NT), nxt is not None, pre=wo_tile)

    lru_c_u = rope_u
    def load_consts():
        P.dma(SP, const_sem, cw_t[:], conv_w.rearrange("l k (c p) -> p l k c", p=128), writes=[const_u], nc_ok=True)
        P.dma(SP, const_sem, cb_t[:], conv_b.rearrange("l (c p) -> p l c", p=128), writes=[const_u], nc_ok=True)
        P.dma(SP, const_sem, ba_t[:], lru_b_a.rearrange("l d (c p) -> p l d c", p=128), writes=[const_u], nc_ok=True)
        P.dma(SP, const_sem, bx_t[:], lru_b_x.rearrange("l d (c p) -> p l d c", p=128), writes=[const_u], nc_ok=True)
        P.dma(SP, const_sem, lam_t[:], lru_lambda.rearrange("l d (c p) -> p l d c", p=128), writes=[const_u], nc_ok=True)
        P.dma(SP, const_sem, h0_t[:], state_lru.rearrange("l d (c p) -> p l d c", p=128), writes=[const_u], nc_ok=True)
        P.dma(SP, const_sem, rope_t[64:96, :, :], rope_cs.rearrange("a r t -> r a t"), writes=[const_u])
        P.dma(SP, const_sem, bmodT[:], b_mod.rearrange("l (v p) -> p l v", p=128), writes=[const_u], nc_ok=True)
        P.dma(SP, const_sem, lngT[:], ln_g.rearrange("l j (c p) -> p l j c", p=128), writes=[const_u], nc_ok=True)
        P.dma(SP, const_sem, lnbT[:], ln_b.rearrange("l j (c p) -> p l j c", p=128), writes=[const_u], nc_ok=True)
        lamf = lam_t[:].rearrange("p l d c -> p (l d c)")
        negcf = negc_t[:].rearrange("p l d c -> p (l d c)")
        neg2cf = neg2c_t[:].rearrange("p l d c -> p (l d c)")
        P.op(ACT, lambda: nc.scalar.activation(out=negcf, in_=lamf, func=AF.Exp, scale=-1.0), reads=[const_u], writes=[rope_u])
        P.op(ACT, lambda: nc.scalar.activation(out=negcf, in_=negcf, func=AF.Ln, bias=1.0, scale=1.0), reads=[rope_u], writes=[rope_u])
        P.op(ACT, lambda: nc.scalar.mul(out=neg2cf, in_=negcf, mul=-16.0), reads=[rope_u], writes=[fso_u])
        P.op(ACT, lambda: nc.scalar.mul(out=negcf, in_=negcf, mul=-8.0), reads=[rope_u, fso_u], writes=[rope_u])
        P.op(DVE, lambda: nc.vector.memset(bd_t[:].rearrange("p a b c n -> p (a b c n)"), 0.0), writes=[bd_u])
        P.op(DVE, lambda: nc.vector.memset(fs_t[:].rearrange("p a b c n -> p (a b c n)"), 0.0), writes=[fs_u])


    P.dma(SP, const_sem, ident_f[:], ident_in, writes=[const_u])
    const2_sem = P.new_sem("d_const2")
    P.dma(GQ, const2_sem, ident_b[:], ident_in, writes=[const_u])
    P.op(DVE, lambda: nc.vector.memset(ones_f[:], 1.0), writes=[screp_u])
    consts_loaded = [False]
    for grp in groups:
        is_s = grp == "S"
        for i in range(NT):
            src = x_sample[i * 128:(i + 1) * 128, :] if is_s else x_prompt[i // 2, (i % 2) * 128:(i % 2 + 1) * 128, :]
            P.dma(SP, x_sem[i], x_t[:, i, :], src, writes=[x_u[i]])
        csrc = c_in if is_s else c_ctx
        P.dma(SP, cond_sem, cond_f[:], csrc.rearrange("(k p) -> p k", p=128), writes=[cond_u], nc_ok=True)
        P.op(ACT, lambda: nc.scalar.activation(out=cond_s[:], in_=cond_f[:], func=AF.Silu), reads=[cond_u], writes=[cond_u])
        for k in range(8):
            P.op(DVE, lambda k=k: nc.vector.tensor_scalar(out=sc_rep[:, k, :], in0=ones_f[:], scalar1=cond_s[:, k:k + 1], scalar2=None,
                                                          op0=ALU.mult),
                 reads=[cond_u, screp_u], writes=[screp_u])
        emit_st_first = True
        if not consts_loaded[0]:
            P.dma(SP, const_sem, bmodT[:, 0, 0:16], b_mod[0, 0:2048].rearrange("(v p) -> p v", p=128), writes=[const_u], nc_ok=True)
        emit_st(0, 0, None)
        for i in range(NT):
            make_h(i, x_t[:, i, :], x_u[i])
        if not consts_loaded[0]:
            load_consts()
            consts_loaded[0] = True
        subs = [(l, j) for l in range(n_layers) for j in range(3)]
        for idx, (l, j) in enumerate(subs):
            nxt = subs[idx + 1] if idx + 1 < len(subs) else None
            if j == 1:
                mixer_sublayer(l, grp, nxt)
            else:
                ffn_sublayer(l, j, nxt)
            if debug and idx == 0:
                dbg_x = nc.dram_tensor("dbg_x", [128, NT, D], F32, kind="ExternalOutput").ap()
                dbg_h = nc.dram_tensor("dbg_h", [128, 8, TOK], BF16, kind="ExternalOutput").ap()
                dbg_e = nc.dram_tensor("dbg_e", [5, 128, D], F32, kind="ExternalOutput").ap()
                dbg_sems = [P.new_sem(f"d_dbg{q}") for q in range(7)]
                P.dma(SP, dbg_sems[5], dbg_x, x_t[:], reads=x_u)
                P.dma(SP, dbg_sems[6], dbg_h, hT[:], reads=hT_u)
                for q, (tl, k_) in enumerate(((e_gm, "gm"), (e_g, "g"), (e_b, "b"))):
                    P.dma(SP, dbg_sems[q], dbg_e[q], tl[:], reads=[e_u[k_]])
        for i in range(NT):
            dst = y_sample[i * 128:(i + 1) * 128, :] if is_s else y_prompt[i // 2, (i % 2) * 128:(i % 2 + 1) * 128, :]
            P.dma(SP, x_sem[i], dst, x_t[:, i, :], reads=[x_u[i]])
        if not is_s:
            b = nextbank("C")
            P.op(PE, lambda: nc.tensor.transpose(banks[b][:, 0:128], fs_t[:].rearrange("p a b c n -> p (a b c n)"), ident_f[:]),
                 reads=[fs_u, const_u], writes=[bank_u[b]])
            P.op(ACT, lambda: nc.scalar.copy(out=fs_o[:], in_=banks[b][:, 0:128]), reads=[bank_u[b]], writes=[fso_u])
            P.dma(SP, fso_sem, new_state, fs_o[:], reads=[fso_u])

    for s in P.sems:
        if s.name.startswith("d_") and s.cnt > 0:
            nc.sync.wait_ge(s.h, s.cnt)
    return P


_CACHE = {}


def _rope_tables():
    n_freq = ROPE // 4
    inv = (10000.0 ** (-np.arange(n_freq, dtype=np.float32) / n_freq)).astype(np.float32)
    t = np.arange(TOK)
    row = (t // 64).astype(np.float32)
    col = (t % 64).astype(np.float32)
    ang = np.concatenate([row[:, None] * inv, col[:, None] * inv], axis=-1).astype(np.float32)
    cos = np.cos(ang).astype(np.float32).T
    sin = np.sin(ang).astype(np.float32).T
    C = np.concatenate([cos, cos], axis=0)
    S = np.concatenate([-sin, sin], axis=0)
    return np.ascontiguousarray(np.stack([C, S], axis=0)).astype(np.float32)


def kernel(**inputs):
    if "prog" not in _CACHE:
        _CACHE["prog"] = build_program()
    P = _CACHE["prog"]
    f = lambda a: np.ascontiguousarray(np.asarray(a, dtype=np.float32))
    shared = {k: f(inputs[k]) for k in (
        "c_ctx", "w_mod", "b_mod", "ln_g", "ln_b", "w_ffn_up", "w_ffn_down", "w_in", "q_norm_g", "kv_norm_g",
        "w_uq", "w_ukv", "conv_w", "conv_b", "lru_w_a", "lru_b_a", "lru_w_x", "lru_b_x", "lru_lambda", "w_o")}
    shared["ident"] = np.eye(128, dtype=np.float32)
    shared["rope_cs"] = _rope_tables()
    xp = f(inputs["x_prompt"]); xs = f(inputs["x_sample"])
    ckv = f(inputs["cache_ckv"]); ckr = f(inputs["cache_krope"]); stl = f(inputs["state_lru"]); cc = f(inputs["c"])
    in_maps = []
    for core in range(8):
        b = core % 4
        m = dict(shared)
        m["x_prompt"] = np.ascontiguousarray(xp[4 * core:4 * core + 4])
        m["x_sample"] = np.ascontiguousarray(xs[b])
        m["cache_ckv"] = np.ascontiguousarray(ckv[b])
        m["cache_krope"] = np.ascontiguousarray(ckr[b])
        m["state_lru"] = np.ascontiguousarray(stl[b])
        m["c"] = np.ascontiguousarray(cc[b])
        in_maps.append(m)
    res = run_bass_kernel_spmd(P.nc, in_maps, core_ids=list(range(8)))
    r = res.results
    y_prompt = np.concatenate([r[c]["y_prompt"] for c in range(8)], axis=0)
    y_sample = np.stack([r[c]["y_sample"] for c in range(4)], axis=0)
    new_ckv = np.concatenate([r[c]["new_ckv"] for c in range(8)], axis=0)
    new_krope = np.concatenate([r[c]["new_krope"] for c in range(8)], axis=0)
    new_state = np.concatenate([r[c]["new_state"].reshape(4, DEPTH, 2, LRUW) for c in range(8)], axis=0)
    return (y_prompt.astype(np.float32), y_sample.astype(np.float32), new_ckv.astype(np.float32),
            new_krope.astype(np.float32), new_state.astype(np.float32))
```

```python
import math
from contextlib import ExitStack

import numpy as np
import concourse.bass as bass
import concourse.mybir as mybir
from concourse.bass_utils import run_bass_kernel_spmd

F32 = mybir.dt.float32
BF16 = mybir.dt.bfloat16
AF = mybir.ActivationFunctionType
ALU = mybir.AluOpType

D = 1024
DEPTH = 4
DFF = 2816
NFC = DFF // 128
TOK = 1024
NT = 8
QL, KVL, ROPE = 384, 256, 32
LRUW = 512
INW = QL + KVL + ROPE + 2 * LRUW
OFF_KV = QL
OFF_KR = QL + KVL
OFF_UX = QL + KVL + ROPE
OFF_UG = OFF_UX + LRUW
NH = 8
ALPHA = (2.0 * DEPTH) ** 0.25
LN_EPS_S = 1e-5 / (ALPHA * ALPHA)
RMS_EPS = 1e-6
ATTN_SCALE = 1.0 / math.sqrt(96.0)
N_LAYERS = DEPTH
GROUPS = ("P", "S")

ARENA_BYTES = 72704


class Sem:
    def __init__(self, handle, name):
        self.h = handle
        self.name = name
        self.cnt = 0


class Unit:
    __slots__ = ("w", "r")

    def __init__(self, fence=None):
        self.w = None
        self.r = dict(fence) if fence else {}


class Eng:
    def __init__(self, h, sem):
        self.h = h
        self.sem = sem
        self.known = {}
        self.relaxed = False


class Prog:
    def __init__(self):
        self.nc = bass.Bass("TRN2", target_bir_lowering=False)
        self.es = ExitStack()
        self.sems = []
        self.n_inst = 0

    def new_sem(self, name):
        s = Sem(self.es.enter_context(self.nc.semaphore(name)), name)
        self.sems.append(s)
        return s

    def fence(self):
        return {s: s.cnt for s in self.sems if s.cnt > 0}

    def sbuf(self, name, shape, dt):
        return self.es.enter_context(self.nc.sbuf_tensor(name, shape, dt))

    def psum(self, name, shape, dt):
        return self.es.enter_context(self.nc.psum_tensor(name, shape, dt))

    def _waits(self, eng, reads, writes, is_dma):
        need = {}

        def add(s, v):
            if need.get(s, 0) < v:
                need[s] = v

        for u in reads:
            if u.w is not None:
                add(*u.w)
        relaxed = eng.relaxed and not is_dma
        for u in writes:
            if u.w is not None and not (relaxed and u.w[0] is eng.sem):
                add(*u.w)
            for s, v in u.r.items():
                if not (relaxed and s is eng.sem):
                    add(s, v)
        for s, v in need.items():
            if eng.known.get(s, 0) < v:
                eng.h.wait_ge(s.h, v)
                eng.known[s] = v
                self.n_inst += 1

    def op(self, eng, fn, reads=(), writes=()):
        self._waits(eng, reads, writes, False)
        inst = fn()
        eng.sem.cnt += 1
        inst.then_inc(eng.sem.h, 1)
        mark = (eng.sem, eng.sem.cnt)
        self.n_inst += 1
        for u in reads:
            u.r[mark[0]] = mark[1]
        for u in writes:
            u.w = mark
            u.r = {}

    def dma(self, eng, sem, out, in_, reads=(), writes=(), nc_ok=False):
        self._waits(eng, reads, writes, True)
        if nc_ok:
            with self.nc.allow_non_contiguous_dma(reason="small strided parameter load"):
                inst = eng.h.dma_start(out=out, in_=in_)
        else:
            inst = eng.h.dma_start(out=out, in_=in_)
        sem.cnt += 16
        inst.then_inc(sem.h, 16)
        mark = (sem, sem.cnt)
        self.n_inst += 1
        for u in reads:
            u.r[mark[0]] = mark[1]
        for u in writes:
            u.w = mark
            u.r = {}


def build_program(n_layers=N_LAYERS, groups=GROUPS, debug=False):
    P = Prog()
    nc = P.nc

    def din(name, shape):
        return nc.dram_tensor(name, list(shape), F32, kind="ExternalInput").ap()

    def dout(name, shape):
        return nc.dram_tensor(name, list(shape), F32, kind="ExternalOutput").ap()

    x_prompt = din("x_prompt", [4, 256, D])
    x_sample = din("x_sample", [TOK, D])
    cache_ckv = din("cache_ckv", [DEPTH, 256, KVL])
    cache_krope = din("cache_krope", [DEPTH, 256, ROPE])
    state_lru = din("state_lru", [DEPTH, 2, LRUW])
    c_in = din("c", [D])
    c_ctx = din("c_ctx", [D])
    w_mod = din("w_mod", [DEPTH, D, 9 * D])
    b_mod = din("b_mod", [DEPTH, 9 * D])
    ln_g = din("ln_g", [DEPTH, 3, D])
    ln_b = din("ln_b", [DEPTH, 3, D])
    w_ffn_up = din("w_ffn_up", [DEPTH, 2, D, 2 * DFF])
    w_ffn_down = din("w_ffn_down", [DEPTH, 2, DFF, D])
    w_in = din("w_in", [DEPTH, D, INW])
    q_norm_g = din("q_norm_g", [DEPTH, QL])
    kv_norm_g = din("kv_norm_g", [DEPTH, KVL])
    w_uq = din("w_uq", [DEPTH, QL, NH * 96])
    w_ukv = din("w_ukv", [DEPTH, KVL, NH * 128])
    conv_w = din("conv_w", [DEPTH, 4, LRUW])
    conv_b = din("conv_b", [DEPTH, LRUW])
    lru_w_a = din("lru_w_a", [DEPTH, 2, 8, 64, 64])
    lru_b_a = din("lru_b_a", [DEPTH, 2, LRUW])
    lru_w_x = din("lru_w_x", [DEPTH, 2, 8, 64, 64])
    lru_b_x = din("lru_b_x", [DEPTH, 2, LRUW])
    lru_lambda = din("lru_lambda", [DEPTH, 2, LRUW])
    w_o = din("w_o", [DEPTH, D, D])
    ident_in = din("ident", [128, 128])
    rope_cs = din("rope_cs", [2, 32, TOK])

    y_prompt = dout("y_prompt", [4, 256, D])
    y_sample = dout("y_sample", [TOK, D])
    new_ckv = dout("new_ckv", [4, DEPTH, 256, KVL])
    new_krope = dout("new_krope", [4, DEPTH, 256, ROPE])
    new_state = dout("new_state", [128, 128])

    PE = Eng(nc.tensor, P.new_sem("s_pe"))
    PE.relaxed = True
    ACT = Eng(nc.scalar, P.new_sem("s_act"))
    DVE = Eng(nc.vector, P.new_sem("s_dve"))
    GQ = Eng(nc.gpsimd, P.new_sem("s_pool"))
    SP = Eng(nc.sync, P.new_sem("s_sp"))

    x_t = P.sbuf("x_t", [128, NT, D], F32)
    hT = P.sbuf("hT", [128, 8, TOK], BF16)
    arena = P.sbuf("arena", [128, ARENA_BYTES // 2], BF16)
    NWB = 4
    wbuf = [P.sbuf(f"wbuf{i}", [128, 4096], BF16) for i in range(NWB)]
    e_gm = P.sbuf("e_gm", [128, D], F32)
    e_g = P.sbuf("e_g", [128, D], F32)
    e_b = P.sbuf("e_b", [128, D], F32)
    NXN = 4
    xn_t = [P.sbuf(f"xn{i}", [128, D], F32) for i in range(NXN)]
    bmodT = P.sbuf("bmodT", [128, DEPTH, 72], F32)
    lngT = P.sbuf("lngT", [128, DEPTH, 3, 8], F32)
    lnbT = P.sbuf("lnbT", [128, DEPTH, 3, 8], F32)
    g2b2 = P.sbuf("g2b2", [128, 2, 16], F32)
    mT_t = P.sbuf("mT_t", [128, 24], F32)
    rope_t = P.sbuf("rope_t", [128, 2, TOK], F32)
    ident_b = P.sbuf("ident_b", [128, 128], BF16)
    ident_f = P.sbuf("ident_f", [128, 128], F32)
    ones_f = P.sbuf("ones_f", [128, 128], F32)
    sc_rep = P.sbuf("sc_rep", [128, 8, 128], BF16)
    cond_f = P.sbuf("cond_f", [128, 8], F32)
    cond_s = P.sbuf("cond_s", [128, 8], F32)
    qg_bc = P.sbuf("qg_bc", [128, QL], F32)
    kg_bc = P.sbuf("kg_bc", [128, KVL], F32)
    cw_t = P.sbuf("cw_t", [128, DEPTH, 4, 4], F32)
    cb_t = P.sbuf("cb_t", [128, DEPTH, 4], F32)
    ba_t = P.sbuf("ba_t", [128, DEPTH, 2, 4], F32)
    bx_t = P.sbuf("bx_t", [128, DEPTH, 2, 4], F32)
    lam_t = P.sbuf("lam_t", [128, DEPTH, 2, 4], F32)
    negc_t = P.sbuf("negc_t", [128, DEPTH, 2, 4], F32)
    neg2c_t = P.sbuf("neg2c_t", [128, DEPTH, 2, 4], F32)
    h0_t = P.sbuf("h0_t", [128, DEPTH, 2, 4], F32)
    fs_t = P.sbuf("fs_t", [128, 4, DEPTH, 2, 4], F32)
    fs_o = P.sbuf("fs_o", [128, 128], F32)
    bd_t = P.sbuf("bd_t", [128, 2, 2, 4, 128], BF16)
    NST = 6
    st_t = [P.sbuf(f"st{i}", [128, 2, 6], F32) for i in range(NST)]
    mv_t = [P.sbuf(f"mv{i}", [128, 2], F32) for i in range(NST)]
    lsm_t = [P.sbuf(f"lsm{i}", [128, 4], F32) for i in range(NST)]
    sm_t = [P.sbuf(f"sm{i}", [128, 8], F32) for i in range(2)]

    bigps = P.psum("bigps", [128, 8 * 512], F32)
    banks = [bigps[:, i * 512:(i + 1) * 512] for i in range(8)]
    bank_u = [Unit() for _ in range(8)]
    rot = {"A": [0, 1], "B": [2, 3], "C": [4, 5], "D": [6, 7]}
    rot_i = {k: 0 for k in rot}

    def nextbank(cls):
        b = rot[cls][rot_i[cls] % len(rot[cls])]
        rot_i[cls] += 1
        return b

    x_u = [Unit() for _ in range(NT)]
    hT_u = [Unit() for _ in range(NT)]
    wbuf_u = [Unit() for _ in range(NWB)]
    wbuf_sem = [P.new_sem(f"d_wb{i}") for i in range(NWB)]
    wb_i = [0]
    e_u = {k: Unit() for k in ("gm", "g", "b")}
    g2b2_u = [Unit(), Unit()]
    mT_u = Unit()
    g2_i = [0]
    e_sem = {k: P.new_sem(f"d_e_{k}") for k in ("g", "b", "bm")}
    xn_u = [Unit() for _ in range(NXN)]
    st_u = [Unit() for _ in range(NST)]
    x_sem = [P.new_sem(f"d_x{i}") for i in range(NT)]
    const_u = Unit()
    const_sem = P.new_sem("d_const")
    cond_u = Unit()
    cond_sem = P.new_sem("d_cond")
    screp_u = Unit()
    norm_u = Unit()
    norm_sem = P.new_sem("d_norm")
    bd_u = Unit()
    bd_sem = P.new_sem("d_bd")
    rope_u = Unit()
    fs_u = Unit()
    fso_u = Unit()
    fso_sem = P.new_sem("d_fso")
    arena_dma_sems = [P.new_sem(f"d_ar{i}") for i in range(8)]

    def aview(off, n_elem, dt):
        if dt == BF16:
            return arena[:, off // 2: off // 2 + n_elem]
        return arena[:, off // 2: off // 2 + 2 * n_elem].bitcast(F32)

    wb_reserved = set()

    def next_wbuf():
        while True:
            i = wb_i[0] % NWB
            wb_i[0] += 1
            if i not in wb_reserved:
                return i

    prefetched = {}

    def load_wup(l, jj, fp):
        wup = w_ffn_up[l, jj]
        wi = next_wbuf()
        wv = wbuf[wi][:, 0:4096].rearrange("p (k n) -> p k n", n=512)
        P.dma(GQ, wbuf_sem[wi], wv[:, :, 0:256], wup[:, fp * 256:(fp + 1) * 256].rearrange("(k p) n -> p k n", p=128),
              writes=[wbuf_u[wi]])
        P.dma(GQ, wbuf_sem[wi], wv[:, :, 256:512],
              wup[:, DFF + fp * 256:DFF + (fp + 1) * 256].rearrange("(k p) n -> p k n", p=128), writes=[wbuf_u[wi]])
        return wi, wv

    def do_prefetch(nxt):
        if nxt is None:
            return
        l2, j2 = nxt
        if j2 == 1:
            prefetched["wA"] = load_w_piece(w_in[l2][:, 0:QL], QL)
            prefetched["wB"] = load_w_piece(w_in[l2][:, OFF_KV:OFF_KV + 288], 288)
        else:
            for fp in (0, 1):
                prefetched[("wup", fp)] = load_wup(l2, 0 if j2 == 0 else 1, fp)

    def load_w_piece(src_ap, ncols, extra=None):
        i = next_wbuf()
        dst = wbuf[i][:, 0:8 * ncols].rearrange("p (k n) -> p k n", n=ncols)
        P.dma(GQ, wbuf_sem[i], dst, src_ap.rearrange("(k p) n -> p k n", p=128), writes=[wbuf_u[i]])
        return i, dst

    def mod_vector(l, v, dest, dest_u, kind):
        P.dma(SP, e_sem["bm"], dest[:], b_mod[l, v * D:(v + 1) * D].partition_broadcast(128), writes=[dest_u])
        for half in range(2):
            c0 = v * D + half * 512
            wi, wv = load_w_piece(w_mod[l][:, c0:c0 + 512], 512)
            b = nextbank("D")
            for k in range(8):
                P.op(PE, lambda k=k: nc.tensor.matmul(banks[b][:], lhsT=sc_rep[:, k, :], rhs=wv[:, k, :],
                                                        start=(k == 0), stop=(k == 7)),
                     reads=[screp_u, wbuf_u[wi]], writes=[bank_u[b]])
            dsl = dest[:, half * 512:(half + 1) * 512]
            P.op(DVE, lambda: nc.vector.tensor_tensor(out=dsl, in0=banks[b][:], in1=dsl, op=ALU.add),
                 reads=[bank_u[b], dest_u], writes=[dest_u])

    def emit_st(l, j, prev):
        b = nextbank("D")
        ps = banks[b][:, 0:32]
        for vi in range(2):
            v = 3 * j + vi
            for half in range(2):
                c0 = v * D + half * 512
                wi, wv = load_w_piece(w_mod[l][:, c0:c0 + 512], 512)
                for cc in range(4):
                    col = 2 * (vi * 8 + half * 4 + cc)
                    for k in range(8):
                        P.op(PE, lambda k=k: nc.tensor.matmul(ps[:, col:col + 2], lhsT=wv[:, k, cc * 128:(cc + 1) * 128],
                                                                rhs=sc_rep[:, k, 0:2], start=(k == 0), stop=(k == 7)),
                             reads=[screp_u, wbuf_u[wi]], writes=[bank_u[b]])
        psv = ps.rearrange("p (c t) -> p c t", t=2)[:, :, 0]
        P.op(DVE, lambda: nc.vector.tensor_tensor(out=mT_t[:, 0:16], in0=psv, in1=bmodT[:, l, 3 * j * 8:3 * j * 8 + 16], op=ALU.add),
             reads=[bank_u[b], const_u], writes=[mT_u])
        gi = g2_i[0] % 2
        g2_i[0] += 1
        G2 = g2b2[:, gi, 0:8]
        B2 = g2b2[:, gi, 8:16]
        if prev is None:
            P.op(DVE, lambda: nc.vector.tensor_scalar(out=G2, in0=mT_t[:, 8:16], scalar1=1.0, scalar2=None, op0=ALU.add),
                 reads=[mT_u], writes=[g2b2_u[gi]])
            P.op(DVE, lambda: nc.vector.tensor_copy(out=B2, in_=mT_t[:, 0:8]), reads=[mT_u], writes=[g2b2_u[gi]])
        else:
            lp, jp = prev
            P.op(DVE, lambda: nc.vector.scalar_tensor_tensor(out=G2, in0=mT_t[:, 8:16], scalar=1.0, in1=lngT[:, lp, jp, :],
                                                              op0=ALU.add, op1=ALU.mult),
                 reads=[mT_u, const_u], writes=[g2b2_u[gi]])
            P.op(DVE, lambda: nc.vector.scalar_tensor_tensor(out=mT_t[:, 16:24], in0=mT_t[:, 8:16], scalar=1.0, in1=lnbT[:, lp, jp, :],
                                                              op0=ALU.add, op1=ALU.mult),
                 reads=[mT_u, const_u], writes=[mT_u])
            P.op(DVE, lambda: nc.vector.tensor_tensor(out=B2, in0=mT_t[:, 16:24], in1=mT_t[:, 0:8], op=ALU.add),
                 reads=[mT_u], writes=[g2b2_u[gi]])

    def emit_gm_gb(l, j):
        mod_vector(l, 3 * j + 2, e_gm, e_u["gm"], "T")
        P.dma(SP, e_sem["g"], e_g[:], ln_g[l, j].partition_broadcast(128), writes=[e_u["g"]])
        P.dma(SP, e_sem["b"], e_b[:], ln_b[l, j].partition_broadcast(128), writes=[e_u["b"]])

    tp_cnt = [0]

    def make_h(i, src, src_u, hi_banks=False):
        gi = (g2_i[0] - 1) % 2
        par = tp_cnt[0] % 2
        tp_cnt[0] += 1
        bks = (6, 7) if (hi_banks or par == 1) else (4, 5)
        for c in range(8):
            b = bks[c // 4]
            P.op(PE, lambda c=c, b=b: nc.tensor.transpose(banks[b][:, (c % 4) * 128:(c % 4 + 1) * 128], src[:, c * 128:(c + 1) * 128], ident_f[:]),
                 reads=[src_u, const_u], writes=[bank_u[b]])
        for c in range(8):
            b = bks[c // 4]
            if c % 2 == 0:
                P.op(ACT, lambda c=c, b=b: nc.scalar.activation(out=hT[:, c, i * 128:(i + 1) * 128], in_=banks[b][:, (c % 4) * 128:(c % 4 + 1) * 128],
                                                               func=AF.Identity, bias=g2b2[:, gi, 8 + c:9 + c], scale=g2b2[:, gi, c:c + 1]),
                     reads=[bank_u[b], g2b2_u[gi]], writes=[hT_u[i]])
            else:
                P.op(DVE, lambda c=c, b=b: nc.vector.tensor_scalar(out=hT[:, c, i * 128:(i + 1) * 128], in0=banks[b][:, (c % 4) * 128:(c % 4 + 1) * 128],
                                                                  scalar1=g2b2[:, gi, c:c + 1], scalar2=g2b2[:, gi, 8 + c:9 + c],
                                                                  op0=ALU.mult, op1=ALU.add),
                     reads=[bank_u[b], g2b2_u[gi]], writes=[hT_u[i]])

    def residual_half(i, dh, b, kfac):
        xs = x_t[:, i, dh * 512:(dh + 1) * 512]
        P.op(DVE, lambda: nc.vector.scalar_tensor_tensor(out=xs, in0=banks[b][:], scalar=kfac, in1=xs, op0=ALU.mult, op1=ALU.add),
             reads=[bank_u[b], x_u[i]], writes=[x_u[i]])

    ln_cnt = [0]

    def epilogue_tiles(tiles, do_h, pre=None, bank_sets=((4, 5), (6, 7))):
        tiles = list(tiles)
        n = len(tiles)
        slot = {}
        gi = (g2_i[0] - 1) % 2

        def stA(j, i):
            sl_ = ln_cnt[0] % NST
            xs_ = ln_cnt[0] % NXN
            ln_cnt[0] += 1
            slot[j] = (sl_, xs_)
            if pre is not None:
                pre(i)
            st, mv = st_t[sl_], mv_t[sl_]
            for hf in range(2):
                P.op(DVE, lambda hf=hf: nc.vector.bn_stats(out=st[:, hf, :], in_=x_t[:, i, hf * 512:(hf + 1) * 512]),
                     reads=[x_u[i]], writes=[st_u[sl_]])
            P.op(DVE, lambda: nc.vector.bn_aggr(out=mv[:], in_=st[:].rearrange("p a b -> p (a b)")),
                 reads=[st_u[sl_]], writes=[st_u[sl_]])

        def stB(j, i):
            sl_, xs_ = slot[j]
            P.op(ACT, lambda: nc.scalar.activation(out=lsm_t[sl_][:, 0:1], in_=mv_t[sl_][:, 1:2], func=AF.Sqrt, bias=LN_EPS_S, scale=1.0),
                 reads=[st_u[sl_]], writes=[st_u[sl_]])

        def stC(j, i):
            sl_, xs_ = slot[j]
            sm = lsm_t[sl_]
            P.op(DVE, lambda: nc.vector.reciprocal(out=sm[:, 1:2], in_=sm[:, 0:1]), reads=[st_u[sl_]], writes=[st_u[sl_]])
            P.op(DVE, lambda: nc.vector.tensor_scalar(out=xn_t[xs_][:], in0=x_t[:, i, :], scalar1=mv_t[sl_][:, 0:1], scalar2=sm[:, 1:2],
                                                      op0=ALU.subtract, op1=ALU.mult),
                 reads=[x_u[i], st_u[sl_]], writes=[xn_u[xs_]])

        def stE(j, i):
            sl_, xs_ = slot[j]
            if j % 4 == 3:
                P.op(DVE, lambda: nc.vector.tensor_tensor(out=x_t[:, i, :], in0=xn_t[xs_][:], in1=e_g[:], op=ALU.mult),
                     reads=[xn_u[xs_], e_u["g"]], writes=[x_u[i]])
                P.op(DVE, lambda: nc.vector.tensor_tensor(out=x_t[:, i, :], in0=x_t[:, i, :], in1=e_b[:], op=ALU.add),
                     reads=[x_u[i], e_u["b"]], writes=[x_u[i]])
            else:
                P.op(GQ, lambda: nc.gpsimd.tensor_tensor(out=x_t[:, i, :], in0=xn_t[xs_][:], in1=e_g[:], op=ALU.mult),
                     reads=[xn_u[xs_], e_u["g"]], writes=[x_u[i]])
                P.op(GQ, lambda: nc.gpsimd.tensor_tensor(out=x_t[:, i, :], in0=x_t[:, i, :], in1=e_b[:], op=ALU.add),
                     reads=[x_u[i], e_u["b"]], writes=[x_u[i]])
            if do_h:
                bks = bank_sets[j % len(bank_sets)]
                for c in range(8):
                    b = bks[c // 4]
                    P.op(PE, lambda c=c, b=b: nc.tensor.transpose(banks[b][:, (c % 4) * 128:(c % 4 + 1) * 128],
                                                                  xn_t[xs_][:, c * 128:(c + 1) * 128], ident_f[:]),
                         reads=[xn_u[xs_], const_u], writes=[bank_u[b]])

        def stF(j, i):
            if not do_h:
                return
            bks = bank_sets[j % len(bank_sets)]
            for c in range(8):
                b = bks[c // 4]
                src = banks[b][:, (c % 4) * 128:(c % 4 + 1) * 128]
                dst = hT[:, c, i * 128:(i + 1) * 128]
                if c % 4 != 3:
                    P.op(ACT, lambda: nc.scalar.activation(out=dst, in_=src, func=AF.Identity, bias=g2b2[:, gi, 8 + c:9 + c],
                                                           scale=g2b2[:, gi, c:c + 1]),
                         reads=[bank_u[b], g2b2_u[gi]], writes=[hT_u[i]])
                else:
                    P.op(DVE, lambda: nc.vector.tensor_scalar(out=dst, in0=src, scalar1=g2b2[:, gi, c:c + 1], scalar2=g2b2[:, gi, 8 + c:9 + c],
                                                              op0=ALU.mult, op1=ALU.add),
                         reads=[bank_u[b], g2b2_u[gi]], writes=[hT_u[i]])

        stages = [stA, stB, stC, stE, stF]
        for step in range(n + len(stages) - 1):
            for k in reversed(range(len(stages))):
                j = step - k
                if 0 <= j < n:
                    stages[k](j, tiles[j])

    def ffn_sublayer(l, j, nxt):
        jj = 0 if j == 0 else 1
        fz = P.fence()
        act_u = [[Unit(fz), Unit(fz)] for _ in range(NFC)]
        actT = aview(0, NFC * TOK, BF16).rearrange("p (f t) -> p f t", t=TOK)
        wd_t = [aview(45056 + 2048 * i, 1024, BF16).rearrange("p (a n) -> p a n", n=512) for i in range(4)]
        wd_u = [Unit(fz) for _ in range(4)]
        sg_t = [aview(45056 + 8192 + 2048 * i, 512, F32) for i in range(2)]
        sg_u = [Unit(fz) for _ in range(2)]
        wup = w_ffn_up[l, jj]
        wdn = w_ffn_down[l, jj]
        sgc = 0
        for fp in range(NFC // 2):
            if fp == 2:
                emit_gm_gb(l, j)
            if fp == 6 and nxt is not None:
                emit_st(nxt[0], nxt[1], (l, j))
            if ("wup", fp) in prefetched:
                wi, wv = prefetched.pop(("wup", fp))
            else:
                wi, wv = load_wup(l, jj, fp)
            for sub in range(2):
                fc = 2 * fp + sub
                for qb in range(2):
                    bg, bu = nextbank("A"), nextbank("B")
                    hr = [hT_u[4 * qb + t] for t in range(4)]
                    for k in range(8):
                        P.op(PE, lambda k=k: nc.tensor.matmul(banks[bg][:], lhsT=wv[:, k, sub * 128:(sub + 1) * 128],
                                                                rhs=hT[:, k, qb * 512:(qb + 1) * 512], start=(k == 0), stop=(k == 7)),
                             reads=[wbuf_u[wi]] + hr, writes=[bank_u[bg]])
                    for k in range(8):
                        P.op(PE, lambda k=k: nc.tensor.matmul(banks[bu][:], lhsT=wv[:, k, 256 + sub * 128:256 + (sub + 1) * 128],
                                                                rhs=hT[:, k, qb * 512:(qb + 1) * 512], start=(k == 0), stop=(k == 7)),
                             reads=[wbuf_u[wi]] + hr, writes=[bank_u[bu]])
                    si = sgc % 2
                    sgc += 1
                    P.op(ACT, lambda: nc.scalar.activation(out=sg_t[si], in_=banks[bg][:], func=AF.Silu),
                         reads=[bank_u[bg]], writes=[sg_u[si]])
                    P.op(DVE, lambda: nc.vector.tensor_tensor(out=actT[:, fc, qb * 512:(qb + 1) * 512], in0=banks[bu][:],
                                                              in1=sg_t[si], op=ALU.mult),
                         reads=[bank_u[bu], sg_u[si]], writes=[act_u[fc][qb]])
        do_prefetch(nxt)
        wdc = 0
        for dh in range(2):
            for f2 in range(NFC // 2):
                wi = wdc % 4
                wdc += 1
                P.dma(GQ, arena_dma_sems[wi], wd_t[wi],
                      wdn[f2 * 256:(f2 + 1) * 256, dh * 512:(dh + 1) * 512].rearrange("(a p) n -> p a n", p=128),
                      writes=[wd_u[wi]])
                for a in range(2):
                    P.op(DVE, lambda a=a: nc.vector.tensor_tensor(out=wd_t[wi][:, a, :], in0=wd_t[wi][:, a, :],
                                                                 in1=e_gm[:, dh * 512:(dh + 1) * 512], op=ALU.mult),
                         reads=[wd_u[wi], e_u["gm"]], writes=[wd_u[wi]])
                for a in range(2):
                    fc = 2 * f2 + a
                    for i in range(NT):
                        P.op(PE, lambda i=i: nc.tensor.matmul(banks[i][:], lhsT=actT[:, fc, i * 128:(i + 1) * 128],
                                                                rhs=wd_t[wi][:, a, :], start=(fc == 0), stop=(fc == NFC - 1)),
                             reads=[act_u[fc][i // 4], wd_u[wi]], writes=[bank_u[i]])
            for i in range(NT):
                residual_half(i, dh, i, 0.5 / ALPHA)
        epilogue_tiles(range(NT), nxt is not None, bank_sets=((0, 1), (2, 3), (4, 5), (6, 7)))

    def mixer_sublayer(l, grp, nxt):
        is_s = grp == "S"
        koff = 256 if is_s else 0
        nkeys = 1280 if is_s else 1024
        nkt = nkeys // 128
        fz = P.fence()
        o = 0
        mixT = aview(o, 8 * TOK, BF16).rearrange("p (e t) -> p e t", t=TOK); o += 16384
        cqT = aview(o, 3 * TOK, BF16).rearrange("p (e t) -> p e t", t=TOK); o += 6144
        ckvT = aview(o, 2 * 1280, BF16).rearrange("p (e t) -> p e t", t=1280); o += 5120
        krT = aview(o, 1280, BF16); o += 2560
        cqn = [aview(o + 768 * i, QL, BF16) for i in range(2)]; o += 1536
        kvo = [aview(o + 1152 * i, 288, F32) for i in range(2)]; o += 2304
        ckvb = [aview(o + 512 * i, KVL, BF16) for i in range(2)]; o += 1024
        ctxc = aview(o, 512, BF16).rearrange("p (t n) -> p t n", n=256); o += 1024
        ctxr = aview(o, 192, BF16).rearrange("p (t n) -> p t n", n=96); o += 384
        sqj = aview(o, QL, BF16); o += 768
        assert o <= 37248, o
        o = 37248
        U0 = o
        xpad = aview(o, 1056, F32); o += 4224
        xc = aview(o, TOK, F32); o += 4096
        xcb = aview(o, TOK, BF16); o += 2048
        gg = aview(o, TOK, F32); o += 4096
        ra = aview(o, TOK, F32); o += 4096
        iu = aview(o, TOK, F32); o += 4096
        ss = aview(o, TOK, F32); o += 4096
        hf_t = aview(o, TOK, F32); o += 4096
        hb2_t = aview(o, TOK, F32); o += 4096
        assert o <= ARENA_BYTES, o
        mix_u = [[Unit(fz) for _ in range(NT)] for _ in range(8)]
        cqT_u = [Unit(fz) for _ in range(NT)]
        ckvT_u = [Unit(fz) for _ in range(nkt)]
        krT_u = Unit(fz)
        m1_u = {k: [Unit(fz), Unit(fz)] for k in ("cqn", "kvo", "ckvb", "sm")}
        ctx_u = Unit(fz)
        sqj_u = Unit(fz)
        lru_u = {k: Unit(fz) for k in ("xpad", "xc", "xcb", "gg", "ra", "iu", "ss", "hf", "hb")}

        P.dma(SP, norm_sem, qg_bc[:], q_norm_g[l].partition_broadcast(128), writes=[norm_u])
        P.dma(SP, norm_sem, kg_bc[:], kv_norm_g[l].partition_broadcast(128), writes=[norm_u])
        for mat, wsrc in enumerate((lru_w_a, lru_w_x)):
            for d in range(2):
                for par in range(2):
                    src = wsrc[l, d].rearrange("(c q) a e -> q a c e", q=2)[par]
                    P.dma(GQ, bd_sem, bd_t[par * 64:(par + 1) * 64, mat, d, :, par * 64:(par + 1) * 64], src,
                          writes=[bd_u], nc_ok=True)

        if "wA" in prefetched:
            wA_i, wA = prefetched.pop("wA")
            wB_i, wB = prefetched.pop("wB")
        else:
            wA_i, wA = load_w_piece(w_in[l][:, 0:QL], QL)
            wB_i, wB = load_w_piece(w_in[l][:, OFF_KV:OFF_KV + 288], 288)
        if is_s:
            P.dma(GQ, arena_dma_sems[4], ctxc, cache_ckv[l].rearrange("(t p) c -> p t c", p=128), writes=[ctx_u])
            P.op(DVE, lambda: nc.vector.memset(ctxr.rearrange("p t n -> p (t n)"), 0.0), writes=[ctx_u])
            P.dma(GQ, arena_dma_sems[4], ctxr[:, :, 64:96], cache_krope[l].rearrange("(t p) c -> p t c", p=128),
                  writes=[ctx_u], nc_ok=True)

        m1b = {}

        def m1_s0(i):
            p = i % 3
            bq, bk = 2 * p, 2 * p + 1
            m1b[i] = (bq, bk)
            for k in range(8):
                P.op(PE, lambda k=k: nc.tensor.matmul(banks[bq][:, 0:QL], lhsT=hT[:, k, i * 128:(i + 1) * 128], rhs=wA[:, k, :],
                                                        start=(k == 0), stop=(k == 7)),
                     reads=[hT_u[i], wbuf_u[wA_i]], writes=[bank_u[bq]])
            for k in range(8):
                P.op(PE, lambda k=k: nc.tensor.matmul(banks[bk][:, 0:288], lhsT=hT[:, k, i * 128:(i + 1) * 128], rhs=wB[:, k, :],
                                                        start=(k == 0), stop=(k == 7)),
                     reads=[hT_u[i], wbuf_u[wB_i]], writes=[bank_u[bk]])

        def m1_s1(i):
            s = i % 2
            bq, bk = m1b[i]
            sm, smu = sm_t[s], m1_u["sm"][s]
            P.op(ACT, lambda: nc.scalar.activation(out=sqj, in_=banks[bq][:, 0:QL], func=AF.Square, accum_out=sm[:, 3:4]),
                 reads=[bank_u[bq]], writes=[sqj_u, smu])
            P.op(ACT, lambda: nc.scalar.activation(out=sqj[:, 0:KVL], in_=banks[bk][:, 0:KVL], func=AF.Square, accum_out=sm[:, 5:6]),
                 reads=[bank_u[bk]], writes=[sqj_u, smu])
            P.op(ACT, lambda: nc.scalar.activation(out=sm[:, 4:5], in_=sm[:, 3:4], func=AF.Sqrt, bias=RMS_EPS, scale=1.0 / QL),
                 reads=[smu], writes=[smu])
            P.op(ACT, lambda: nc.scalar.activation(out=sm[:, 6:7], in_=sm[:, 5:6], func=AF.Sqrt, bias=RMS_EPS, scale=1.0 / KVL),
                 reads=[smu], writes=[smu])

        def m1_s2(i):
            s = i % 2
            bq, bk = m1b[i]
            sm, smu = sm_t[s], m1_u["sm"][s]
            P.op(DVE, lambda: nc.vector.reciprocal(out=sm[:, 4:5], in_=sm[:, 4:5]), reads=[smu], writes=[smu])
            P.op(DVE, lambda: nc.vector.reciprocal(out=sm[:, 6:7], in_=sm[:, 6:7]), reads=[smu], writes=[smu])
            P.op(DVE, lambda: nc.vector.scalar_tensor_tensor(out=cqn[s], in0=banks[bq][:, 0:QL], scalar=sm[:, 4:5], in1=qg_bc[:],
                                                              op0=ALU.mult, op1=ALU.mult),
                 reads=[bank_u[bq], smu, norm_u], writes=[m1_u["cqn"][s]])
            P.op(DVE, lambda: nc.vector.scalar_tensor_tensor(out=kvo[s][:, 0:KVL], in0=banks[bk][:, 0:KVL], scalar=sm[:, 6:7],
                                                              in1=kg_bc[:], op0=ALU.mult, op1=ALU.mult),
                 reads=[bank_u[bk], smu, norm_u], writes=[m1_u["kvo"][s]])
            if not is_s:
                P.op(DVE, lambda: nc.vector.tensor_copy(out=kvo[s][:, KVL:288], in_=banks[bk][:, KVL:288]),
                     reads=[bank_u[bk]], writes=[m1_u["kvo"][s]])

        def m1_s3(i):
            s = i % 2
            if not is_s:
                seq, t0 = i // 2, (i % 2) * 128
                P.dma(SP, arena_dma_sems[5 + s], new_ckv[seq, l, t0:t0 + 128, :], kvo[s][:, 0:KVL], reads=[m1_u["kvo"][s]])
                P.dma(SP, arena_dma_sems[5 + s], new_krope[seq, l, t0:t0 + 128, :], kvo[s][:, KVL:288], reads=[m1_u["kvo"][s]])
            P.op(ACT, lambda: nc.scalar.copy(out=ckvb[s], in_=kvo[s][:, 0:KVL]), reads=[m1_u["kvo"][s]], writes=[m1_u["ckvb"][s]])
            pT = banks[6][:].bitcast(BF16)
            for c in range(3):
                P.op(PE, lambda c=c: nc.tensor.transpose(pT[:, c * 128:(c + 1) * 128], cqn[s][:, c * 128:(c + 1) * 128], ident_b[:]),
                     reads=[m1_u["cqn"][s], const_u], writes=[bank_u[6]])

        def m1_s4(i):
            s = i % 2
            pT6 = banks[6][:].bitcast(BF16)
            P.op(ACT, lambda: nc.scalar.copy(out=cqT[:, :, i * 128:(i + 1) * 128], in_=pT6[:, 0:384].rearrange("p (c n) -> p c n", n=128)),
                 reads=[bank_u[6]], writes=[cqT_u[i]])
            pT = banks[7][:].bitcast(BF16)
            for c in range(2):
                P.op(PE, lambda c=c: nc.tensor.transpose(pT[:, c * 128:(c + 1) * 128], ckvb[s][:, c * 128:(c + 1) * 128], ident_b[:]),
                     reads=[m1_u["ckvb"][s], const_u], writes=[bank_u[7]])

        def m1_s5(i):
            pT = banks[7][:].bitcast(BF16)
            kt = (koff // 128) + i
            P.op(ACT, lambda: nc.scalar.copy(out=ckvT[:, :, koff + i * 128:koff + (i + 1) * 128],
                                             in_=pT[:, 0:256].rearrange("p (c n) -> p c n", n=128)),
                 reads=[bank_u[7]], writes=[ckvT_u[kt]])

        m1_stages = [m1_s0, m1_s1, m1_s2, m1_s3, m1_s4, m1_s5]
        for step in range(NT + len(m1_stages) - 1):
            for k in reversed(range(len(m1_stages))):
                i = step - k
                if 0 <= i < NT:
                    m1_stages[k](i)

        if is_s:
            b = nextbank("C")
            pT = banks[b][:].bitcast(BF16)
            for c in range(2):
                for t in range(2):
                    P.op(PE, lambda c=c, t=t: nc.tensor.transpose(pT[:, (c * 2 + t) * 128:(c * 2 + t + 1) * 128],
                                                                  ctxc[:, t, c * 128:(c + 1) * 128], ident_b[:]),
                         reads=[ctx_u, const_u], writes=[bank_u[b]])
            P.op(ACT, lambda: nc.scalar.copy(out=ckvT[:, :, 0:256], in_=pT[:, 0:512].rearrange("p (c n) -> p c n", n=256)),
                 reads=[bank_u[b]], writes=[ckvT_u[0], ckvT_u[1]])
            b = nextbank("C")
            pT = banks[b][:].bitcast(BF16)
            for t in range(2):
                P.op(PE, lambda t=t: nc.tensor.transpose(pT[0:96, t * 128:(t + 1) * 128], ctxr[:, t, :], ident_b[:]),
                     reads=[ctx_u, const_u], writes=[bank_u[b]])
            P.op(ACT, lambda: nc.scalar.copy(out=krT[64:96, 0:256], in_=pT[64:96, 0:256]),
                 reads=[bank_u[b]], writes=[krT_u])

        if is_s:
            wS_i = next_wbuf()
            wS = wbuf[wS_i][:, 0:8 * 96].rearrange("p (k n) -> p k n", n=96)
            P.op(DVE, lambda: nc.vector.memset(wbuf[wS_i][:, 0:8 * 96], 0.0), writes=[wbuf_u[wS_i]])
            P.dma(GQ, wbuf_sem[wS_i], wS[:, :, 64:80], w_in[l][:, OFF_KR + 16:OFF_KR + 32].rearrange("(k p) n -> p k n", p=128),
                  writes=[wbuf_u[wS_i]], nc_ok=True)
            P.dma(GQ, wbuf_sem[wS_i], wS[:, :, 80:96], w_in[l][:, OFF_KR:OFF_KR + 16].rearrange("(k p) n -> p k n", p=128),
                  writes=[wbuf_u[wS_i]], nc_ok=True)
        for qb in range(2):
            hr = [hT_u[4 * qb + t] for t in range(4)]
            ba_ = nextbank("A")
            for k in range(8):
                P.op(PE, lambda k=k: nc.tensor.matmul(banks[ba_][0:96, :], lhsT=wB[:, k, 192:288], rhs=hT[:, k, qb * 512:(qb + 1) * 512],
                                                        start=(k == 0), stop=(k == 7)),
                     reads=[wbuf_u[wB_i]] + hr, writes=[bank_u[ba_]])
            if not is_s:
                P.op(ACT, lambda: nc.scalar.copy(out=krT[64:96, qb * 512:(qb + 1) * 512], in_=banks[ba_][64:96, :]),
                     reads=[bank_u[ba_]], writes=[krT_u])
            else:
                bb_ = nextbank("B")
                for k in range(8):
                    P.op(PE, lambda k=k: nc.tensor.matmul(banks[bb_][0:96, :], lhsT=wS[:, k, :], rhs=hT[:, k, qb * 512:(qb + 1) * 512],
                                                            start=(k == 0), stop=(k == 7)),
                         reads=[wbuf_u[wS_i]] + hr, writes=[bank_u[bb_]])
                t1 = ra[64:96, 0:512]
                t2 = iu[64:96, 0:512]
                P.op(DVE, lambda: nc.vector.tensor_tensor(out=t1, in0=banks[ba_][64:96, :], in1=rope_t[64:96, 0, qb * 512:(qb + 1) * 512],
                                                          op=ALU.mult), reads=[bank_u[ba_], const_u], writes=[lru_u["ra"]])
                P.op(DVE, lambda: nc.vector.tensor_tensor(out=t2, in0=banks[bb_][64:96, :], in1=rope_t[64:96, 1, qb * 512:(qb + 1) * 512],
                                                          op=ALU.mult), reads=[bank_u[bb_], const_u], writes=[lru_u["iu"]])
                P.op(DVE, lambda: nc.vector.tensor_tensor(out=krT[64:96, 256 + qb * 512:256 + (qb + 1) * 512], in0=t1, in1=t2, op=ALU.add),
                     reads=[lru_u["ra"], lru_u["iu"]], writes=[krT_u])

        def fetch_attn_weights():
            wq_i = next_wbuf()
            wq = wbuf[wq_i][:, 0:3 * 800].rearrange("p (k n) -> p k n", n=800)
            P.op(DVE, lambda: nc.vector.memset(wq[:, :, 768:800], 0.0), writes=[wbuf_u[wq_i]])
            P.dma(GQ, wbuf_sem[wq_i], wq[:, :, 0:768], w_uq[l].rearrange("(k p) n -> p k n", p=128), writes=[wbuf_u[wq_i]])
            wkv_i = next_wbuf()
            wkv = wbuf[wkv_i][:, 0:2048].rearrange("p (k n) -> p k n", n=1024)
            P.dma(GQ, wbuf_sem[wkv_i], wkv, w_ukv[l].rearrange("(k p) n -> p k n", p=128), writes=[wbuf_u[wkv_i]])
            if is_s:
                wqs_i = next_wbuf()
                wqs_f = wbuf[wqs_i][:, 0:3 * 800].rearrange("p (k n) -> p k n", n=800)
                wqs = wqs_f[:, :, 0:768].rearrange("p k (h e) -> p k h e", e=96)
                P.op(DVE, lambda: nc.vector.memset(wbuf[wqs_i][:, 0:3 * 800], 0.0), writes=[wbuf_u[wqs_i]])
                src4 = w_uq[l].rearrange("(k p) (h e) -> p k h e", p=128, e=96)
                for k in range(3):
                    P.dma(GQ, wbuf_sem[wqs_i], wqs[:, k, :, 64:80], src4[:, k, :, 80:96], writes=[wbuf_u[wqs_i]], nc_ok=True)
                    P.dma(GQ, wbuf_sem[wqs_i], wqs[:, k, :, 80:96], src4[:, k, :, 64:80], writes=[wbuf_u[wqs_i]], nc_ok=True)
            wb_reserved.update({wq_i, wkv_i} | ({wqs_i} if is_s else set()))
            return (wq_i, wq, wkv_i, wkv) + ((wqs_i, wqs_f, wqs) if is_s else (None, None, None))

        segs = [(0, TOK)] if is_s else [(s_ * 256, 256) for s_ in range(4)]
        nseg = len(segs)
        L = segs[0][1]
        LP = L + 3
        xpad_v = xpad[:, 0:nseg * LP].rearrange("p (s t) -> p s t", t=LP)
        P.op(DVE, lambda: nc.vector.memset(xpad, 0.0), writes=[lru_u["xpad"]])
        def load_lru_w(c):
            wi = next_wbuf()
            wv = wbuf[wi][:, 0:2048].rearrange("p (k n) -> p k n", n=256)
            P.dma(GQ, wbuf_sem[wi], wv[:, :, 0:128], w_in[l][:, OFF_UX + c * 128:OFF_UX + (c + 1) * 128].rearrange("(k p) n -> p k n", p=128),
                  writes=[wbuf_u[wi]])
            P.dma(GQ, wbuf_sem[wi], wv[:, :, 128:256], w_in[l][:, OFF_UG + c * 128:OFF_UG + (c + 1) * 128].rearrange("(k p) n -> p k n", p=128),
                  writes=[wbuf_u[wi]])
            return wi, wv

        lru_w0 = load_lru_w(0)
        wq_i, wq, wkv_i, wkv, wqs_i, wqs_f, wqs = fetch_attn_weights()
        for c in range(4):
            wi, wv = lru_w0 if c == 0 else load_lru_w(c)
            for qb in range(2):
                hr = [hT_u[4 * qb + t] for t in range(4)]
                bx_, bg_ = nextbank("A"), nextbank("B")
                for k in range(8):
                    P.op(PE, lambda k=k: nc.tensor.matmul(banks[bx_][:], lhsT=wv[:, k, 0:128], rhs=hT[:, k, qb * 512:(qb + 1) * 512],
                                                            start=(k == 0), stop=(k == 7)),
                         reads=[wbuf_u[wi]] + hr, writes=[bank_u[bx_]])
                for k in range(8):
                    P.op(PE, lambda k=k: nc.tensor.matmul(banks[bg_][:], lhsT=wv[:, k, 128:256], rhs=hT[:, k, qb * 512:(qb + 1) * 512],
                                                            start=(k == 0), stop=(k == 7)),
                         reads=[wbuf_u[wi]] + hr, writes=[bank_u[bg_]])
                if is_s:
                    P.op(ACT, lambda: nc.scalar.copy(out=xpad_v[:, 0, 2 + qb * 512:2 + (qb + 1) * 512], in_=banks[bx_][:]),
                         reads=[bank_u[bx_]], writes=[lru_u["xpad"]])
                else:
                    P.op(ACT, lambda: nc.scalar.copy(out=xpad_v[:, 2 * qb:2 * qb + 2, 2:2 + L],
                                                     in_=banks[bx_][:].rearrange("p (s t) -> p s t", t=L)),
                         reads=[bank_u[bx_]], writes=[lru_u["xpad"]])
                P.op(ACT, lambda: nc.scalar.activation(out=gg[:, qb * 512:(qb + 1) * 512], in_=banks[bg_][:], func=AF.Gelu_apprx_tanh),
                     reads=[bank_u[bg_]], writes=[lru_u["gg"]])
            xc_v = xc.rearrange("p (s t) -> p s t", t=L)
            P.op(DVE, lambda: nc.vector.tensor_scalar(out=xc_v, in0=xpad_v[:, :, 0:L], scalar1=cw_t[:, l, 0, c:c + 1],
                                                      scalar2=cb_t[:, l, c:c + 1], op0=ALU.mult, op1=ALU.add),
                 reads=[lru_u["xpad"], const_u], writes=[lru_u["xc"]])
            for kk in range(1, 4):
                P.op(DVE, lambda kk=kk: nc.vector.scalar_tensor_tensor(out=xc_v, in0=xpad_v[:, :, kk:kk + L], scalar=cw_t[:, l, kk, c:c + 1],
                                                                        in1=xc_v, op0=ALU.mult, op1=ALU.add),
                     reads=[lru_u["xpad"], lru_u["xc"], const_u], writes=[lru_u["xc"]])
            P.op(ACT, lambda: nc.scalar.copy(out=xcb, in_=xc), reads=[lru_u["xc"]], writes=[lru_u["xcb"]])
            for d in range(2):
                hdst, hdu = (hf_t, lru_u["hf"]) if d == 0 else (hb2_t, lru_u["hb"])
                for qb in range(2):
                    br_, bi_ = nextbank("A"), nextbank("B")
                    sl = slice(qb * 512, (qb + 1) * 512)
                    P.op(PE, lambda: nc.tensor.matmul(banks[br_][:], lhsT=bd_t[:, 0, d, c, :], rhs=xcb[:, sl], start=True, stop=True),
                         reads=[bd_u, lru_u["xcb"]], writes=[bank_u[br_]])
                    P.op(PE, lambda: nc.tensor.matmul(banks[bi_][:], lhsT=bd_t[:, 1, d, c, :], rhs=xcb[:, sl], start=True, stop=True),
                         reads=[bd_u, lru_u["xcb"]], writes=[bank_u[bi_]])
                    P.op(ACT, lambda: nc.scalar.activation(out=ra[:, sl], in_=banks[br_][:], func=AF.Sigmoid, bias=ba_t[:, l, d, c:c + 1], scale=1.0),
                         reads=[bank_u[br_], const_u], writes=[lru_u["ra"]])
                    P.op(ACT, lambda: nc.scalar.activation(out=iu[:, sl], in_=banks[bi_][:], func=AF.Sigmoid, bias=bx_t[:, l, d, c:c + 1], scale=1.0),
                         reads=[bank_u[bi_], const_u], writes=[lru_u["iu"]])
                P.op(ACT, lambda: nc.scalar.activation(out=ss, in_=ra, func=AF.Exp, scale=neg2c_t[:, l, d, c:c + 1]),
                     reads=[lru_u["ra"], lru_c_u, fso_u], writes=[lru_u["ss"]])
                P.op(ACT, lambda: nc.scalar.activation(out=ra, in_=ra, func=AF.Exp, scale=negc_t[:, l, d, c:c + 1]),
                     reads=[lru_u["ra"], lru_c_u], writes=[lru_u["ra"]])
                P.op(ACT, lambda: nc.scalar.activation(out=ss, in_=ss, func=AF.Sqrt, bias=1.0, scale=-1.0),
                     reads=[lru_u["ss"]], writes=[lru_u["ss"]])
                P.op(DVE, lambda: nc.vector.tensor_tensor(out=iu, in0=iu, in1=xc, op=ALU.mult),
                     reads=[lru_u["iu"], lru_u["xc"]], writes=[lru_u["iu"]])
                P.op(DVE, lambda: nc.vector.tensor_tensor(out=iu, in0=iu, in1=ss, op=ALU.mult),
                     reads=[lru_u["iu"], lru_u["ss"]], writes=[lru_u["iu"]])
                for (s0, ln) in segs:
                    if d == 0:
                        o_ap, a_ap, u_ap = hdst[:, s0:s0 + ln], ra[:, s0:s0 + ln], iu[:, s0:s0 + ln]
                    else:
                        o_ap, a_ap, u_ap = hdst[:, s0:s0 + ln][:, ::-1], ra[:, s0:s0 + ln][:, ::-1], iu[:, s0:s0 + ln][:, ::-1]
                    init = h0_t[:, l, d, c:c + 1] if is_s else 0.0
                    P.op(DVE, lambda o_ap=o_ap, a_ap=a_ap, u_ap=u_ap, init=init: nc.vector.tensor_tensor_scan(
                        out=o_ap, data0=a_ap, data1=u_ap, initial=init, op0=ALU.mult, op1=ALU.add),
                         reads=[lru_u["ra"], lru_u["iu"], const_u], writes=[hdu])
            if not is_s:
                hfv = hf_t.rearrange("p (s t) -> p s t", t=L)
                hbv = hb2_t.rearrange("p (s t) -> p s t", t=L)
                P.op(ACT, lambda: nc.scalar.copy(out=fs_t[:, :, l, 0, c:c + 1], in_=hfv[:, :, L - 1:L]),
                     reads=[lru_u["hf"]], writes=[fs_u])
                P.op(ACT, lambda: nc.scalar.copy(out=fs_t[:, :, l, 1, c:c + 1], in_=hbv[:, :, 0:1]),
                     reads=[lru_u["hb"]], writes=[fs_u])
            P.op(DVE, lambda: nc.vector.tensor_tensor(out=hf_t, in0=hf_t, in1=hb2_t, op=ALU.add),
                 reads=[lru_u["hf"], lru_u["hb"]], writes=[lru_u["hf"]])
            P.op(DVE, lambda: nc.vector.tensor_tensor(out=mixT[:, 4 + c, :], in0=hf_t, in1=gg, op=ALU.mult),
                 reads=[lru_u["hf"], lru_u["gg"]], writes=mix_u[4 + c])

        fz2 = P.fence()
        o = U0
        hd = []
        for s_ in range(2):
            qh = aview(o, TOK, BF16); o += 2048
            Kh = aview(o, 1280, BF16); o += 2560
            Vh = aview(o, 1280, BF16).rearrange("p (t n) -> p t n", n=128); o += 2560
            hd.append((qh, Kh, Vh, Unit(fz2), Unit(fz2), Unit(fz2)))
        pT_t = []
        for s_ in range(2):
            pT_t.append((aview(o, 1024, BF16), Unit(fz2))); o += 2048
        rc_t = []
        for s_ in range(2):
            rc_t.append((aview(o, 512, F32), Unit(fz2))); o += 2048
        rp_t = []
        for s_ in range(2):
            rp_t.append((aview(o, 512, F32), Unit(fz2))); o += 2048
        assert o <= ARENA_BYTES
        for s_ in range(2):
            P.op(DVE, lambda s_=s_: nc.vector.memset(hd[s_][2][:, :, 64:128], 1.0), writes=[hd[s_][5]])
        if is_s:
            asegs = [(0, 512, list(range(10))), (512, 512, list(range(10)))]
        else:
            asegs = [(s_ * 256, 256, [2 * s_, 2 * s_ + 1]) for s_ in range(4)]
        ptc = [0]
        rcc = [0]

        def prep_head(h):
            qh, Kh, Vh, qh_u, Kh_u, Vh_u = hd[h % 2]
            pieces = []

            def q_piece(qb):
                sl = slice(qb * 512, (qb + 1) * 512)
                cr = [cqT_u[4 * qb + t] for t in range(4)]
                ba_ = nextbank("A")
                for k in range(3):
                    P.op(PE, lambda k=k: nc.tensor.matmul(banks[ba_][:, :], lhsT=wq[:, k, h * 96:h * 96 + 128], rhs=cqT[:, k, sl],
                                                            start=(k == 0), stop=(k == 2)),
                         reads=[wbuf_u[wq_i]] + cr, writes=[bank_u[ba_]])
                if not is_s:
                    P.op(ACT, lambda: nc.scalar.copy(out=qh[0:96, sl], in_=banks[ba_][0:96, :]), reads=[bank_u[ba_]], writes=[qh_u])
                else:
                    bb_ = nextbank("B")
                    for k in range(3):
                        P.op(PE, lambda k=k: nc.tensor.matmul(banks[bb_][:, :], lhsT=wqs_f[:, k, h * 96:h * 96 + 128], rhs=cqT[:, k, sl],
                                                                start=(k == 0), stop=(k == 2)),
                             reads=[wbuf_u[wqs_i]] + cr, writes=[bank_u[bb_]])
                    P.op(ACT, lambda: nc.scalar.copy(out=qh[0:64, sl], in_=banks[ba_][0:64, :]), reads=[bank_u[ba_]], writes=[qh_u])
                    (t1, t1u), (t2, t2u) = rp_t[0], rp_t[1]
                    P.op(DVE, lambda: nc.vector.tensor_tensor(out=t1[64:96, :], in0=banks[ba_][64:96, :], in1=rope_t[64:96, 0, sl], op=ALU.mult),
                         reads=[bank_u[ba_], const_u], writes=[t1u])
                    P.op(DVE, lambda: nc.vector.tensor_tensor(out=t2[64:96, :], in0=banks[bb_][64:96, :], in1=rope_t[64:96, 1, sl], op=ALU.mult),
                         reads=[bank_u[bb_], const_u], writes=[t2u])
                    P.op(DVE, lambda: nc.vector.tensor_tensor(out=qh[64:96, sl], in0=t1[64:96, :], in1=t2[64:96, :], op=ALU.add),
                         reads=[t1u, t2u], writes=[qh_u])
            for qb in range(2):
                pieces.append(lambda qb=qb: q_piece(qb))

            def k_piece(k0):
                n = min(512, nkeys - k0)
                kr_ = [ckvT_u[t] for t in range(k0 // 128, (k0 + n) // 128)]
                ba_ = nextbank("A")
                for k in range(2):
                    P.op(PE, lambda k=k: nc.tensor.matmul(banks[ba_][:, 0:n], lhsT=wkv[:, k, h * 128:h * 128 + 128], rhs=ckvT[:, k, k0:k0 + n],
                                                            start=(k == 0), stop=(k == 1)),
                         reads=[wbuf_u[wkv_i]] + kr_, writes=[bank_u[ba_]])
                P.op(DVE, lambda: nc.vector.tensor_copy(out=Kh[0:64, k0:k0 + n], in_=banks[ba_][0:64, 0:n]), reads=[bank_u[ba_]], writes=[Kh_u])

            for k0 in range(0, nkeys, 512):
                pieces.append(lambda k0=k0: k_piece(k0))
            pieces.append(lambda: P.op(DVE, lambda: nc.vector.tensor_copy(out=Kh[64:96, 0:nkeys], in_=krT[64:96, 0:nkeys]),
                                       reads=[krT_u], writes=[Kh_u]))

            def v_piece(t0):
                nt_ = min(8, nkt - t0)
                bb_ = nextbank("B")
                pv = banks[bb_][:, 0:nt_ * 64].rearrange("p (t n) -> p t n", n=64)
                for t in range(nt_):
                    for k in range(2):
                        P.op(PE, lambda k=k, t=t: nc.tensor.matmul(pv[:, t, :], lhsT=ckvT[:, k, (t0 + t) * 128:(t0 + t + 1) * 128],
                                                                     rhs=wkv[:, k, h * 128 + 64:h * 128 + 128], start=(k == 0), stop=(k == 1)),
                             reads=[wbuf_u[wkv_i], ckvT_u[t0 + t]], writes=[bank_u[bb_]])
                P.op(DVE, lambda: nc.vector.tensor_copy(out=Vh[:, t0:t0 + nt_, 0:64], in_=pv), reads=[bank_u[bb_]], writes=[Vh_u])

            for t0 in range(0, nkt, 8):
                pieces.append(lambda t0=t0: v_piece(t0))
            return pieces

        sc_cnt = [0]
        acc_cnt = [0]
        pending_norm = []

        def attn_head(h, pend):
            qh, Kh, Vh, qh_u, Kh_u, Vh_u = hd[h % 2]
            for (q0, nq, kts) in asegs:
                bo_ = (2, 3)[acc_cnt[0] % 2] if is_s else (3, 6, 7)[acc_cnt[0] % 3]
                acc_cnt[0] += 1
                groups_ = [(kts[2 * p], kts[2 * p + 1]) for p in range(len(kts) // 2)]
                sb = {}

                def score(gi):
                    if is_s:
                        b0 = (4, 6)[sc_cnt[0] % 2]
                        sc_cnt[0] += 1
                        outs = [(banks[b0][:, 0:nq], bank_u[b0]), (banks[b0 + 1][:, 0:nq], bank_u[b0 + 1])]
                        us = [bank_u[b0], bank_u[b0 + 1]]
                    else:
                        b0 = nextbank("C")
                        outs = [(banks[b0][:, 0:nq], bank_u[b0]), (banks[b0][:, nq:2 * nq], bank_u[b0])]
                        us = [bank_u[b0]]
                    for j, kt in enumerate(groups_[gi]):
                        o_ap, o_u = outs[j]
                        P.op(PE, lambda: nc.tensor.matmul(o_ap, lhsT=Kh[0:96, kt * 128:(kt + 1) * 128], rhs=qh[0:96, q0:q0 + nq],
                                                           start=True, stop=True),
                             reads=[Kh_u, qh_u], writes=[o_u])
                    sb[gi] = (b0, us)

                score(0)
                for gi, grp_ in enumerate(groups_):
                    if gi + 1 < len(groups_):
                        score(gi + 1)
                    if pend:
                        pend.pop(0)()
                    if pend and not is_s:
                        pend.pop(0)()
                    b0, us = sb[gi]
                    pt, ptu = pT_t[ptc[0] % 2]
                    ptc[0] += 1
                    P.op(ACT, lambda: nc.scalar.activation(out=pt[:, 0:2 * nq], in_=bigps[:, b0 * 512:b0 * 512 + 2 * nq], func=AF.Exp,
                                                           scale=ATTN_SCALE),
                         reads=us, writes=[ptu])
                    for j, kt in enumerate(grp_):
                        P.op(PE, lambda: nc.tensor.matmul(banks[bo_][:, 0:nq], lhsT=Vh[:, kt, :], rhs=pt[:, j * nq:(j + 1) * nq],
                                                           start=(gi == 0 and j == 0), stop=(gi == len(groups_) - 1 and j == 1)),
                             reads=[Vh_u, ptu], writes=[bank_u[bo_]])
                    if gi == 0:
                        while pending_norm:
                            pending_norm.pop(0)()

                def norm(bo_=bo_, q0=q0, nq=nq, h=h):
                    rc, rcu = rc_t[rcc[0] % 2]
                    rcc[0] += 1
                    P.op(DVE, lambda: nc.vector.reciprocal(out=rc[64:128, 0:nq], in_=banks[bo_][64:128, 0:nq]), reads=[bank_u[bo_]], writes=[rcu])
                    pb = (h % 2) * 64
                    mu = [mix_u[h // 2][t] for t in range(q0 // 128, (q0 + nq) // 128)]
                    P.op(DVE, lambda: nc.vector.tensor_tensor(out=mixT[pb:pb + 64, h // 2, q0:q0 + nq], in0=banks[bo_][0:64, 0:nq],
                                                              in1=rc[64:128, 0:nq], op=ALU.mult),
                         reads=[bank_u[bo_], rcu], writes=mu)

                pending_norm.append(norm)

        rot_a_saved, rot_b_saved = rot["A"], rot["B"]
        if is_s:
            rot["A"], rot["B"] = [0], [1]
        else:
            rot["B"] = [2]
        for pc in prep_head(0):
            pc()
        for h in range(NH):
            pend = prep_head(h + 1) if h + 1 < NH else []
            attn_head(h, pend)
            while pend:
                pend.pop(0)()
        while pending_norm:
            pending_norm.pop(0)()
        rot["A"], rot["B"] = rot_a_saved, rot_b_saved
        wb_reserved.clear()

        emit_gm_gb(l, 1)
        if nxt is not None:
            emit_st(nxt[0], nxt[1], (l, 1))
        wo_v = []
        for dh in range(2):
            wo_v.append(load_w_piece(w_o[l][:, dh * 512:(dh + 1) * 512], 512))
        for dh in range(2):
            wi, wv = wo_v[dh]
            for e in range(8):
                P.op(DVE, lambda e=e: nc.vector.tensor_tensor(out=wv[:, e, :], in0=wv[:, e, :], in1=e_gm[:, dh * 512:(dh + 1) * 512], op=ALU.mult),
                     reads=[wbuf_u[wi], e_u["gm"]], writes=[wbuf_u[wi]])
        do_prefetch(nxt)

        def wo_tile(i):
            for dh in range(2):
                wi, wv = wo_v[dh]
                b = nextbank("A") if dh == 0 else nextbank("B")
                for e in range(8):
                    P.op(PE, lambda e=e: nc.tensor.matmul(banks[b][:], lhsT=mixT[:, e, i * 128:(i + 1) * 128], rhs=wv[:, e, :],
                                                            start=(e == 0), stop=(e == 7)),
                         reads=[mix_u[e][i], wbuf_u[wi]], writes=[bank_u[b]])
                residual_half(i, dh, b, 1.0 / ALPHA)

        epilogue_tiles(range(NT), nxt is not None, pre=wo_tile)

    lru_c_u = rope_u
    def load_consts():
        P.dma(SP, const_sem, cw_t[:], conv_w.rearrange("l k (c p) -> p l k c", p=128), writes=[const_u], nc_ok=True)
        P.dma(SP, const_sem, cb_t[:], conv_b.rearrange("l (c p) -> p l c", p=128), writes=[const_u], nc_ok=True)
        P.dma(SP, const_sem, ba_t[:], lru_b_a.rearrange("l d (c p) -> p l d c", p=128), writes=[const_u], nc_ok=True)
        P.dma(SP, const_sem, bx_t[:], lru_b_x.rearrange("l d (c p) -> p l d c", p=128), writes=[const_u], nc_ok=True)
        P.dma(SP, const_sem, lam_t[:], lru_lambda.rearrange("l d (c p) -> p l d c", p=128), writes=[const_u], nc_ok=True)
        P.dma(SP, const_sem, h0_t[:], state_lru.rearrange("l d (c p) -> p l d c", p=128), writes=[const_u], nc_ok=True)
        P.dma(SP, const_sem, rope_t[64:96, :, :], rope_cs.rearrange("a r t -> r a t"), writes=[const_u])
        P.dma(SP, const_sem, bmodT[:], b_mod.rearrange("l (v p) -> p l v", p=128), writes=[const_u], nc_ok=True)
        P.dma(SP, const_sem, lngT[:], ln_g.rearrange("l j (c p) -> p l j c", p=128), writes=[const_u], nc_ok=True)
        P.dma(SP, const_sem, lnbT[:], ln_b.rearrange("l j (c p) -> p l j c", p=128), writes=[const_u], nc_ok=True)
        lamf = lam_t[:].rearrange("p l d c -> p (l d c)")
        negcf = negc_t[:].rearrange("p l d c -> p (l d c)")
        neg2cf = neg2c_t[:].rearrange("p l d c -> p (l d c)")
        P.op(ACT, lambda: nc.scalar.activation(out=negcf, in_=lamf, func=AF.Exp, scale=-1.0), reads=[const_u], writes=[rope_u])
        P.op(ACT, lambda: nc.scalar.activation(out=negcf, in_=negcf, func=AF.Ln, bias=1.0, scale=1.0), reads=[rope_u], writes=[rope_u])
        P.op(ACT, lambda: nc.scalar.mul(out=neg2cf, in_=negcf, mul=-16.0), reads=[rope_u], writes=[fso_u])
        P.op(ACT, lambda: nc.scalar.mul(out=negcf, in_=negcf, mul=-8.0), reads=[rope_u, fso_u], writes=[rope_u])
        P.op(DVE, lambda: nc.vector.memset(bd_t[:].rearrange("p a b c n -> p (a b c n)"), 0.0), writes=[bd_u])
        P.op(DVE, lambda: nc.vector.memset(fs_t[:].rearrange("p a b c n -> p (a b c n)"), 0.0), writes=[fs_u])


    P.dma(SP, const_sem, ident_f[:], ident_in, writes=[const_u])
    const2_sem = P.new_sem("d_const2")
    P.dma(GQ, const2_sem, ident_b[:], ident_in, writes=[const_u])
    P.op(DVE, lambda: nc.vector.memset(ones_f[:], 1.0), writes=[screp_u])
    consts_loaded = [False]
    for grp in groups:
        is_s = grp == "S"
        for i in range(NT):
            src = x_sample[i * 128:(i + 1) * 128, :] if is_s else x_prompt[i // 2, (i % 2) * 128:(i % 2 + 1) * 128, :]
            P.dma(SP, x_sem[i], x_t[:, i, :], src, writes=[x_u[i]])
        csrc = c_in if is_s else c_ctx
        P.dma(SP, cond_sem, cond_f[:], csrc.rearrange("(k p) -> p k", p=128), writes=[cond_u], nc_ok=True)
        P.op(ACT, lambda: nc.scalar.activation(out=cond_s[:], in_=cond_f[:], func=AF.Silu), reads=[cond_u], writes=[cond_u])
        for k in range(8):
            P.op(DVE, lambda k=k: nc.vector.tensor_scalar(out=sc_rep[:, k, :], in0=ones_f[:], scalar1=cond_s[:, k:k + 1], scalar2=None,
                                                          op0=ALU.mult),
                 reads=[cond_u, screp_u], writes=[screp_u])
        emit_st_first = True
        if not consts_loaded[0]:
            P.dma(SP, const_sem, bmodT[:, 0, 0:16], b_mod[0, 0:2048].rearrange("(v p) -> p v", p=128), writes=[const_u], nc_ok=True)
        emit_st(0, 0, None)
        for i in range(NT):
            make_h(i, x_t[:, i, :], x_u[i])
        if not consts_loaded[0]:
            load_consts()
            consts_loaded[0] = True
        subs = [(l, j) for l in range(n_layers) for j in range(3)]
        for idx, (l, j) in enumerate(subs):
            nxt = subs[idx + 1] if idx + 1 < len(subs) else None
            if j == 1:
                mixer_sublayer(l, grp, nxt)
            else:
                ffn_sublayer(l, j, nxt)
            if debug and idx == 0:
                dbg_x = nc.dram_tensor("dbg_x", [128, NT, D], F32, kind="ExternalOutput").ap()
                dbg_h = nc.dram_tensor("dbg_h", [128, 8, TOK], BF16, kind="ExternalOutput").ap()
                dbg_e = nc.dram_tensor("dbg_e", [5, 128, D], F32, kind="ExternalOutput").ap()
                dbg_sems = [P.new_sem(f"d_dbg{q}") for q in range(7)]
                P.dma(SP, dbg_sems[5], dbg_x, x_t[:], reads=x_u)
                P.dma(SP, dbg_sems[6], dbg_h, hT[:], reads=hT_u)
                for q, (tl, k_) in enumerate(((e_gm, "gm"), (e_g, "g"), (e_b, "b"))):
                    P.dma(SP, dbg_sems[q], dbg_e[q], tl[:], reads=[e_u[k_]])
        for i in range(NT):
            dst = y_sample[i * 128:(i + 1) * 128, :] if is_s else y_prompt[i // 2, (i % 2) * 128:(i % 2 + 1) * 128, :]
            P.dma(SP, x_sem[i], dst, x_t[:, i, :], reads=[x_u[i]])
        if not is_s:
            b = nextbank("C")
            P.op(PE, lambda: nc.tensor.transpose(banks[b][:, 0:128], fs_t[:].rearrange("p a b c n -> p (a b c n)"), ident_f[:]),
                 reads=[fs_u, const_u], writes=[bank_u[b]])
            P.op(ACT, lambda: nc.scalar.copy(out=fs_o[:], in_=banks[b][:, 0:128]), reads=[bank_u[b]], writes=[fso_u])
            P.dma(SP, fso_sem, new_state, fs_o[:], reads=[fso_u])

    for s in P.sems:
        if s.name.startswith("d_") and s.cnt > 0:
            nc.sync.wait_ge(s.h, s.cnt)
    return P


_CACHE = {}


def _rope_tables():
    n_freq = ROPE // 4
    inv = (10000.0 ** (-np.arange(n_freq, dtype=np.float32) / n_freq)).astype(np.float32)
    t = np.arange(TOK)
    row = (t // 64).astype(np.float32)
    col = (t % 64).astype(np.float32)
    ang = np.concatenate([row[:, None] * inv, col[:, None] * inv], axis=-1).astype(np.float32)
    cos = np.cos(ang).astype(np.float32).T
    sin = np.sin(ang).astype(np.float32).T
    C = np.concatenate([cos, cos], axis=0)
    S = np.concatenate([-sin, sin], axis=0)
    return np.ascontiguousarray(np.stack([C, S], axis=0)).astype(np.float32)


def kernel(**inputs):
    if "prog" not in _CACHE:
        _CACHE["prog"] = build_program()
    P = _CACHE["prog"]
    f = lambda a: np.ascontiguousarray(np.asarray(a, dtype=np.float32))
    shared = {k: f(inputs[k]) for k in (
        "c_ctx", "w_mod", "b_mod", "ln_g", "ln_b", "w_ffn_up", "w_ffn_down", "w_in", "q_norm_g", "kv_norm_g",
        "w_uq", "w_ukv", "conv_w", "conv_b", "lru_w_a", "lru_b_a", "lru_w_x", "lru_b_x", "lru_lambda", "w_o")}
    shared["ident"] = np.eye(128, dtype=np.float32)
    shared["rope_cs"] = _rope_tables()
    xp = f(inputs["x_prompt"]); xs = f(inputs["x_sample"])
    ckv = f(inputs["cache_ckv"]); ckr = f(inputs["cache_krope"]); stl = f(inputs["state_lru"]); cc = f(inputs["c"])
    in_maps = []
    for core in range(8):
        b = core % 4
        m = dict(shared)
        m["x_prompt"] = np.ascontiguousarray(xp[4 * core:4 * core + 4])
        m["x_sample"] = np.ascontiguousarray(xs[b])
        m["cache_ckv"] = np.ascontiguousarray(ckv[b])
        m["cache_krope"] = np.ascontiguousarray(ckr[b])
        m["state_lru"] = np.ascontiguousarray(stl[b])
        m["c"] = np.ascontiguousarray(cc[b])
        in_maps.append(m)
    res = run_bass_kernel_spmd(P.nc, in_maps, core_ids=list(range(8)))
    r = res.results
    y_prompt = np.concatenate([r[c]["y_prompt"] for c in range(8)], axis=0)
    y_sample = np.stack([r[c]["y_sample"] for c in range(4)], axis=0)
    new_ckv = np.concatenate([r[c]["new_ckv"] for c in range(8)], axis=0)
    new_krope = np.concatenate([r[c]["new_krope"] for c in range(8)], axis=0)
    new_state = np.concatenate([r[c]["new_state"].reshape(4, DEPTH, 2, LRUW) for c in range(8)], axis=0)
    return (y_prompt.astype(np.float32), y_sample.astype(np.float32), new_ckv.astype(np.float32),
            new_krope.astype(np.float32), new_state.astype(np.float32))
```
